# Optimizing a Trainium2 kernel written in Bass

```python
import math
import jax
import jax.numpy as jnp
from jax import lax
import numpy as np

D_MODEL = 1024
BATCH = 16
SEQ = 4096
DEPTH = 1
DEC_BATCH = 8
DEC_SEQ = 64
PAST_LEN = 4096

CHUNK = 64
N_META = 16
EPS = 1e-6
MLA_HEADS = 8
Q_LORA = 384
KV_LORA = 256
NOPE_DIM = 64
ROPE_DIM = 32
QK_HEAD = NOPE_DIM + ROPE_DIM
V_HEAD = 64
MLA_WIDTH = MLA_HEADS * V_HEAD
ROPE_BASE = 10000.0
Q_BLOCK = 128
GDN_HEADS = 4
GDN_DK = 128
GDN_DV = 128
GDN_QK_WIDTH = GDN_HEADS * GDN_DK
GDN_V_WIDTH = GDN_HEADS * GDN_DV
CONV_W = 4
CONV_CH = 2 * GDN_QK_WIDTH + GDN_V_WIDTH
GDN_CHUNK = 64
MIX_WIDTH = MLA_WIDTH + GDN_V_WIDTH
D_FF = 4 * D_MODEL
SPLIT_Q = Q_LORA
SPLIT_KV = SPLIT_Q + KV_LORA
SPLIT_ROPE = SPLIT_KV + ROPE_DIM
SPLIT_QKV = SPLIT_ROPE + CONV_CH
SPLIT_GATE = SPLIT_QKV + GDN_V_WIDTH
SPLIT_BETA = SPLIT_GATE + GDN_HEADS
IN_COLS = SPLIT_BETA + GDN_HEADS
IN_SPLITS = (SPLIT_Q, SPLIT_KV, SPLIT_ROPE, SPLIT_QKV, SPLIT_GATE, SPLIT_BETA)

kernel_name = 'hymba_mla_gdn_streaming_step'


def rms_norm(x, g):
    xf = x.astype(jnp.float32)
    y = xf * lax.rsqrt(jnp.mean(xf * xf, axis=-1, keepdims=True) + EPS)
    return (y * g.astype(jnp.float32)).astype(x.dtype)


def l2_norm(x):
    xf = x.astype(jnp.float32)
    return (xf * lax.rsqrt(jnp.sum(xf * xf, axis=-1, keepdims=True) + EPS)).astype(x.dtype)


def apply_rope(x, pos):
    half = ROPE_DIM // 2
    inv_freq = ROPE_BASE ** (-jnp.arange(half, dtype=jnp.float32) / half)
    ang = pos.astype(jnp.float32)[:, None] * inv_freq[None, :]
    cos = jnp.cos(ang)[None, :, None, :]
    sin = jnp.sin(ang)[None, :, None, :]
    xf = x.astype(jnp.float32)
    x1, x2 = xf[..., :half], xf[..., half:]
    return jnp.concatenate([x1 * cos - x2 * sin, x2 * cos + x1 * sin], axis=-1).astype(x.dtype)


def input_projection(x, pos, lp):
    h = rms_norm(x, lp['attn_norm'])
    z = jnp.einsum('bld,dc->blc', h, lp['w_in'])
    q_lat, kv_lat, k_rope_raw, qkv, gate, beta_logit, a_logit = jnp.split(z, IN_SPLITS, axis=-1)
    w_uq = lp['w_uq'].reshape(Q_LORA, MLA_HEADS, QK_HEAD)
    q = jnp.einsum('blr,rhd->blhd', rms_norm(q_lat, lp['q_a_norm']), w_uq)
    q = rms_norm(q, lp['q_norm'])
    q = jnp.concatenate([q[..., :NOPE_DIM], apply_rope(q[..., NOPE_DIM:], pos)], axis=-1)
    kv_lat = rms_norm(kv_lat, lp['kv_a_norm'])
    return q, kv_lat, k_rope_raw, qkv, gate, beta_logit, a_logit


def mla_keys_values(kv_lat, k_rope_raw, pos, lp):
    B, L, _ = kv_lat.shape
    k_nope = jnp.einsum('blr,rhd->blhd', kv_lat, lp['w_uk'].reshape(KV_LORA, MLA_HEADS, NOPE_DIM))
    v = jnp.einsum('blr,rhd->blhd', kv_lat, lp['w_uv'].reshape(KV_LORA, MLA_HEADS, V_HEAD))
    k_rope = jnp.broadcast_to(k_rope_raw[:, :, None, :], (B, L, MLA_HEADS, ROPE_DIM))
    k = rms_norm(jnp.concatenate([k_nope, k_rope], axis=-1), lp['k_norm'])
    k = jnp.concatenate([k[..., :NOPE_DIM], apply_rope(k[..., NOPE_DIM:], pos)], axis=-1)
    return k, v


def attend(q, k, v, mask):
    s = jnp.einsum('bqhd,bkhd->bhqk', q, k).astype(jnp.float32) * (QK_HEAD ** -0.5)
    if mask is not None:
        s = jnp.where(mask, s, -jnp.inf)
    p = jax.nn.softmax(s, axis=-1).astype(v.dtype)
    return jnp.einsum('bhqk,bkhd->bqhd', p, v)


def chunk_causal_attention(q, k, v, chunk_id):
    B, L, H, D = q.shape
    nb = -(-L // Q_BLOCK)
    pad = nb * Q_BLOCK - L
    qb = jnp.pad(q, ((0, 0), (0, pad), (0, 0), (0, 0))).reshape(B, nb, Q_BLOCK, H, D).transpose(1, 0, 2, 3, 4)
    cb = jnp.pad(chunk_id, (0, pad), constant_values=L).reshape(nb, Q_BLOCK)

    def one_block(args):
        q_blk, c_blk = args
        return attend(q_blk, k, v, c_blk[:, None] >= chunk_id[None, :])

    o = lax.map(one_block, (qb, cb))
    return o.transpose(1, 0, 2, 3, 4).reshape(B, nb * Q_BLOCK, H, V_HEAD)[:, :L]


def causal_conv_silu(u, buf, w):
    xp = jnp.concatenate([buf.astype(u.dtype), u], axis=1)
    y = lax.conv_general_dilated(xp, w.astype(u.dtype)[:, None, :], window_strides=(1,), padding='VALID',
                                 dimension_numbers=('NWC', 'WIO', 'NWC'), feature_group_count=CONV_CH)
    return jax.nn.silu(y), xp[:, -(CONV_W - 1):]


def gated_delta_chunked(q, k, v, g, beta, state0):
    B, L, H, _ = q.shape
    C = GDN_CHUNK
    n = -(-L // C)
    pad = n * C - L

    def blocks(t):
        t = jnp.pad(t.astype(jnp.float32), ((0, 0), (0, pad)) + ((0, 0),) * (t.ndim - 2))
        t = t.reshape((B, n, C) + t.shape[2:])
        return t.transpose((1, 0, 3, 2, 4)[:t.ndim])

    qc, kc, vc = blocks(q), blocks(k), blocks(v)
    gc = jnp.cumsum(blocks(g), axis=-1)
    bc = blocks(beta)
    idx = jnp.arange(C)
    causal = idx[:, None] >= idx[None, :]
    strict = idx[:, None] > idx[None, :]
    decay = jnp.exp(jnp.where(causal, gc[..., :, None] - gc[..., None, :], -jnp.inf))
    kb = kc * bc[..., None]
    a = jnp.where(strict, jnp.einsum('...id,...jd->...ij', kb, kc) * decay, 0.0)
    eye = jnp.eye(C, dtype=jnp.float32)
    t_inv = lax.linalg.triangular_solve(eye + a, jnp.broadcast_to(eye, a.shape), left_side=True, lower=True)
    u = t_inv @ (vc * bc[..., None])
    w = t_inv @ (kb * jnp.exp(gc)[..., None])
    qk = jnp.einsum('...id,...jd->...ij', qc, kc) * decay

    def step(S, xs):
        q_i, k_i, u_i, w_i, qk_i, g_i = xs
        v_new = u_i - w_i @ S
        o = (q_i * jnp.exp(g_i)[..., None]) @ S + qk_i @ v_new
        g_last = g_i[..., -1:]
        S = S * jnp.exp(g_last)[..., None] + jnp.einsum('bhck,bhcv->bhkv', k_i * jnp.exp(g_last - g_i)[..., None], v_new)
        return S, o

    S, o = lax.scan(step, state0.astype(jnp.float32), (qc, kc, u, w, qk, gc))
    o = o.transpose(1, 0, 3, 2, 4).reshape(B, n * C, H, GDN_DV)[:, :L]
    return o.astype(v.dtype), S.astype(state0.dtype)


def gdn_branch(qkv, gate, beta_logit, a_logit, conv_buf, state0, lp):
    B, L, _ = qkv.shape
    conv_out, new_buf = causal_conv_silu(qkv, conv_buf, lp['conv_w'])
    q, k, v = jnp.split(conv_out, (GDN_QK_WIDTH, 2 * GDN_QK_WIDTH), axis=-1)
    q = l2_norm(q.reshape(B, L, GDN_HEADS, GDN_DK)) * (GDN_DK ** -0.5)
    k = l2_norm(k.reshape(B, L, GDN_HEADS, GDN_DK))
    v = v.reshape(B, L, GDN_HEADS, GDN_DV)
    beta = jax.nn.sigmoid(beta_logit.astype(jnp.float32))
    g = -jnp.exp(lp['a_log'].astype(jnp.float32)) * jax.nn.softplus(a_logit.astype(jnp.float32) + lp['dt_bias'].astype(jnp.float32))
    o, S = gated_delta_chunked(q, k, v, g, beta, state0)
    o = rms_norm(o, lp['gdn_out_norm']) * jax.nn.silu(gate.reshape(B, L, GDN_HEADS, GDN_DV))
    return o.reshape(B, L, GDN_V_WIDTH), S, new_buf


def finish_layer(x, attn, gdn, lp):
    B, L, _ = x.shape
    mix = jnp.concatenate([rms_norm(attn.reshape(B, L, MLA_WIDTH), lp['mla_out_norm']), gdn], axis=-1)
    x = x + jnp.einsum('blc,cd->bld', mix, lp['w_out'])
    h = rms_norm(x, lp['mlp_norm'])
    up = jnp.square(jax.nn.relu(jnp.einsum('bld,df->blf', h, lp['w_up'])))
    return x + jnp.einsum('blf,fd->bld', up, lp['w_down'])


def setup_inputs(seed: int = 0) -> dict:
    key = jax.random.key(seed)
    keys = jax.random.split(key, 28)
    f32 = jnp.float32

    def normal(i, shape, scale):
        return jax.random.normal(keys[i], shape, f32) * scale

    def gain(i, shape):
        return 1.0 + 0.01 * jax.random.normal(keys[i], shape, f32)

    dt = jnp.exp(jax.random.uniform(keys[24], (DEPTH, GDN_HEADS), f32, math.log(1e-3), math.log(1e-1)))
    return {
        'x_prompt': normal(0, (BATCH, SEQ, D_MODEL), 1.0),
        'x_sample': normal(1, (DEC_BATCH, DEC_SEQ, D_MODEL), 1.0),
        'cache_kv_latent': normal(2, (DEPTH, DEC_BATCH, N_META + PAST_LEN, KV_LORA), 1.0),
        'cache_k_rope': normal(3, (DEPTH, DEC_BATCH, N_META + PAST_LEN, ROPE_DIM), 1.0),
        'state_gdn': normal(4, (DEPTH, DEC_BATCH, GDN_HEADS, GDN_DK, GDN_DV), 0.5),
        'state_conv': normal(5, (DEPTH, DEC_BATCH, CONV_W - 1, CONV_CH), 1.0),
        'meta_tokens': normal(6, (N_META, D_MODEL), 1.0),
        'attn_norm': gain(7, (DEPTH, D_MODEL)),
        'w_in': normal(8, (DEPTH, D_MODEL, IN_COLS), D_MODEL ** -0.5),
        'q_a_norm': gain(9, (DEPTH, Q_LORA)),
        'w_uq': normal(10, (DEPTH, Q_LORA, MLA_HEADS * QK_HEAD), Q_LORA ** -0.5),
        'kv_a_norm': gain(11, (DEPTH, KV_LORA)),
        'w_uk': normal(12, (DEPTH, KV_LORA, MLA_HEADS * NOPE_DIM), KV_LORA ** -0.5),
        'w_uv': normal(13, (DEPTH, KV_LORA, MLA_HEADS * V_HEAD), KV_LORA ** -0.5),
        'q_norm': gain(14, (DEPTH, QK_HEAD)),
        'k_norm': gain(15, (DEPTH, QK_HEAD)),
        'mla_out_norm': gain(16, (DEPTH, MLA_WIDTH)),
        'conv_w': normal(17, (DEPTH, CONV_W, CONV_CH), CONV_W ** -0.5),
        'a_log': jnp.log(jax.random.uniform(keys[18], (DEPTH, GDN_HEADS), f32, 1.0, 16.0)),
        'dt_bias': dt + jnp.log(-jnp.expm1(-dt)),
        'gdn_out_norm': gain(19, (DEPTH, GDN_DV)),
        'w_out': normal(20, (DEPTH, MIX_WIDTH, D_MODEL), MIX_WIDTH ** -0.5),
        'mlp_norm': gain(21, (DEPTH, D_MODEL)),
        'w_up': normal(22, (DEPTH, D_MODEL, D_FF), D_MODEL ** -0.5),
        'w_down': normal(23, (DEPTH, D_FF, D_MODEL), D_FF ** -0.5),
    }


def reference(x_prompt, x_sample, cache_kv_latent, cache_k_rope, state_gdn, state_conv,
              meta_tokens, attn_norm, w_in, q_a_norm, w_uq, kv_a_norm, w_uk, w_uv, q_norm, k_norm,
              mla_out_norm, conv_w, a_log, dt_bias, gdn_out_norm, w_out, mlp_norm, w_up, w_down):
    B = x_prompt.shape[0]
    xp = jnp.concatenate([jnp.broadcast_to(meta_tokens.astype(x_prompt.dtype)[None], (B, N_META, D_MODEL)), x_prompt], axis=1)
    Lp = xp.shape[1]
    pos_p = jnp.arange(Lp)
    frame_p = pos_p - N_META
    chunk_p = jnp.where(frame_p < 0, -1, frame_p // CHUNK)
    xs = x_sample
    Bs, Ls = xs.shape[:2]
    Lc = cache_kv_latent.shape[2]
    pos_s = Lc + jnp.arange(Ls)
    pos_cs = jnp.arange(Lc + Ls)
    p_lat, p_rope, p_S, p_buf, s_lat, s_rope, s_S, s_buf = [], [], [], [], [], [], [], []
    for l in range(DEPTH):
        lp = {'attn_norm': attn_norm[l], 'w_in': w_in[l], 'q_a_norm': q_a_norm[l], 'w_uq': w_uq[l],
              'kv_a_norm': kv_a_norm[l], 'w_uk': w_uk[l], 'w_uv': w_uv[l], 'q_norm': q_norm[l],
              'k_norm': k_norm[l], 'mla_out_norm': mla_out_norm[l], 'conv_w': conv_w[l], 'a_log': a_log[l],
              'dt_bias': dt_bias[l], 'gdn_out_norm': gdn_out_norm[l], 'w_out': w_out[l],
              'mlp_norm': mlp_norm[l], 'w_up': w_up[l], 'w_down': w_down[l]}
        q, kv_lat, k_rope_raw, qkv, gate, bl, al = input_projection(xp, pos_p, lp)
        k, v = mla_keys_values(kv_lat, k_rope_raw, pos_p, lp)
        attn = chunk_causal_attention(q, k, v, chunk_p)
        gdn, S_new, buf_new = gdn_branch(qkv, gate, bl, al,
                                         jnp.zeros((B, CONV_W - 1, CONV_CH), xp.dtype),
                                         jnp.zeros((B, GDN_HEADS, GDN_DK, GDN_DV), xp.dtype), lp)
        xp = finish_layer(xp, attn, gdn, lp)
        p_lat.append(kv_lat)
        p_rope.append(k_rope_raw)
        p_S.append(S_new)
        p_buf.append(buf_new)
        q, kv_lat, k_rope_raw, qkv, gate, bl, al = input_projection(xs, pos_s, lp)
        lat_all = jnp.concatenate([cache_kv_latent[l].astype(kv_lat.dtype), kv_lat], axis=1)
        rope_all = jnp.concatenate([cache_k_rope[l].astype(k_rope_raw.dtype), k_rope_raw], axis=1)
        k, v = mla_keys_values(lat_all, rope_all, pos_cs, lp)
        attn = attend(q, k, v, None)
        gdn, S_new, buf_new = gdn_branch(qkv, gate, bl, al, state_conv[l], state_gdn[l], lp)
        xs = finish_layer(xs, attn, gdn, lp)
        s_lat.append(kv_lat)
        s_rope.append(k_rope_raw)
        s_S.append(S_new)
        s_buf.append(buf_new)
    return (xp[:, N_META:], xs, jnp.stack(p_lat), jnp.stack(p_rope), jnp.stack(p_S), jnp.stack(p_buf),
            jnp.stack(s_lat), jnp.stack(s_rope), jnp.stack(s_S), jnp.stack(s_buf))
```

```python
import numpy as np
import ml_dtypes
from contextlib import ExitStack
import concourse.bass as bass
import concourse.mybir as mybir
from concourse.bass_utils import run_bass_kernel_spmd

F32 = mybir.dt.float32
BF16 = mybir.dt.bfloat16
AF = mybir.ActivationFunctionType
ALU = mybir.AluOpType
AX = mybir.AxisListType
EPS = 1e-6
NEG = -30000.0
ENGS = ("sp", "act", "dve", "pool", "pe")


class Sched:
    def __init__(self, nc, tag):
        self.nc = nc
        self.tag = tag
        self.lists = {e: [] for e in ENGS}
        self.last_writer = {}
        self.readers = {}
        self.dma_count = {}
        self.waited = {e: {} for e in ENGS}

    def _deps(self, eng, reads, writes):
        deps = []
        for k in reads:
            w = self.last_writer.get(k)
            if w is not None:
                deps.append((w, True))
        for k in writes:
            w = self.last_writer.get(k)
            if w is not None:
                deps.append((w, False))
            for t in self.readers.get(k, {}).values():
                deps.append((t, False))
        out = []
        wd = self.waited[eng]
        for t, raw in deps:
            if t[0] == "E" and t[1] == eng and eng == "pe":
                continue
            sid = (t[0], t[1])
            if wd.get(sid, -1) >= t[2]:
                continue
            wd[sid] = t[2]
            out.append(t)
        return out

    def _commit(self, token, stream, reads, writes):
        for k in reads:
            self.readers.setdefault(k, {})[stream] = token
        for k in writes:
            self.last_writer[k] = token
            self.readers[k] = {}

    def op(self, eng, fn, reads=(), writes=()):
        deps = self._deps(eng, reads, writes)
        idx = len(self.lists[eng])
        self.lists[eng].append({"fn": fn, "deps": deps, "flag": False, "dma": None})
        self._commit(("E", eng, idx), eng, reads, writes)

    def dma(self, eng, key, fn, reads=(), writes=(), n=1):
        deps = self._deps(eng, reads, writes)
        c = self.dma_count.get(key, 0) + 16 * n
        self.dma_count[key] = c
        self.lists[eng].append({"fn": fn, "deps": deps, "flag": False, "dma": key})
        self._commit(("D", key, c), "D" + key, reads, writes)

    def emit(self):
        nc = self.nc
        for e in ENGS:
            for rec in self.lists[e]:
                for t in rec["deps"]:
                    if t[0] == "E":
                        self.lists[t[1]][t[2]]["flag"] = True
        val = {}
        for e in ENGS:
            c = 0
            v = []
            for rec in self.lists[e]:
                if rec["flag"] and rec["dma"] is None:
                    c += 1
                v.append(c)
            val[e] = v
        with ExitStack() as st:
            esem = {e: st.enter_context(nc.semaphore(self.tag + "s_" + e)) for e in ENGS}
            dsem = {k: st.enter_context(nc.semaphore(self.tag + "d_" + k)) for k in self.dma_count}
            block = st.enter_context(nc.Block())
            final = dict(self.dma_count)

            def run(e, engine):
                for rec in self.lists[e]:
                    for t in rec["deps"]:
                        if t[0] == "E":
                            engine.wait_ge(esem[t[1]], val[t[1]][t[2]])
                        else:
                            engine.wait_ge(dsem[t[1]], t[2])
                    r = rec["fn"](engine)
                    if rec["dma"] is not None:
                        if not isinstance(r, (list, tuple)):
                            r = [r]
                        for ins in r:
                            ins.then_inc(dsem[rec["dma"]], 16)
                    elif rec["flag"]:
                        r.then_inc(esem[e], 1)
                if e == "sp":
                    for k, c in final.items():
                        engine.wait_ge(dsem[k], c)

            @block.sync
            def _(eng):
                run("sp", eng)

            @block.scalar
            def _(eng):
                run("act", eng)

            @block.vector
            def _(eng):
                run("dve", eng)

            @block.gpsimd
            def _(eng):
                run("pool", eng)

            @block.tensor
            def _(eng):
                run("pe", eng)


class Ops:
    def __init__(self, S):
        self.S = S

    def act(self, out, in_, func, r, w, **kw):
        self.S.op("act", lambda e: e.activation(out=out, in_=in_, func=func, **kw), r, w)

    def tt(self, eng, out, in0, in1, op, r, w):
        self.S.op(eng, lambda e: e.tensor_tensor(out=out, in0=in0, in1=in1, op=op), r, w)

    def stt(self, eng, out, in0, scalar, in1, op0, op1, r, w):
        self.S.op(eng, lambda e: e.scalar_tensor_tensor(out=out, in0=in0, scalar=scalar, in1=in1, op0=op0, op1=op1), r, w)

    def tsm(self, eng, out, in0, s1, r, w):
        self.S.op(eng, lambda e: e.tensor_scalar_mul(out=out, in0=in0, scalar1=s1), r, w)

    def asc(self, out, in_, sc, r, w):
        self.S.op("act", lambda e: e.activation(out=out, in_=in_, func=AF.Copy, scale=sc), r, w)

    def tsa(self, eng, out, in0, s1, r, w):
        self.S.op(eng, lambda e: e.tensor_scalar_add(out=out, in0=in0, scalar1=s1), r, w)

    def cp(self, eng, out, in_, r, w):
        if eng == "act":
            self.S.op("act", lambda e: e.copy(out=out, in_=in_), r, w)
        else:
            self.S.op(eng, lambda e: e.tensor_copy(out=out, in_=in_), r, w)

    def recip(self, out, in_, r, w):
        self.S.op("dve", lambda e: e.reciprocal(out=out, in_=in_), r, w)

    def red(self, out, in_, r, w):
        self.S.op("dve", lambda e: e.tensor_reduce(out=out, in_=in_, axis=AX.X, op=ALU.add), r, w)

    def mset(self, eng, ap, v, w):
        self.S.op(eng, lambda e: e.memset(ap, v), (), w)

    def mm(self, out, lhsT, rhs, start, stop, r, w):
        self.S.op("pe", lambda e: e.matmul(out, lhsT=lhsT, rhs=rhs, start=start, stop=stop), r, w)

    def tr(self, out, in_, ident, r, w):
        self.S.op("pe", lambda e: e.transpose(out=out, in_=in_, identity=ident), r, w)

    def ld(self, key, out, in_, r, w, slow=False):
        if slow:
            self.S.dma("sp", key, lambda e: e.dma_start(out=out, in_=in_, allow_slow_non_contiguous=True), r, w)
        else:
            self.S.dma("sp", key, lambda e: e.dma_start(out=out, in_=in_), r, w)


def host_consts(npos):
    c = {}
    c["c_ident"] = np.eye(128, dtype=np.float32)
    p = np.arange(128)[:, None]
    q = np.arange(128)[None, :]
    c["c_U"] = (p <= q).astype(np.float32)
    c["c_mLs"] = np.where(p > q, 0.0, NEG).astype(np.float32)
    c["c_mU"] = np.where(q >= p, 0.0, NEG).astype(np.float32)
    lv = np.zeros((128, 7, 128), np.float32)
    for l in range(7):
        b = 1 << l
        lv[:, l, :] = ((p // (2 * b) == q // (2 * b)) & (p % (2 * b) >= b) & (q % (2 * b) < b)).astype(np.float32)
    c["c_lvl"] = lv
    half = 16
    inv_freq = (np.float32(10000.0) ** (-np.arange(half, dtype=np.float32) / np.float32(half))).astype(np.float32)
    ang = np.arange(npos, dtype=np.float32)[:, None] * inv_freq[None, :]
    c["c_cs"] = np.concatenate([np.cos(ang), np.sin(ang)], axis=1).astype(np.float32)
    return c


import os
_STOP = os.environ.get('KSTOP', '')
_CUT = float(os.environ.get('KCUT', '99'))


def build_nc(NP, SEQ, PAST):
    NFT = SEQ // 128
    LP = 16 + SEQ
    CFT = PAST // 128
    LC = 16 + PAST
    NSEQ = NP + 1
    NCMAX = max(LP, LC + 64)
    NKT = max(NFT + 1, CFT + 2)
    NQT = max(NFT, 1)
    NX1 = NP * NFT + 1
    NPOS = max(LP, LC + 64)

    nc = bass.Bass("TRN2", target_bir_lowering=False)

    def din(name, shape):
        return nc.dram_tensor(name, shape, F32, kind="ExternalInput").ap()

    def dout(name, shape):
        return nc.dram_tensor(name, shape, F32, kind="ExternalOutput").ap()

    xp = din("xp", [NP, SEQ, 1024])
    meta = din("meta", [16, 1024])
    xs = din("xs", [64, 1024])
    clat = din("clat", [LC, 256])
    crope = din("crope", [LC, 32])
    sgdn = din("sgdn", [4, 128, 128])
    sconv = din("sconv", [3, 1536])
    w_in = din("w_in", [1024, 2728])
    w_uq = din("w_uq", [384, 768])
    w_uk = din("w_uk", [256, 512])
    w_uv = din("w_uv", [256, 512])
    w_out = din("w_out", [1024, 1024])
    w_up = din("w_up", [1024, 4096])
    w_down = din("w_down", [4096, 1024])
    attn_norm = din("attn_norm", [1024])
    q_a_norm = din("q_a_norm", [384])
    kv_a_norm = din("kv_a_norm", [256])
    q_norm = din("q_norm", [96])
    k_norm = din("k_norm", [96])
    mla_out_norm = din("mla_out_norm", [512])
    conv_w = din("conv_w", [4, 1536])
    a_log = din("a_log", [4])
    dt_bias = din("dt_bias", [4])
    gdn_out_norm = din("gdn_out_norm", [128])
    mlp_norm = din("mlp_norm", [1024])
    c_ident = din("c_ident", [128, 128])
    c_U = din("c_U", [128, 128])
    c_mLs = din("c_mLs", [128, 128])
    c_mU = din("c_mU", [128, 128])
    c_lvl = din("c_lvl", [128, 7, 128])
    c_cs = din("c_cs", [NPOS, 32])

    y_p = dout("y_p", [NP, SEQ, 1024])
    y_s = dout("y_s", [64, 1024])
    p_lat = dout("p_lat", [NP, LP, 256])
    p_rope = dout("p_rope", [NP, LP, 32])
    p_S = dout("p_S", [NP, 4, 128, 128])
    p_conv = dout("p_conv", [NP, 3, 1536])
    s_lat = dout("s_lat", [64, 256])
    s_rope = dout("s_rope", [64, 32])
    s_S = dout("s_S", [4, 128, 128])
    s_conv = dout("s_conv", [3, 1536])

    KT_d = nc.dram_tensor("KT_d", [NSEQ, 128, 8, NCMAX], BF16, kind="Internal").ap()
    V_d = nc.dram_tensor("V_d", [NSEQ, NKT, 128, 520], BF16, kind="Internal").ap()
    qT_d = nc.dram_tensor("qT_d", [NSEQ, NQT, 128, 8, 128], BF16, kind="Internal").ap()
    gm_d = nc.dram_tensor("gm_d", [NSEQ, NQT, 128, 512], BF16, kind="Internal").ap()
    x1_d = nc.dram_tensor("x1_d", [NX1, 128, 1024], F32, kind="Internal").ap()

    ATT_SCALE = 96.0 ** -0.5

    with ExitStack() as st:
        def sb(name, shape, dt=F32):
            return st.enter_context(nc.sbuf_tensor(name, shape, dt))

        PS = st.enter_context(nc.psum_tensor("ps", [128, 4096], F32))
        PSB = PS[:].bitcast(BF16)

        def bank(b, w=512, off=0):
            return PS[:, b * 512 + off: b * 512 + off + w]

        S = Sched(nc, "a")
        O = Ops(S)

        w_in_b = sb("w_in_b", [128, 8, 2728], BF16)
        w_uq_b = sb("w_uq_b", [128, 3, 768], BF16)
        w_uk_b = sb("w_uk_b", [128, 2, 512], BF16)
        w_uv_b = sb("w_uv_b", [128, 2, 512], BF16)
        convd = sb("convd", [128, 48, 128], BF16)
        stg = sb("stg", [128, 2728], F32)
        gA = sb("gA", [128, 8])
        gQ = sb("gQ", [128, 3])
        cw = sb("cw", [128, 4, 12])
        ident_f = sb("ident_f", [128, 128])
        ident_b = sb("ident_b", [128, 128], BF16)
        U_f = sb("U_f", [128, 128])
        ones_f = sb("ones_f", [128, 128])
        mLs = sb("mLs", [128, 128])
        mU = sb("mU", [128, 128])
        lvl = sb("lvl", [128, 7, 128])
        gkv = sb("gkv", [128, 256])
        gq = sb("gq", [128, 96])
        gk = sb("gk", [128, 96])
        go = sb("go", [128, 128])
        negA = sb("negA", [128, 4])
        dtb = sb("dtb", [128, 4])
        eps_t = sb("eps_t", [128, 1])
        one_t = sb("one_t", [128, 1])

        O.ld("c0", ident_f[:], c_ident, [], ["ident_f"])
        O.ld("c1", U_f[:], c_U, [], ["U_f"])
        O.ld("c2", mLs[:], c_mLs, [], ["mLs"])
        O.ld("c3", mU[:], c_mU, [], ["mU"])
        O.ld("c4", lvl[:], c_lvl, [], ["lvl"])
        O.ld("c5", gkv[:], kv_a_norm.partition_broadcast(128), [], ["gkv"])
        O.ld("c6", gq[:], q_norm.partition_broadcast(128), [], ["gq"])
        O.ld("c7", gk[:], k_norm.partition_broadcast(128), [], ["gk"])
        O.ld("c8", go[:], gdn_out_norm.partition_broadcast(128), [], ["go"])
        O.ld("c9", negA[:], a_log.partition_broadcast(128), [], ["negA"])
        O.ld("c10", dtb[:], dt_bias.partition_broadcast(128), [], ["dtb"])
        tmpT = sb("tmpT", [48, 128])

        def ld_T(dst, src2d, k, key):
            O.ld("ldT", tmpT[:k, :], src2d, [], ["tmpT"])
            O.tr(PS[:, 0:k], tmpT[:k, :], ident_f[:k, :k], ["tmpT", "ident_f"], ["b0"])
            O.cp("dve", dst, PS[:, 0:k], ["b0"], [key])

        ld_T(gA[:, :], attn_norm.rearrange("(k p) -> k p", p=128), 8, "gA")
        ld_T(gQ[:, :], q_a_norm.rearrange("(k p) -> k p", p=128), 3, "gQ")
        ld_T(cw[:, :, :].rearrange("p w c -> p (w c)"), conv_w.rearrange("w (c p) -> (w c) p", p=128), 48, "cw")
        O.cp("dve", ident_b[:], ident_f[:], ["ident_f"], ["ident_b"])
        O.mset("dve", ones_f[:], 1.0, ["ones_f"])
        O.mset("dve", eps_t[:], EPS, ["eps_t"])
        O.mset("dve", one_t[:], 1.0, ["one_t"])
        O.act(negA[:], negA[:], AF.Exp, ["negA"], ["negA"])
        O.tsm("dve", negA[:], negA[:], -1.0, ["negA"], ["negA"])
        for kc in range(8):
            O.ld("stg", stg[:], w_in[kc * 128:(kc + 1) * 128, :], [], ["stg"])
            O.tsm("dve" if kc % 2 == 0 else "pool", w_in_b[:, kc, :], stg[:], gA[:, kc:kc + 1], ["stg", "gA"], ["w_in_b"])
        for kc in range(3):
            O.ld("stg", stg[:, 0:768], w_uq[kc * 128:(kc + 1) * 128, :], [], ["stg"])
            O.tsm("dve", w_uq_b[:, kc, :], stg[:, 0:768], gQ[:, kc:kc + 1], ["stg", "gQ"], ["w_uq_b"])
        for kc in range(2):
            O.ld("stg", stg[:, 0:512], w_uk[kc * 128:(kc + 1) * 128, :], [], ["stg"])
            O.cp("dve", w_uk_b[:, kc, :], stg[:, 0:512], ["stg"], ["w_uk_b"])
            O.ld("stg", stg[:, 0:512], w_uv[kc * 128:(kc + 1) * 128, :], [], ["stg"])
            O.cp("pool", w_uv_b[:, kc, :], stg[:, 0:512], ["stg"], ["w_uv_b"])
        for w4 in range(4):
            for c in range(12):
                O.tsm("dve" if c % 2 == 0 else "pool", convd[:, w4 * 12 + c, :], ident_f[:], cw[:, w4, c:c + 1],
                      ["ident_f", "cw"], ["convd"])

        xt2 = [sb("xt%d" % i, [128, 1024]) for i in range(2)]
        junk = sb("junk", [128, 1024], BF16)
        scF = sb("scF", [128, 768])
        scB = sb("scB", [128, 512])
        sc2 = sb("sc2", [128, 1024])
        st1 = sb("st1", [128, 64])
        xn = sb("xn", [128, 1024], BF16)
        xT = sb("xT", [128, 8, 128], BF16)
        zTh = sb("zTh", [128, 12, 131], BF16)
        sg2 = [sb("sg%d" % i, [128, 512]) for i in range(2)]
        hs2 = [sb("hs%d" % i, [128, 32]) for i in range(2)]
        k_tm2 = [sb("k_tm%d" % i, [128, 4, 128]) for i in range(2)]
        kbg2 = [sb("kbg%d" % i, [128, 4, 128], BF16) for i in range(2)]
        vb2 = [sb("vb%d" % i, [128, 4, 128], BF16) for i in range(2)]
        gkT2 = [sb("gkT%d" % i, [128, 4, 128], BF16) for i in range(2)]
        gqT2 = [sb("gqT%d" % i, [128, 4, 128], BF16) for i in range(2)]
        cst = sb("cst", [128, 1536])
        qln = sb("qln", [128, 384], BF16)
        qlT = sb("qlT", [128, 3, 128], BF16)
        qtmp = sb("qtmp", [128, 8, 96])
        rtmp = sb("rtmp", [128, 8, 64])
        q_pad = sb("q_pad", [128, 8, 128], BF16)
        qT = sb("qT", [128, 8, 128], BF16)
        lat_n = sb("lat_n", [128, 256])
        lat_b = sb("lat_b", [128, 256], BF16)
        latT = sb("latT", [128, 2, 128], BF16)
        rope_raw = sb("rope_raw", [128, 32])
        cs_t = sb("cs_t", [128, 32])
        rg = sb("rg", [128, 32])
        rr = sb("rr", [128, 32])
        t4 = sb("t4", [128, 64])
        k_pad = sb("k_pad", [128, 8, 128], BF16)
        kTt = sb("kTt", [128, 8, 128], BF16)
        v_b = sb("v_b", [128, 8, 65], BF16)
        csT = sb("csT", [128, 12, 128], BF16)
        qkv_tm = sb("qkv_tm", [128, 12, 128])
        e_dec = sb("e_dec", [128, 4])
        e_last = sb("e_last", [128, 4])
        rs8 = sb("rs8", [128, 8])
        UG = sb("UG", [128, 4, 128])
        dtmp = sb("dtmp", [128, 4, 128])
        MQs = sb("MQs", [128, 4, 128])
        MQT = sb("MQT", [128, 4, 128])
        A_b = sb("A_b", [128, 4, 128], BF16)
        At_b = sb("At_b", [128, 4, 128], BF16)
        TTb = sb("TTb", [128, 4, 2, 128], BF16)
        TTf = sb("TTf", [128, 4, 2, 128])
        X_b = sb("X_b", [128, 4, 128], BF16)
        k_tmb = sb("k_tmb", [128, 4, 128], BF16)
        q_tm = sb("q_tm", [128, 4, 128], BF16)
        kdec = sb("kdec", [128, 4, 128], BF16)
        u_sb = sb("u_sb", [128, 4, 128])
        wT_b = sb("wT_b", [128, 4, 128], BF16)
        vnb = sb("vnb", [128, 4, 128], BF16)
        qkT_b = sb("qkT_b", [128, 4, 128], BF16)
        oi = sb("oi", [128, 4, 128])
        o_sb = sb("o_sb", [128, 4, 128])
        Sg = sb("Sg", [128, 4, 128])
        Sgb = sb("Sgb", [128, 4, 128], BF16)
        gm_b = sb("gm_b", [128, 512], BF16)

        O.mset("dve", q_pad[:], 0.0, ["q_pad"])
        O.mset("dve", k_pad[:], 0.0, ["k_pad"])
        O.mset("dve", v_b[:], 1.0, ["v_b"])

        def rsq(out, in_, scale, n, r, w):
            O.act(out, in_, AF.Ln, r + ["eps_t"], w, scale=scale, bias=eps_t[:n, 0:1])
            O.act(out, out, AF.Exp, w, w, scale=-0.5)

        def rope_ops(o1, o2, x1, x2, cos, sin, ta, tb, tc, td, r, w, tk):
            O.tt("pool", ta, x1, cos, ALU.mult, r, [tk + "a"])
            O.tt("pool", tb, x2, sin, ALU.mult, r, [tk + "b"])
            O.tt("pool", tc, x2, cos, ALU.mult, r, [tk + "c"])
            O.tt("pool", td, x1, sin, ALU.mult, r, [tk + "d"])
            O.tt("pool", o1, ta, tb, ALU.subtract, [tk + "a", tk + "b"], w)
            O.tt("pool", o2, tc, td, ALU.add, [tk + "c", tk + "d"], w)

        tvF = PSB[:, 0:1024].rearrange("p (a b) -> p a b", a=8)

        def kv_build(q, j, n, col0, pos0):
            O.ld("cs", cs_t[:n, :], c_cs[pos0:pos0 + n, :], [], ["cs_t"])
            O.cp("act", lat_b[:n, :], lat_n[:n, :], ["lat_n"], ["lat_b"])
            for kc in range(2):
                O.tr(tvF[:, kc, :n], lat_b[:n, kc * 128:(kc + 1) * 128], ident_b[:n, :n], ["lat_b", "ident_b"], ["b0"])
            O.cp("dve", latT[:, :, :n], tvF[:, 0:2, :n], ["b0"], ["latT"])
            for kc in range(2):
                O.mm(bank(3)[:n, :], latT[:, kc, :n], w_uk_b[:, kc, :], kc == 0, kc == 1, ["latT", "w_uk_b"], ["b3"])
            for kc in range(2):
                O.mm(bank(4)[:n, :], latT[:, kc, :n], w_uv_b[:, kc, :], kc == 0, kc == 1, ["latT", "w_uv_b"], ["b4"])
            yield
            O.cp("act", v_b[:n, :, 0:64], bank(4)[:n, :].rearrange("p (h d) -> p h d", h=8), ["b4"], ["v_b"])
            O.S.dma("sp", "vd", lambda e: e.dma_start(out=V_d[q, j, 0:n, :], in_=v_b[:n, :, :].rearrange("p h d -> p (h d)")),
                    ["v_b"], [])
            O.act(scF[:n, 0:512], bank(3)[:n, :], AF.Square, ["b3"], ["scF"])
            O.red(st1[:n, 0:8], scF[:n, 0:512].rearrange("p (h d) -> p h d", h=8), ["scF"], ["st_k"])
            O.act(junk[:n, 0:32], rope_raw[:n, :], AF.Square, ["rope_raw"], ["junk", "st_kr"], accum_out=st1[:n, 8:9])
            O.tsa("dve", st1[:n, 0:8], st1[:n, 0:8], st1[:n, 8:9], ["st_k", "st_kr"], ["st_k"])
            rsq(st1[:n, 0:8], st1[:n, 0:8], 1.0 / 96, n, ["st_k"], ["st_k"])
            yield
            kn3 = bank(3)[:n, :].rearrange("p (h d) -> p h d", h=8)
            O.tt("dve", rtmp[:n, :, :], kn3, st1[:n, 0:8].unsqueeze(2).to_broadcast([n, 8, 64]), ALU.mult,
                 ["b3", "st_k"], ["rtmp"])
            O.tt("pool", k_pad[:n, :, 0:64], rtmp[:n, :, :], gk[:n, 0:64].unsqueeze(1).to_broadcast([n, 8, 64]), ALU.mult,
                 ["rtmp", "gk"], ["k_pad"])
            O.tt("pool", rg[:n, :], rope_raw[:n, :], gk[:n, 64:96], ALU.mult, ["rope_raw", "gk"], ["rg"])
            rope_ops(rr[:n, 0:16], rr[:n, 16:32], rg[:n, 0:16], rg[:n, 16:32], cs_t[:n, 0:16], cs_t[:n, 16:32],
                     t4[:n, 0:16], t4[:n, 16:32], t4[:n, 32:48], t4[:n, 48:64], ["rg", "cs_t"], ["rr"], "t4")
            O.tt("dve", k_pad[:n, :, 64:96], rr[:n, :].unsqueeze(1).to_broadcast([n, 8, 32]),
                 st1[:n, 0:8].unsqueeze(2).to_broadcast([n, 8, 32]), ALU.mult, ["rr", "st_k"], ["k_pad"])
            yield
            for h in range(8):
                O.tr(tvF[:, h, :n], k_pad[:n, h, :], ident_b[:n, :n], ["k_pad", "ident_b"], ["b0"])
            O.cp("act", kTt[:, :, :n], tvF[:, :, :n], ["b0"], ["kTt"])
            O.S.dma("sp", "ktd", lambda e: e.dma_start(out=KT_d[q, :, :, col0:col0 + n], in_=kTt[:, :, :n]), ["kTt"], [])
            yield

        def cache_kv(q, j, n, row0):
            O.ld("lat", lat_n[:n, :], clat[row0:row0 + n, :], [], ["lat_n"])
            O.ld("rop", rope_raw[:n, :], crope[row0:row0 + n, :], [], ["rope_raw"])
            yield from kv_build(q, j, n, row0, row0)

        def load_x(c):
            O.ld("xt%d" % c["par"], xt2[c["par"]][:c["n"], :], c["x_src"], [], ["xt%d" % c["par"]])

        def front(c, nxt):
            q, j, qi, n, pos0 = c["q"], c["j"], c["qi"], c["n"], c["pos0"]
            need_out, first, last, is_sample, par = c["need_out"], c["first"], c["last"], c["is_sample"], c["par"]
            xt = xt2[par]
            xk, zk, sgk, hk = "xt%d" % par, "zTh", "sg%d" % par, "hs%d" % par
            hs, k_tm, kbg, vb, gkT, gqT = hs2[par], k_tm2[par], kbg2[par], vb2[par], gkT2[par], gqT2[par]
            beta, g_t, gc, ngc, e_gc, bge = (hs[:, 0:4], hs[:, 4:8], hs[:, 8:12], hs[:, 12:16], hs[:, 16:20], hs[:, 20:24])
            if c["load_self"]:
                load_x(c)
            if nxt is not None:
                load_x(nxt)
            O.act(junk[:n, 0:1024], xt[:n, :], AF.Square, [xk], ["junk", "st_x"], accum_out=st1[:n, 16:17])
            rsq(st1[:n, 16:17], st1[:n, 16:17], 1.0 / 1024, n, ["st_x"], ["st_x"])
            O.asc(xn[:n, :], xt[:n, :], st1[:n, 16:17], [xk, "st_x"], ["xn"])
            for kc in range(8):
                O.tr(tvF[:, kc, :n], xn[:n, kc * 128:(kc + 1) * 128], ident_b[:n, :n], ["xn", "ident_b"], ["b0"])
            O.cp("dve", xT[:, :, :n], tvF[:, :, :n], ["b0"], ["xT"])
            yield
            Z1a, Z1b, Z3, Z4 = bank(1), bank(2, 160), bank(3), bank(2, 8, 256)
            for (dst, c0, c1, key) in ((Z1a, 0, 512, "b1"), (Z1b, 512, 672, "b2"), (Z3, 2208, 2720, "b3"), (Z4, 2720, 2728, "b2x")):
                for kc in range(8):
                    O.mm(dst[:n, :], xT[:, kc, :n], w_in_b[:, kc, c0:c1], kc == 0, kc == 7, ["xT", "w_in_b"], [key])
            yield
            if need_out:
                sg = sg2[par]
                O.act(sg[:n, :], Z3[:n, :], AF.Exp, ["b3"], [sgk], scale=-1.0)
                O.tsa("dve", sg[:n, :], sg[:n, :], 1.0, [sgk], [sgk])
                O.recip(sg[:n, :], sg[:n, :], [sgk], [sgk])
                O.tt("dve", sg[:n, :], sg[:n, :], Z3[:n, :], ALU.mult, [sgk, "b3"], [sgk])
            if first:
                if is_sample:
                    ld_T(cst[:, 0:36], sconv.rearrange("w (c p) -> (w c) p", p=128), 36, "cst")
                    for w3 in range(3):
                        O.cp("dve", zTh[:, :, w3], cst[:, w3 * 12:(w3 + 1) * 12], ["cst"], [zk])
                else:
                    O.mset("dve", zTh[:, :, 0:3], 0.0, [zk])
            else:
                pn = c["prev_n"]
                O.cp("pool", zTh[:, :, 0:3], zTh[:, :, pn:pn + 3], [zk], [zk])
            yield
            for g in range(3):
                bk = 4 if g % 2 == 0 else 3
                zc = bank(bk)[:, 0:4 * n].rearrange("p (c t) -> p c t", c=4)
                for c4 in range(4):
                    cc = 4 * g + c4
                    for kc in range(8):
                        O.mm(zc[:, c4, :], w_in_b[:, kc, 672 + cc * 128: 672 + (cc + 1) * 128], xT[:, kc, :n], kc == 0, kc == 7,
                             ["xT", "w_in_b"], ["b%d" % bk])
                O.cp("act", zTh[:, 4 * g:4 * g + 4, 3:3 + n], zc[:, :, :], ["b%d" % bk], [zk])
                yield
            if last:
                for c3 in range(3):
                    bk = 3 if c3 % 2 == 0 else 4
                    for kc in range(8):
                        O.mm(bank(bk)[:3, :], xT[:, kc, n - 3:n], w_in_b[:, kc, 672 + c3 * 512: 672 + (c3 + 1) * 512],
                             kc == 0, kc == 7, ["xT", "w_in_b"], ["b%d" % bk])
                    O.cp("act", cst[:3, c3 * 512:(c3 + 1) * 512], bank(bk)[:3, :], ["b%d" % bk], ["cst"])
                conv_out = c["conv_out"]
                O.S.dma("sp", "cso", lambda e: e.dma_start(out=conv_out, in_=cst[:3, :]), ["cst"], [])
                yield
            if need_out:
                O.act(junk[:n, 0:384], Z1a[:n, 0:384], AF.Square, ["b1"], ["junk", "st_q"], accum_out=st1[:n, 17:18])
                rsq(st1[:n, 17:18], st1[:n, 17:18], 1.0 / 384, n, ["st_q"], ["st_q"])
                O.act(qln[:n, :], Z1a[:n, 0:384], AF.Copy, ["b1", "st_q"], ["qln"], scale=st1[:n, 17:18])
                for kc in range(3):
                    O.tr(tvF[:, kc, :n], qln[:n, kc * 128:(kc + 1) * 128], ident_b[:n, :n], ["qln", "ident_b"], ["b0"])
                O.cp("dve", qlT[:, :, :n], tvF[:, 0:3, :n], ["b0"], ["qlT"])
                qr = PS[:, 3 * 512: 3 * 512 + 768]
                for (c0, c1) in ((0, 512), (512, 768)):
                    for kc in range(3):
                        O.mm(qr[:n, c0:c1], qlT[:, kc, :n], w_uq_b[:, kc, c0:c1], kc == 0, kc == 2, ["qlT", "w_uq_b"], ["b3", "b4"])
                yield
                O.act(scF[:n, 0:768], qr[:n, :], AF.Square, ["b3", "b4"], ["scF"])
                O.red(st1[:n, 24:32], scF[:n, 0:768].rearrange("p (h d) -> p h d", h=8), ["scF"], ["st_qh"])
                rsq(st1[:n, 24:32], st1[:n, 24:32], 1.0 / 96, n, ["st_qh"], ["st_qh"])
                qr3 = qr[:n, :].rearrange("p (h d) -> p h d", h=8)
                O.tt("dve", qtmp[:n, :, :], qr3, st1[:n, 24:32].unsqueeze(2).to_broadcast([n, 8, 96]), ALU.mult,
                     ["b3", "b4", "st_qh"], ["qtmp"])
                O.tt("pool", qtmp[:n, :, :], qtmp[:n, :, :], gq[:n, :].unsqueeze(1).to_broadcast([n, 8, 96]), ALU.mult,
                     ["qtmp", "gq"], ["qtmp"])
                O.ld("cs", cs_t[:n, :], c_cs[pos0:pos0 + n, :], [], ["cs_t"])
                cosb = cs_t[:n, 0:16].unsqueeze(1).to_broadcast([n, 8, 16])
                sinb = cs_t[:n, 16:32].unsqueeze(1).to_broadcast([n, 8, 16])
                rope_ops(q_pad[:n, :, 64:80], q_pad[:n, :, 80:96], qtmp[:n, :, 64:80], qtmp[:n, :, 80:96], cosb, sinb,
                         rtmp[:n, :, 0:16], rtmp[:n, :, 16:32], rtmp[:n, :, 32:48], rtmp[:n, :, 48:64],
                         ["qtmp", "cs_t"], ["q_pad"], "rtmp")
                O.cp("dve", q_pad[:n, :, 0:64], qtmp[:n, :, 0:64], ["qtmp"], ["q_pad"])
                yield
                for h in range(8):
                    O.tr(tvF[:, h, :n], q_pad[:n, h, :], ident_b[:n, :n], ["q_pad", "ident_b"], ["b0"])
                O.cp("act", qT[:, :, :n], tvF[:, :, :n], ["b0"], ["qT"])
                O.S.dma("sp", "qtd", lambda e: e.dma_start(out=qT_d[q, qi, :, :, 0:n], in_=qT[:, :, :n]), ["qT"], [])
                yield
            kvl = PS[:, 512 + 384: 512 + 640]
            O.act(junk[:n, 0:256], kvl[:n, :], AF.Square, ["b1", "b2"], ["junk", "st_kv"], accum_out=st1[:n, 18:19])
            rsq(st1[:n, 18:19], st1[:n, 18:19], 1.0 / 256, n, ["st_kv"], ["st_kv"])
            O.stt("dve", lat_n[:n, :], kvl[:n, :], st1[:n, 18:19], gkv[:n, :], ALU.mult, ALU.mult,
                  ["b1", "b2", "st_kv", "gkv"], ["lat_n"])
            O.cp("act", rope_raw[:n, :], Z1b[:n, 128:160], ["b2"], ["rope_raw"])
            lat_out, rope_out = c["lat_out"], c["rope_out"]
            O.S.dma("sp", "lato", lambda e: e.dma_start(out=lat_out, in_=lat_n[:n, :]), ["lat_n"], [])
            O.S.dma("sp", "ropeo", lambda e: e.dma_start(out=rope_out, in_=rope_raw[:n, :]), ["rope_raw"], [])
            yield
            yield from kv_build(q, j, n, pos0, pos0)
            O.act(beta[:n, :], Z4[:n, 0:4], AF.Exp, ["b2x"], [hk], scale=-1.0)
            O.tsa("dve", beta[:n, :], beta[:n, :], 1.0, [hk], [hk])
            O.recip(beta[:n, :], beta[:n, :], [hk], [hk])
            O.tt("dve", g_t[:n, :], Z4[:n, 4:8], dtb[:n, :], ALU.add, ["b2x", "dtb"], [hk])
            O.act(g_t[:n, :], g_t[:n, :], AF.Exp, [hk], [hk])
            O.act(g_t[:n, :], g_t[:n, :], AF.Ln, [hk, "one_t"], [hk], bias=one_t[:n, 0:1])
            O.tt("dve", g_t[:n, :], g_t[:n, :], negA[:n, :], ALU.mult, [hk, "negA"], [hk])
            gcp = bank(2, 4, 300)
            O.mm(gcp[:n, :], U_f[:n, :n], g_t[:n, :], True, True, ["U_f", hk], ["b2y"])
            O.cp("dve", gc[:n, :], gcp[:n, :], ["b2y"], [hk])
            O.tsm("dve", ngc[:n, :], gc[:n, :], -1.0, [hk], [hk])
            O.act(e_gc[:n, :], gc[:n, :], AF.Exp, [hk], [hk])
            O.tt("dve", bge[:n, :], beta[:n, :], e_gc[:n, :], ALU.mult, [hk], [hk])
            yield
            g_lo = 0 if need_out else 1
            c_lo = 4 * g_lo
            for g in range(g_lo, 3):
                bk = 3 if g % 2 == 0 else 4
                cps = bank(bk)[:, 0:4 * n].rearrange("p (c t) -> p c t", c=4)
                for c4 in range(4):
                    cc = 4 * g + c4
                    for w4 in range(4):
                        O.mm(cps[:, c4, :], convd[:, w4 * 12 + cc, :], zTh[:, cc, w4:w4 + n], w4 == 0, w4 == 3,
                             ["convd", zk], ["b%d" % bk])
                sv = scB[:, 0:4 * n].rearrange("p (c t) -> p c t", c=4)
                O.act(sv, cps, AF.Exp, ["b%d" % bk], ["scB"], scale=-1.0)
                O.tsa("dve", sv, sv, 1.0, ["scB"], ["scB"])
                O.recip(sv, sv, ["scB"], ["scB"])
                O.tt("dve", csT[:, 4 * g:4 * g + 4, :n], cps, sv, ALU.mult, ["b%d" % bk, "scB"], ["csT"])
                yield
            tvg = PSB[:, 0:512].rearrange("p (a b) -> p a b", a=4)
            for g in range(g_lo, 3):
                for c4 in range(4):
                    O.tr(tvg[:n, c4, :], csT[:, 4 * g + c4, :n], ident_b[:, :], ["csT", "ident_b"], ["b0"])
                O.cp("act", qkv_tm[:n, 4 * g:4 * g + 4, :], tvg[:n, :, :], ["b0"], ["qkv_tm"])
            yield
            sc2v = sc2[:, :].rearrange("p (c t) -> p c t", c=8)
            O.act(sc2v[:n, c_lo:8, :], qkv_tm[:n, c_lo:8, :], AF.Square, ["qkv_tm"], ["sc2"])
            O.red(rs8[:n, c_lo:8], sc2v[:n, c_lo:8, :], ["sc2"], ["rs8"])
            O.act(rs8[:n, c_lo:8], rs8[:n, c_lo:8], AF.Ln, ["rs8", "eps_t"], ["rs8"], bias=eps_t[:n, 0:1])
            O.act(rs8[:n, c_lo:8], rs8[:n, c_lo:8], AF.Exp, ["rs8"], ["rs8"], scale=-0.5)
            if need_out:
                O.tsm("dve", rs8[:n, 0:4], rs8[:n, 0:4], 128.0 ** -0.5, ["rs8"], ["rs8"])
                O.tt("dve", q_tm[:n, :, :], qkv_tm[:n, 0:4, :], rs8[:n, 0:4].unsqueeze(2).to_broadcast([n, 4, 128]), ALU.mult,
                     ["qkv_tm", "rs8"], ["q_tm"])
            O.tt("dve", k_tm[:n, :, :], qkv_tm[:n, 4:8, :], rs8[:n, 4:8].unsqueeze(2).to_broadcast([n, 4, 128]), ALU.mult,
                 ["qkv_tm", "rs8"], ["k_tm%d" % par])
            O.tt("dve", k_tmb[:n, :, :], qkv_tm[:n, 4:8, :], rs8[:n, 4:8].unsqueeze(2).to_broadcast([n, 4, 128]), ALU.mult,
                 ["qkv_tm", "rs8"], ["k_tmb"])
            yield
            for h in range(4):
                O.tsm("dve", kbg[:n, h, :], k_tm[:n, h, :], bge[:n, h:h + 1], ["k_tm%d" % par, hk], ["kbg%d" % par])
                O.tsm("pool", vb[:n, h, :], qkv_tm[:n, 8 + h, :], beta[:n, h:h + 1], ["qkv_tm", hk], ["vb%d" % par])
            for h in range(4):
                O.tr(tvF[:, h, :n], k_tmb[:n, h, :], ident_b[:n, :n], ["k_tmb", "ident_b"], ["b0"])
            if need_out:
                for h in range(4):
                    O.tr(tvF[:, 4 + h, :n], q_tm[:n, h, :], ident_b[:n, :n], ["q_tm", "ident_b"], ["b0"])
                O.cp("act", gqT[:, :, :n], tvF[:, 4:8, :n], ["b0"], ["gqT%d" % par])
            O.cp("dve", gkT[:, :, :n], tvF[:, 0:4, :n], ["b0"], ["gkT%d" % par])
            yield

        tvB = PSB[:, 5 * 1024: 6 * 1024].rearrange("p (a b) -> p a b", a=8)

        def back(c):
            q, j, qi, n, pos0 = c["q"], c["j"], c["qi"], c["n"], c["pos0"]
            need_out, first, last, is_sample, par = c["need_out"], c["first"], c["last"], c["is_sample"], c["par"]
            sgk, hk = "sg%d" % par, "hs%d" % par
            hs, k_tm, kbg, vb, gkT, gqT = hs2[par], k_tm2[par], kbg2[par], vb2[par], gkT2[par], gqT2[par]
            beta, g_t, gc, ngc, e_gc, bge = (hs[:, 0:4], hs[:, 4:8], hs[:, 8:12], hs[:, 12:16], hs[:, 16:20], hs[:, 20:24])
            kk, kbk, vbk, gkk, gqk = "k_tm%d" % par, "kbg%d" % par, "vb%d" % par, "gkT%d" % par, "gqT%d" % par
            O.tt("dve", UG[:n, :, :n], U_f[:n, :n].unsqueeze(1).to_broadcast([n, 4, n]),
                 g_t[:n, :].unsqueeze(2).to_broadcast([n, 4, n]), ALU.mult, ["U_f", hk], ["UG"])
            Grow = bank(6)[:, 0:4 * n].rearrange("p (h t) -> p h t", h=4)
            O.mm(Grow, ones_f[:n, :], UG[:n, :, :n], True, True, ["ones_f", "UG"], ["b6"])
            O.tt("dve", e_dec[:n, :], Grow[:n, :, n - 1], gc[:n, :], ALU.subtract, ["b6", hk], ["e_dec"])
            O.act(e_dec[:n, :], e_dec[:n, :], AF.Exp, ["e_dec"], ["e_dec"])
            O.act(e_last[:, :], Grow[:, :, n - 1], AF.Exp, ["b6"], ["e_last"])
            for h in range(4):
                O.tt("dve", dtmp[:n, h, :n], Grow[:n, h, :], mLs[:n, :n], ALU.subtract, ["b6", "mLs"], ["dtmp"])
            for h in range(4):
                O.act(MQs[:n, h, :n], dtmp[:n, h, :n], AF.Exp, ["dtmp", hk], ["MQs"], bias=gc[:n, h:h + 1], scale=-1.0)
            yield
            if need_out:
                for h in range(4):
                    O.tt("dve", dtmp[:n, h, :n], Grow[:n, h, :], mU[:n, :n], ALU.add, ["b6", "mU"], ["dtmp"])
                for h in range(4):
                    O.act(MQT[:n, h, :n], dtmp[:n, h, :n], AF.Exp, ["dtmp", hk], ["MQT"], bias=ngc[:n, h:h + 1])
                yield
            for h in range(4):
                O.tsm("pool", kdec[:n, h, :], k_tm[:n, h, :], e_dec[:n, h:h + 1], [kk, "e_dec"], ["kdec"])
            Gp = bank(6)[:, 0:4 * n].rearrange("p (h t) -> p h t", h=4)
            for h in range(4):
                O.mm(Gp[:n, h, :], gkT[:, h, :n], gkT[:, h, :n], True, True, [gkk], ["b6"])
            for h in range(4):
                O.tsm("dve", dtmp[:n, h, :n], Gp[:n, h, :], beta[:n, h:h + 1], ["b6", hk], ["dtmp"])
            O.tt("dve", A_b[:n, :, :n], dtmp[:n, :, :n], MQs[:n, :, :n], ALU.mult, ["dtmp", "MQs"], ["A_b"])
            tv7 = PSB[:, 7 * 1024: 8 * 1024].rearrange("p (a b) -> p a b", a=8)
            for h in range(4):
                O.tr(tv7[:n, h, :n], A_b[:n, h, :n], ident_b[:n, :n], ["A_b", "ident_b"], ["b7"])
            O.cp("act", At_b[:n, :, :n], tv7[:n, 0:4, :n], ["b7"], ["At_b"])
            for s2 in range(2):
                O.cp("pool", TTb[:n, :, s2, :n], ident_f[:n, :n].unsqueeze(1).to_broadcast([n, 4, n]), ["ident_f"], ["TTb"])
                O.cp("pool", TTf[:n, :, s2, :n], ident_f[:n, :n].unsqueeze(1).to_broadcast([n, 4, n]), ["ident_f"], ["TTf"])
            yield
            Xp = bank(5)[:, 0:4 * n].rearrange("p (h t) -> p h t", h=4)
            Yp = PS[:, 6 * 512: 8 * 512].rearrange("p (h s t) -> p h s t", h=4, s=2)
            l = 0
            while (1 << l) < n:
                for h in range(4):
                    O.mm(Xp[:n, h, :], At_b[:n, h, :n], TTb[:n, h, 0, :n], True, True, ["At_b", "TTb"], ["b5"])
                for h in range(4):
                    O.tt("dve", X_b[:n, h, :n], Xp[:n, h, :], lvl[:n, l, :n], ALU.mult, ["b5", "lvl"], ["X_b"])
                for h in range(4):
                    O.mm(Yp[:n, h, 0, :n], TTb[:n, h, 1, :n], X_b[:n, h, :n], True, True, ["TTb", "X_b"], ["b6", "b7"])
                    O.mm(Yp[:n, h, 1, :n], X_b[:n, h, :n], TTb[:n, h, 1, :n], True, True, ["TTb", "X_b"], ["b6", "b7"])
                O.tt("dve", TTf[:n, :, :, :n], TTf[:n, :, :, :n], Yp[:n, :, :, :n], ALU.subtract, ["TTf", "b6", "b7"], ["TTf"])
                O.cp("act", TTb[:n, :, :, :n], TTf[:n, :, :, :n], ["TTf"], ["TTb"])
                l += 1
                yield
            up_ = bank(5)[:, :].rearrange("p (h t) -> p h t", h=4)
            wTp = bank(6)[:, 0:4 * n].rearrange("p (h t) -> p h t", h=4)
            for h in range(4):
                O.mm(up_[:n, h, :], TTb[:n, h, 1, :n], vb[:n, h, :], True, True, ["TTb", vbk], ["b5"])
            for h in range(4):
                O.mm(wTp[:, h, :], kbg[:n, h, :], TTb[:n, h, 1, :n], True, True, ["TTb", kbk], ["b6"])
            O.cp("act", u_sb[:n, :, :], up_[:n, :, :], ["b5"], ["u_sb"])
            O.cp("act", wT_b[:, :, :n], wTp[:, :, :], ["b6"], ["wT_b"])
            yield
            if first:
                if is_sample:
                    O.ld("sg", Sg[:, :, :], sgdn.rearrange("h k v -> k h v"), [], ["Sg"])
                else:
                    O.mset("pool", Sg[:, :, :], 0.0, ["Sg"])
                O.cp("act", Sgb[:, :, :], Sg[:, :, :], ["Sg"], ["Sgb"])
            wSp = bank(7)[:, :].rearrange("p (h t) -> p h t", h=4)
            qSp = bank(5)[:, :].rearrange("p (h t) -> p h t", h=4)
            qkTp = bank(6)[:, 0:4 * n].rearrange("p (h t) -> p h t", h=4)
            for h in range(4):
                O.mm(wSp[:n, h, :], wT_b[:, h, :n], Sgb[:, h, :], True, True, ["wT_b", "Sgb"], ["b7"])
            O.tt("dve", vnb[:n, :, :], u_sb[:n, :, :], wSp[:n, :, :], ALU.subtract, ["u_sb", "b7"], ["vnb"])
            if need_out:
                for h in range(4):
                    O.mm(qSp[:n, h, :], gqT[:, h, :n], Sgb[:, h, :], True, True, [gqk, "Sgb"], ["b5"])
                for h in range(4):
                    O.mm(qkTp[:n, h, :], gkT[:, h, :n], gqT[:, h, :n], True, True, [gkk, gqk], ["b6"])
                for h in range(4):
                    O.tsm("dve", oi[:n, h, :], qSp[:n, h, :], e_gc[:n, h:h + 1], ["b5", hk], ["oi"])
                O.tt("dve", qkT_b[:n, :, :n], qkTp[:n, :, :], MQT[:n, :, :n], ALU.mult, ["b6", "MQT"], ["qkT_b"])
                o2p = bank(7)[:, :].rearrange("p (h t) -> p h t", h=4)
                for h in range(4):
                    O.mm(o2p[:n, h, :], qkT_b[:n, h, :n], vnb[:n, h, :], True, True, ["qkT_b", "vnb"], ["b7"])
                O.tt("dve", o_sb[:n, :, :], oi[:n, :, :], o2p[:n, :, :], ALU.add, ["oi", "b7"], ["o_sb"])
            yield
            dSp = bank(5)[:, :].rearrange("p (h t) -> p h t", h=4)
            for h in range(4):
                O.mm(dSp[:, h, :], kdec[:n, h, :], vnb[:n, h, :], True, True, ["kdec", "vnb"], ["b5"])
            for h in range(4):
                O.tsm("dve", Sg[:, h, :], Sg[:, h, :], e_last[:, h:h + 1], ["Sg", "e_last"], ["Sg"])
            O.tt("dve", Sg[:, :, :], Sg[:, :, :], dSp[:, :, :], ALU.add, ["Sg", "b5"], ["Sg"])
            O.cp("act", Sgb[:, :, :], Sg[:, :, :], ["Sg"], ["Sgb"])
            if last:
                S_out = c["S_out"]
                O.S.dma("sp", "so", lambda e: e.dma_start(out=S_out.rearrange("h k v -> k h v"), in_=Sg[:, :, :]), ["Sg"], [])
            if need_out:
                sg = sg2[par]
                O.act(sc2[:n, 0:512], o_sb[:n, :, :].rearrange("p h d -> p (h d)"), AF.Square, ["o_sb"], ["sc2"])
                O.red(st1[:n, 40:44], sc2[:n, 0:512].rearrange("p (h d) -> p h d", h=4), ["sc2"], ["st_o"])
                rsq(st1[:n, 40:44], st1[:n, 40:44], 1.0 / 128, n, ["st_o"], ["st_o"])
                for h in range(4):
                    O.tsm("dve", o_sb[:n, h, :], o_sb[:n, h, :], st1[:n, 40 + h:41 + h], ["o_sb", "st_o"], ["o_sb"])
                O.tt("pool", o_sb[:n, :, :], o_sb[:n, :, :], go[:n, :].unsqueeze(1).to_broadcast([n, 4, 128]), ALU.mult,
                     ["o_sb", "go"], ["o_sb"])
                O.tt("dve", gm_b[:n, :], o_sb[:n, :, :].rearrange("p h d -> p (h d)"), sg[:n, :], ALU.mult, ["o_sb", sgk], ["gm_b"])
                O.S.dma("sp", "gmd", lambda e: e.dma_start(out=gm_d[q, qi, 0:n, :], in_=gm_b[:n, :]), ["gm_b"], [])
            yield

        jobs = []
        if _STOP != 'prep':
            for s in range(NP):
                jobs.append(dict(q=s, j=0, qi=0, n=16, pos0=0, x_src=meta, need_out=False, first=True, last=False, is_sample=False,
                                 lat_out=p_lat[s, 0:16, :], rope_out=p_rope[s, 0:16, :], conv_out=None, S_out=None, prev_n=0))
                for i in range(NFT):
                    jobs.append(dict(q=s, j=i + 1, qi=i, n=128, pos0=16 + 128 * i, x_src=xp[s, 128 * i:128 * (i + 1), :],
                                     need_out=True, first=False, last=(i == NFT - 1), is_sample=False,
                                     lat_out=p_lat[s, 16 + 128 * i:16 + 128 * (i + 1), :],
                                     rope_out=p_rope[s, 16 + 128 * i:16 + 128 * (i + 1), :],
                                     conv_out=p_conv[s], S_out=p_S[s], prev_n=(16 if i == 0 else 128)))
            jobs.append(dict(q=NP, j=CFT + 1, qi=0, n=64, pos0=LC, x_src=xs, need_out=True, first=True, last=True, is_sample=True,
                             lat_out=s_lat, rope_out=s_rope, conv_out=s_conv, S_out=s_S, prev_n=0))
        for k, c in enumerate(jobs):
            c["par"] = k % 2
            c["load_self"] = (k == 0)
        cache_jobs = [(NP, 0, 16, 0)] + [(NP, i + 1, 128, 16 + 128 * i) for i in range(CFT)]
        if _STOP == 'prep':
            cache_jobs = []

        def run_all(g):
            for _ in g:
                pass

        def interleave(ga, gb):
            da = db = False
            while not (da and db):
                if not da:
                    try:
                        next(ga)
                    except StopIteration:
                        da = True
                if not db:
                    try:
                        next(gb)
                    except StopIteration:
                        db = True

        def front_plus(k):
            yield from front(jobs[k], jobs[k + 1] if k + 1 < len(jobs) else None)
            if cache_jobs and (k % 2 == 1 or len(jobs) - k <= len(cache_jobs)):
                cj = cache_jobs.pop(0)
                yield from cache_kv(*cj)

        if jobs:
            run_all(front_plus(0))
            for k in range(len(jobs)):
                if k + 1 < len(jobs):
                    interleave(back(jobs[k]), front_plus(k + 1))
                else:
                    run_all(back(jobs[k]))
        while cache_jobs:
            run_all(cache_kv(*cache_jobs.pop(0)))
        S.emit()
    if _STOP in ('prep', '1a'):
        return nc

    nc.all_engine_barrier()

    with ExitStack() as st:
        def sb(name, shape, dt=F32):
            return st.enter_context(nc.sbuf_tensor(name, shape, dt))

        PS = st.enter_context(nc.psum_tensor("psb", [128, 4096], F32))
        PSB = PS[:].bitcast(BF16)

        def bank(b, w=512, off=0):
            return PS[:, b * 512 + off: b * 512 + off + w]

        S = Sched(nc, "b")
        O = Ops(S)
        KTc = sb("KTc", [128, 8, NCMAX], BF16)
        Vc = sb("Vc", [128, NKT, 520], BF16)
        w_out_b = sb("w_out_b", [128, 8, 1024], BF16)
        stg = sb("stgb", [128, 1024])
        gM = sb("gM", [128, 4])
        ident_f = sb("ident_fb", [128, 128])
        ident_b = sb("ident_bb", [128, 128], BF16)
        rowa = sb("rowa", [1, 128], BF16)
        rowb = sb("rowb", [1, 128], BF16)
        eps_t = sb("eps_tb", [128, 1])
        qTt2 = [sb("qTt%d" % i, [128, 8, 128], BF16) for i in range(2)]
        mix_b2 = [sb("mix_b%d" % i, [128, 1024], BF16) for i in range(2)]
        mixT = sb("mixT", [128, 8, 128], BF16)
        xt2 = [sb("xtb%d" % i, [128, 1024]) for i in range(2)]
        x1s2 = [sb("x1s%d" % i, [128, 1024]) for i in range(2)]
        P_sb = sb("P_sb", [128, 3, 4, 128], BF16)
        attn_tm = sb("attn_tm", [128, 512])
        junk = sb("junkb", [128, 512], BF16)
        st1 = sb("st1b", [128, 16])

        O.ld("c0", ident_f[:], c_ident, [], ["ident_f"])
        O.cp("dve", ident_b[:], ident_f[:], ["ident_f"], ["ident_b"])
        O.mset("dve", eps_t[:], EPS, ["eps_t"])
        O.mset("dve", rowa[:, :], 1.0, ["rowa"])
        O.mset("dve", rowa[:, 0:64], 0.0, ["rowa"])
        O.mset("dve", rowb[:, :], -30000.0, ["rowb"])
        O.mset("dve", rowb[:, 64:128], 0.0, ["rowb"])
        tmpT = sb("tmpTb", [8, 128])
        O.ld("ldT", tmpT[:4, :], mla_out_norm.rearrange("(k p) -> k p", p=128), [], ["tmpT"])
        O.tr(PS[:, 0:4], tmpT[:4, :], ident_f[:4, :4], ["tmpT", "ident_f"], ["bs0"])
        O.cp("dve", gM[:, :], PS[:, 0:4], ["bs0"], ["gM"])
        for kc in range(8):
            O.ld("stg", stg[:], w_out[kc * 128:(kc + 1) * 128, :], [], ["stg"])
            if kc < 4:
                O.tsm("dve", w_out_b[:, kc, :], stg[:], gM[:, kc:kc + 1], ["stg", "gM"], ["w_out_b"])
            else:
                O.cp("dve", w_out_b[:, kc, :], stg[:], ["stg"], ["w_out_b"])

        SBANK = (0, 1, 7)
        tcount = [0]

        def attn_seq(q, tiles):
            units = []
            for ti, (qi, n, x_src, keytiles, diag_j, x1_idx) in enumerate(tiles):
                groups = []
                cur = []
                for kt in keytiles:
                    if kt[1] != 128:
                        if cur:
                            groups.append(cur)
                            cur = []
                        groups.append([kt])
                    else:
                        cur.append(kt)
                        if len(cur) == 4:
                            groups.append(cur)
                            cur = []
                if cur:
                    groups.append(cur)
                for h in range(8):
                    for gi, g in enumerate(groups):
                        units.append((ti, h, g, gi == 0, gi == len(groups) - 1))
            tpar = {}

            def prologue(ti):
                qi, n, x_src, keytiles, diag_j, x1_idx = tiles[ti]
                p = tcount[0] % 2
                tcount[0] += 1
                tpar[ti] = p
                O.ld("qt%d" % p, qTt2[p][:, :, :n], qT_d[q, qi, :, :, 0:n], [], ["qTt%d" % p])
                O.ld("gm%d" % p, mix_b2[p][:n, 512:1024], gm_d[q, qi, 0:n, :], [], ["mix_g%d" % p])
                O.ld("xt%d" % p, xt2[p][:n, :], x_src, [], ["xt%d" % p])

            def qk(ui):
                ti, h, g, fg, lg = units[ui]
                qi, n, x_src, keytiles, diag_j, x1_idx = tiles[ti]
                if ti not in tpar:
                    prologue(ti)
                p = tpar[ti]
                par = ui % 3
                Sp = bank(SBANK[par])[:, :].rearrange("p (s t) -> p s t", s=4)
                for s_, (j, nk, col0) in enumerate(g):
                    dg = (j == diag_j)
                    O.mm(Sp[:nk, s_, :n], KTc[0:96, h, col0:col0 + nk], qTt2[p][0:96, h, :n], True, not dg,
                         ["KTc", "qTt%d" % p], ["bs%d" % par])
                    if dg:
                        O.mm(Sp[:nk, s_, :n], rowa[0:1, :nk], rowb[0:1, :n], False, True, ["rowa", "rowb"], ["bs%d" % par])

            def expv(ui):
                ti, h, g, fg, lg = units[ui]
                qi, n, x_src, keytiles, diag_j, x1_idx = tiles[ti]
                p = tpar[ti]
                par = ui % 3
                Sp = bank(SBANK[par])[:, :].rearrange("p (s t) -> p s t", s=4)
                Op = bank(2 + (h % 2))
                ok = "bo%d" % (h % 2)
                nk0 = g[0][1]
                O.act(P_sb[:nk0, par, 0:len(g), :n], Sp[:nk0, 0:len(g), :n], AF.Exp, ["bs%d" % par], ["P%d" % par], scale=ATT_SCALE)
                for s_, (j, nk, col0) in enumerate(g):
                    O.mm(Op[:n, 0:65], P_sb[:nk, par, s_, :n], Vc[:nk, j, h * 65:(h + 1) * 65], fg and s_ == 0,
                         lg and s_ == len(g) - 1, ["P%d" % par, "Vc"], [ok])
                if lg:
                    O.recip(st1[:n, h:h + 1], Op[:n, 64:65], [ok], ["st_r%d" % h])
                    O.tsm("dve", attn_tm[:n, h * 64:(h + 1) * 64], Op[:n, 0:64], st1[:n, h:h + 1], [ok, "st_r%d" % h], ["attn_tm"])
                    if h == 7:
                        epilogue(ti)

            def epilogue(ti):
                qi, n, x_src, keytiles, diag_j, x1_idx = tiles[ti]
                p = tpar[ti]
                mix_b = mix_b2[p]
                O.act(junk[:n, :], attn_tm[:n, :], AF.Square, ["attn_tm"], ["junk", "st_a"], accum_out=st1[:n, 8:9])
                O.act(st1[:n, 8:9], st1[:n, 8:9], AF.Ln, ["st_a", "eps_t"], ["st_a"], scale=1.0 / 512, bias=eps_t[:n, 0:1])
                O.act(st1[:n, 8:9], st1[:n, 8:9], AF.Exp, ["st_a"], ["st_a"], scale=-0.5)
                O.asc(mix_b[:n, 0:512], attn_tm[:n, :], st1[:n, 8:9], ["attn_tm", "st_a"], ["mix_a%d" % p])
                tv = PSB[:, 4 * 1024: 5 * 1024].rearrange("p (a b) -> p a b", a=8)
                for c in range(8):
                    O.tr(tv[:, c, :n], mix_b[:n, c * 128:(c + 1) * 128], ident_b[:n, :n],
                         ["mix_a%d" % p, "mix_g%d" % p, "ident_b"], ["b4"])
                O.cp("dve", mixT[:, :, :n], tv[:, :, :n], ["b4"], ["mixT"])
                x1p = PS[:, 5 * 512: 7 * 512]
                for half in range(2):
                    for c in range(8):
                        O.mm(x1p[:n, half * 512:(half + 1) * 512], mixT[:, c, :n], w_out_b[:, c, half * 512:(half + 1) * 512],
                             c == 0, c == 7, ["mixT", "w_out_b"], ["b56"])
                O.tt("dve", x1s2[p][:n, :], x1p[:n, :], xt2[p][:n, :], ALU.add, ["b56", "xt%d" % p], ["x1s%d" % p])
                O.S.dma("pool", "x1o%d" % p, lambda e: e.dma_start(out=x1_d[x1_idx, 0:n, :], in_=x1s2[p][:n, :]), ["x1s%d" % p], [])

            LOOK = 2
            for ui in range(min(LOOK, len(units))):
                qk(ui)
            for ui in range(len(units)):
                if ui + LOOK < len(units):
                    qk(ui + LOOK)
                expv(ui)

        def load_cache(q, ncols, kts):
            for h in range(8):
                O.ld("ktc", KTc[:, h, 0:ncols], KT_d[q, :, h, 0:ncols], [], ["KTc"])
            full = [j for (j, nk, c0) in kts if nk == 128]
            if full:
                for j0 in range(full[0], full[-1] + 1, 4):
                    j1 = min(j0 + 4, full[-1] + 1)
                    O.ld("vc", Vc[:, j0:j1, :], V_d[q, j0:j1, :, :].rearrange("t p c -> p t c"), [], ["Vc"])
            for (j, nk, c0) in kts:
                if nk != 128:
                    O.ld("vc", Vc[:nk, j, :], V_d[q, j, 0:nk, :], [], ["Vc"])

        for s in range(NP):
            load_cache(s, LP, [(0, 16, 0)] + [(jj + 1, 128, 16 + 128 * jj) for jj in range(NFT)])
            tl = []
            for i in range(NFT):
                kts = [(0, 16, 0)] + [(jj + 1, 128, 16 + 128 * jj) for jj in range(i + 1)]
                tl.append((i, 128, xp[s, 128 * i:128 * (i + 1), :], kts, i + 1, s * NFT + i))
            attn_seq(s, tl)
        kts = [(0, 16, 0)] + [(jj + 1, 128, 16 + 128 * jj) for jj in range(CFT)] + [(CFT + 1, 64, LC)]
        load_cache(NP, LC + 64, kts)
        attn_seq(NP, [(0, 64, xs, kts, -1, NP * NFT)])
        S.emit()
    if _STOP == '1b':
        return nc

    nc.all_engine_barrier()

    with ExitStack() as st:
        def sb(name, shape, dt=F32):
            return st.enter_context(nc.sbuf_tensor(name, shape, dt))

        PS = st.enter_context(nc.psum_tensor("psc", [128, 4096], F32))
        PSB = PS[:].bitcast(BF16)
        S = Sched(nc, "c")
        O = Ops(S)
        w_up_b = sb("w_up_b", [128, 8, 4096], BF16)
        w_dn_b = sb("w_dn_b", [128, 32, 1024], BF16)
        gP = sb("gP", [128, 8])
        ident_f = sb("ident_fc", [128, 128])
        ident_b = sb("ident_bc", [128, 128], BF16)
        eps_t = sb("eps_tc", [128, 1])
        x1t2 = [sb("x1t%d" % i, [128, 2, 1024]) for i in range(2)]
        hn2 = [sb("hn%d" % i, [128, 1024], BF16) for i in range(2)]
        hT2 = [sb("hT%d" % i, [128, 8, 256], BF16) for i in range(2)]
        rl = sb("rl", [128, 2, 512])
        upT = sb("upT", [128, 32, 256], BF16)
        ys2 = [sb("ys%d" % i, [128, 1024]) for i in range(2)]
        junk = sb("junkc", [128, 1024], BF16)
        st1 = sb("st1c", [128, 8])

        O.ld("c0", ident_f[:], c_ident, [], ["ident_f"])
        O.cp("dve", ident_b[:], ident_f[:], ["ident_f"], ["ident_b"])
        O.mset("dve", eps_t[:], EPS, ["eps_t"])
        tmpT = sb("tmpTc", [8, 128])
        O.ld("ldT", tmpT[:8, :], mlp_norm.rearrange("(k p) -> k p", p=128), [], ["tmpT"])
        O.tr(PS[:, 0:8], tmpT[:8, :], ident_f[:8, :8], ["tmpT", "ident_f"], ["b0"])
        O.cp("dve", gP[:, :], PS[:, 0:8], ["b0"], ["gP"])
        for kc in range(8):
            sp_ = kc % 2
            stg = x1t2[sp_][:, :, :].rearrange("p a d -> p (a d)")
            for hf in range(2):
                O.ld("stg%d" % sp_, stg[:, :], w_up[kc * 128:(kc + 1) * 128, hf * 2048:(hf + 1) * 2048], [], ["x1t%d" % sp_])
                O.tsm("dve" if hf == 0 else "pool", w_up_b[:, kc, hf * 2048:(hf + 1) * 2048], stg[:, :], gP[:, kc:kc + 1],
                      ["x1t%d" % sp_, "gP"], ["w_up_b"])
        for c2 in range(16):
            sp_ = c2 % 2
            O.ld("stg%d" % sp_, x1t2[sp_][:, :, :], w_down[c2 * 256:(c2 + 1) * 256, :].rearrange("(f p) d -> p f d", p=128),
                 [], ["x1t%d" % sp_])
            O.cp("dve" if sp_ == 0 else "pool", w_dn_b[:, 2 * c2:2 * c2 + 2, :], x1t2[sp_][:, :, :], ["x1t%d" % sp_], ["w_dn_b"])

        def mlp_front_a(bi, subs):
            p = bi % 2
            for si, (idx, n, y_out) in enumerate(subs):
                O.ld("x1%d" % p, x1t2[p][:n, si, :], x1_d[idx, 0:n, :], [], ["x1t%d" % p])
            for si, (idx, n, y_out) in enumerate(subs):
                O.act(junk[:n, :], x1t2[p][:n, si, :], AF.Square, ["x1t%d" % p], ["junk", "st"], accum_out=st1[:n, si:si + 1])
                O.act(st1[:n, si:si + 1], st1[:n, si:si + 1], AF.Ln, ["st", "eps_t"], ["st"], scale=1.0 / 1024, bias=eps_t[:n, 0:1])
                O.act(st1[:n, si:si + 1], st1[:n, si:si + 1], AF.Exp, ["st"], ["st"], scale=-0.5)
                O.asc(hn2[si][:n, :], x1t2[p][:n, si, :], st1[:n, si:si + 1], ["x1t%d" % p, "st"], ["hn%d" % si])

        def mlp_front_b(bi, subs):
            p = bi % 2
            for si, (idx, n, y_out) in enumerate(subs):
                tv = PSB[:, 0:1024].rearrange("p (a b) -> p a b", a=8)
                for kc in range(8):
                    O.tr(tv[:, kc, :n], hn2[si][:n, kc * 128:(kc + 1) * 128], ident_b[:n, :n], ["hn%d" % si, "ident_b"], ["b0"])
                O.cp("dve", hT2[p][:, :, si * 128:si * 128 + n], tv[:, :, :n], ["b0"], ["hT%d" % p])

        def mlp_body(bi, subs, mid=None):
            p = bi % 2
            nt = sum(n for (_, n, _) in subs) if len(subs) == 1 else 256
            for fc in range(32):
                par = fc % 2
                reg = PS[:, (1 + par) * 512:(1 + par) * 512 + nt]
                rk = "bu%d" % par
                for kc in range(8):
                    O.mm(reg, w_up_b[:, kc, fc * 128:(fc + 1) * 128], hT2[p][:, kc, 0:nt], kc == 0, kc == 7,
                         ["w_up_b", "hT%d" % p], [rk])
                O.act(rl[:, par, 0:nt], reg, AF.Relu, [rk], ["rl%d" % par])
                O.tt("dve", upT[:, fc, 0:nt], rl[:, par, 0:nt], rl[:, par, 0:nt], ALU.mult,
                     ["rl%d" % par], ["upT"])
            if mid is not None:
                mid()
            for si, (idx, n, y_out) in enumerate(subs):
                yb = 5 if si == 0 else 3
                yp = PS[:, yb * 512: (yb + 2) * 512]
                yk = "by%d" % si
                for half in range(2):
                    for fc in range(32):
                        O.mm(yp[:n, half * 512:(half + 1) * 512], upT[:, fc, si * 128:si * 128 + n],
                             w_dn_b[:, fc, half * 512:(half + 1) * 512], fc == 0, fc == 31, ["upT", "w_dn_b"], [yk])
                ys = ys2[si]
                O.tt("dve", ys[:n, :], yp[:n, :], x1t2[p][:n, si, :], ALU.add, [yk, "x1t%d" % p], ["ys%d" % si])
                O.S.dma("pool", "yo%d" % si, lambda e, ys=ys, n=n, y_out=y_out: e.dma_start(out=y_out, in_=ys[:n, :]),
                        ["ys%d" % si], [])

        blocks = []
        flat = []
        for s in range(NP):
            for i in range(NFT):
                flat.append((s * NFT + i, 128, y_p[s, 128 * i:128 * (i + 1), :]))
        for i in range(0, len(flat), 2):
            blocks.append(flat[i:i + 2])
        blocks.append([(NP * NFT, 64, y_s)])
        mlp_front_a(0, blocks[0])
        mlp_front_b(0, blocks[0])
        for bi in range(len(blocks)):
            if bi + 1 < len(blocks):
                mlp_front_a(bi + 1, blocks[bi + 1])
                mlp_body(bi, blocks[bi], mid=lambda bi=bi: mlp_front_b(bi + 1, blocks[bi + 1]))
            else:
                mlp_body(bi, blocks[bi])
        S.emit()
    return nc


_NC_CACHE = {}


def run_cores(per_core_inputs, NP, SEQ, PAST):
    key = (NP, SEQ, PAST)
    if key not in _NC_CACHE:
        _NC_CACHE[key] = build_nc(NP, SEQ, PAST)
    nc = _NC_CACHE[key]
    res = run_bass_kernel_spmd(nc, per_core_inputs, core_ids=list(range(len(per_core_inputs))))
    return res.results


def make_core_inputs(c, NP, inputs, consts):
    f = lambda a: np.ascontiguousarray(np.asarray(a), dtype=np.float32)
    d = {
        "xp": f(inputs["x_prompt"][NP * c:NP * (c + 1)]),
        "meta": f(inputs["meta_tokens"]),
        "xs": f(inputs["x_sample"][c]),
        "clat": f(inputs["cache_kv_latent"][0, c]),
        "crope": f(inputs["cache_k_rope"][0, c]),
        "sgdn": f(inputs["state_gdn"][0, c]),
        "sconv": f(inputs["state_conv"][0, c]),
    }
    for k in ("w_in", "w_uq", "w_uk", "w_uv", "w_out", "w_up", "w_down", "attn_norm", "q_a_norm", "kv_a_norm", "q_norm",
              "k_norm", "mla_out_norm", "conv_w", "a_log", "dt_bias", "gdn_out_norm", "mlp_norm"):
        d[k] = f(inputs[k][0])
    d.update(consts)
    return d


def kernel(**inputs):
    NCORES = 8
    B, SEQ = inputs["x_prompt"].shape[:2]
    NP = B // NCORES
    PAST = inputs["cache_kv_latent"].shape[2] - 16
    LP = 16 + SEQ
    consts = host_consts(max(LP, 16 + PAST + 64))
    ins = [make_core_inputs(c, NP, inputs, consts) for c in range(NCORES)]
    res = run_cores(ins, NP, SEQ, PAST)
    cat = lambda k: np.concatenate([r[k] for r in res], axis=0)
    stk = lambda k: np.stack([r[k] for r in res], axis=0)
    y_p = cat("y_p")
    y_s = stk("y_s")
    return (y_p, y_s, cat("p_lat")[None], cat("p_rope")[None], cat("p_S")[None], cat("p_conv")[None],
            stk("s_lat")[None], stk("s_rope")[None], stk("s_S")[None], stk("s_conv")[None])
```

```python
import numpy as np
import ml_dtypes
from contextlib import ExitStack
import concourse.bass as bass
import concourse.mybir as mybir
from concourse.bass_utils import run_bass_kernel_spmd

F32 = mybir.dt.float32
BF16 = mybir.dt.bfloat16
AF = mybir.ActivationFunctionType
ALU = mybir.AluOpType
AX = mybir.AxisListType
EPS = 1e-6
NEG = -30000.0
ENGS = ("sp", "act", "dve", "pool", "pe")


class Sched:
    def __init__(self, nc, tag):
        self.nc = nc
        self.tag = tag
        self.lists = {e: [] for e in ENGS}
        self.last_writer = {}
        self.readers = {}
        self.dma_count = {}
        self.waited = {e: {} for e in ENGS}

    def _deps(self, eng, reads, writes):
        deps = []
        for k in reads:
            w = self.last_writer.get(k)
            if w is not None:
                deps.append((w, True))
        for k in writes:
            w = self.last_writer.get(k)
            if w is not None:
                deps.append((w, False))
            for t in self.readers.get(k, {}).values():
                deps.append((t, False))
        out = []
        wd = self.waited[eng]
        for t, raw in deps:
            if t[0] == "E" and t[1] == eng and eng == "pe":
                continue
            sid = (t[0], t[1])
            if wd.get(sid, -1) >= t[2]:
                continue
            wd[sid] = t[2]
            out.append(t)
        return out

    def _commit(self, token, stream, reads, writes):
        for k in reads:
            self.readers.setdefault(k, {})[stream] = token
        for k in writes:
            self.last_writer[k] = token
            self.readers[k] = {}

    def op(self, eng, fn, reads=(), writes=()):
        deps = self._deps(eng, reads, writes)
        idx = len(self.lists[eng])
        self.lists[eng].append({"fn": fn, "deps": deps, "flag": False, "dma": None})
        self._commit(("E", eng, idx), eng, reads, writes)

    def dma(self, eng, key, fn, reads=(), writes=(), n=1):
        deps = self._deps(eng, reads, writes)
        c = self.dma_count.get(key, 0) + 16 * n
        self.dma_count[key] = c
        self.lists[eng].append({"fn": fn, "deps": deps, "flag": False, "dma": key})
        self._commit(("D", key, c), "D" + key, reads, writes)

    def emit(self):
        nc = self.nc
        for e in ENGS:
            for rec in self.lists[e]:
                for t in rec["deps"]:
                    if t[0] == "E":
                        self.lists[t[1]][t[2]]["flag"] = True
        val = {}
        for e in ENGS:
            c = 0
            v = []
            for rec in self.lists[e]:
                if rec["flag"] and rec["dma"] is None:
                    c += 1
                v.append(c)
            val[e] = v
        with ExitStack() as st:
            esem = {e: st.enter_context(nc.semaphore(self.tag + "s_" + e)) for e in ENGS}
            dsem = {k: st.enter_context(nc.semaphore(self.tag + "d_" + k)) for k in self.dma_count}
            block = st.enter_context(nc.Block())
            final = dict(self.dma_count)

            def run(e, engine):
                for rec in self.lists[e]:
                    for t in rec["deps"]:
                        if t[0] == "E":
                            engine.wait_ge(esem[t[1]], val[t[1]][t[2]])
                        else:
                            engine.wait_ge(dsem[t[1]], t[2])
                    r = rec["fn"](engine)
                    if rec["dma"] is not None:
                        if not isinstance(r, (list, tuple)):
                            r = [r]
                        for ins in r:
                            ins.then_inc(dsem[rec["dma"]], 16)
                    elif rec["flag"]:
                        r.then_inc(esem[e], 1)
                if e == "sp":
                    for k, c in final.items():
                        engine.wait_ge(dsem[k], c)

            @block.sync
            def _(eng):
                run("sp", eng)

            @block.scalar
            def _(eng):
                run("act", eng)

            @block.vector
            def _(eng):
                run("dve", eng)

            @block.gpsimd
            def _(eng):
                run("pool", eng)

            @block.tensor
            def _(eng):
                run("pe", eng)


class Ops:
    def __init__(self, S):
        self.S = S

    def act(self, out, in_, func, r, w, **kw):
        self.S.op("act", lambda e: e.activation(out=out, in_=in_, func=func, **kw), r, w)

    def tt(self, eng, out, in0, in1, op, r, w):
        self.S.op(eng, lambda e: e.tensor_tensor(out=out, in0=in0, in1=in1, op=op), r, w)

    def stt(self, eng, out, in0, scalar, in1, op0, op1, r, w):
        self.S.op(eng, lambda e: e.scalar_tensor_tensor(out=out, in0=in0, scalar=scalar, in1=in1, op0=op0, op1=op1), r, w)

    def tsm(self, eng, out, in0, s1, r, w):
        self.S.op(eng, lambda e: e.tensor_scalar_mul(out=out, in0=in0, scalar1=s1), r, w)

    def asc(self, out, in_, sc, r, w):
        self.S.op("act", lambda e: e.activation(out=out, in_=in_, func=AF.Copy, scale=sc), r, w)

    def tsa(self, eng, out, in0, s1, r, w):
        self.S.op(eng, lambda e: e.tensor_scalar_add(out=out, in0=in0, scalar1=s1), r, w)

    def cp(self, eng, out, in_, r, w):
        if eng == "act":
            self.S.op("act", lambda e: e.copy(out=out, in_=in_), r, w)
        else:
            self.S.op(eng, lambda e: e.tensor_copy(out=out, in_=in_), r, w)

    def recip(self, out, in_, r, w):
        self.S.op("dve", lambda e: e.reciprocal(out=out, in_=in_), r, w)

    def red(self, out, in_, r, w):
        self.S.op("dve", lambda e: e.tensor_reduce(out=out, in_=in_, axis=AX.X, op=ALU.add), r, w)

    def mset(self, eng, ap, v, w):
        self.S.op(eng, lambda e: e.memset(ap, v), (), w)

    def mm(self, out, lhsT, rhs, start, stop, r, w):
        self.S.op("pe", lambda e: e.matmul(out, lhsT=lhsT, rhs=rhs, start=start, stop=stop), r, w)

    def tr(self, out, in_, ident, r, w):
        self.S.op("pe", lambda e: e.transpose(out=out, in_=in_, identity=ident), r, w)

    def ld(self, key, out, in_, r, w, slow=False):
        if slow:
            self.S.dma("sp", key, lambda e: e.dma_start(out=out, in_=in_, allow_slow_non_contiguous=True), r, w)
        else:
            self.S.dma("sp", key, lambda e: e.dma_start(out=out, in_=in_), r, w)


def host_consts(npos):
    c = {}
    c["c_ident"] = np.eye(128, dtype=np.float32)
    p = np.arange(128)[:, None]
    q = np.arange(128)[None, :]
    c["c_U"] = (p <= q).astype(np.float32)
    c["c_mLs"] = np.where(p > q, 0.0, NEG).astype(np.float32)
    c["c_mU"] = np.where(q >= p, 0.0, NEG).astype(np.float32)
    lv = np.zeros((128, 7, 128), np.float32)
    for l in range(7):
        b = 1 << l
        lv[:, l, :] = ((p // (2 * b) == q // (2 * b)) & (p % (2 * b) >= b) & (q % (2 * b) < b)).astype(np.float32)
    c["c_lvl"] = lv
    half = 16
    inv_freq = (np.float32(10000.0) ** (-np.arange(half, dtype=np.float32) / np.float32(half))).astype(np.float32)
    ang = np.arange(npos, dtype=np.float32)[:, None] * inv_freq[None, :]
    c["c_cs"] = np.concatenate([np.cos(ang), np.sin(ang)], axis=1).astype(np.float32)
    return c


import os
_STOP = os.environ.get('KSTOP', '')
_CUT = float(os.environ.get('KCUT', '99'))


def build_nc(NP, SEQ, PAST):
    NFT = SEQ // 128
    LP = 16 + SEQ
    CFT = PAST // 128
    LC = 16 + PAST
    NSEQ = NP + 1
    NCMAX = max(LP, LC + 64)
    NKT = max(NFT + 1, CFT + 2)
    NQT = max(NFT, 1)
    NX1 = NP * NFT + 1
    NPOS = max(LP, LC + 64)

    nc = bass.Bass("TRN2", target_bir_lowering=False)

    def din(name, shape):
        return nc.dram_tensor(name, shape, F32, kind="ExternalInput").ap()

    def dout(name, shape):
        return nc.dram_tensor(name, shape, F32, kind="ExternalOutput").ap()

    xp = din("xp", [NP, SEQ, 1024])
    meta = din("meta", [16, 1024])
    xs = din("xs", [64, 1024])
    clat = din("clat", [LC, 256])
    crope = din("crope", [LC, 32])
    sgdn = din("sgdn", [4, 128, 128])
    sconv = din("sconv", [3, 1536])
    w_in = din("w_in", [1024, 2728])
    w_uq = din("w_uq", [384, 768])
    w_uk = din("w_uk", [256, 512])
    w_uv = din("w_uv", [256, 512])
    w_out = din("w_out", [1024, 1024])
    w_up = din("w_up", [1024, 4096])
    w_down = din("w_down", [4096, 1024])
    attn_norm = din("attn_norm", [1024])
    q_a_norm = din("q_a_norm", [384])
    kv_a_norm = din("kv_a_norm", [256])
    q_norm = din("q_norm", [96])
    k_norm = din("k_norm", [96])
    mla_out_norm = din("mla_out_norm", [512])
    conv_w = din("conv_w", [4, 1536])
    a_log = din("a_log", [4])
    dt_bias = din("dt_bias", [4])
    gdn_out_norm = din("gdn_out_norm", [128])
    mlp_norm = din("mlp_norm", [1024])
    c_ident = din("c_ident", [128, 128])
    c_U = din("c_U", [128, 128])
    c_mLs = din("c_mLs", [128, 128])
    c_mU = din("c_mU", [128, 128])
    c_lvl = din("c_lvl", [128, 7, 128])
    c_cs = din("c_cs", [NPOS, 32])

    y_p = dout("y_p", [NP, SEQ, 1024])
    y_s = dout("y_s", [64, 1024])
    p_lat = dout("p_lat", [NP, LP, 256])
    p_rope = dout("p_rope", [NP, LP, 32])
    p_S = dout("p_S", [NP, 4, 128, 128])
    p_conv = dout("p_conv", [NP, 3, 1536])
    s_lat = dout("s_lat", [64, 256])
    s_rope = dout("s_rope", [64, 32])
    s_S = dout("s_S", [4, 128, 128])
    s_conv = dout("s_conv", [3, 1536])

    KT_d = nc.dram_tensor("KT_d", [NSEQ, 128, 8, NCMAX], BF16, kind="Internal").ap()
    V_d = nc.dram_tensor("V_d", [NSEQ, NKT, 128, 520], BF16, kind="Internal").ap()
    qT_d = nc.dram_tensor("qT_d", [NSEQ, NQT, 128, 8, 128], BF16, kind="Internal").ap()
    gm_d = nc.dram_tensor("gm_d", [NSEQ, NQT, 128, 512], BF16, kind="Internal").ap()
    x1_d = nc.dram_tensor("x1_d", [NX1, 128, 1024], F32, kind="Internal").ap()

    ATT_SCALE = 96.0 ** -0.5

    with ExitStack() as st:
        def sb(name, shape, dt=F32):
            return st.enter_context(nc.sbuf_tensor(name, shape, dt))

        PS = st.enter_context(nc.psum_tensor("ps", [128, 4096], F32))
        PSB = PS[:].bitcast(BF16)

        def bank(b, w=512, off=0):
            return PS[:, b * 512 + off: b * 512 + off + w]

        S = Sched(nc, "a")
        O = Ops(S)

        w_in_b = sb("w_in_b", [128, 8, 2728], BF16)
        w_uq_b = sb("w_uq_b", [128, 3, 768], BF16)
        w_uk_b = sb("w_uk_b", [128, 2, 512], BF16)
        w_uv_b = sb("w_uv_b", [128, 2, 512], BF16)
        convd = sb("convd", [128, 48, 128], BF16)
        stg = sb("stg", [128, 2728], F32)
        gA = sb("gA", [128, 8])
        gQ = sb("gQ", [128, 3])
        cw = sb("cw", [128, 4, 12])
        ident_f = sb("ident_f", [128, 128])
        ident_b = sb("ident_b", [128, 128], BF16)
        U_f = sb("U_f", [128, 128])
        ones_f = sb("ones_f", [128, 128])
        mLs = sb("mLs", [128, 128])
        mU = sb("mU", [128, 128])
        lvl = sb("lvl", [128, 7, 128])
        gkv = sb("gkv", [128, 256])
        gq = sb("gq", [128, 96])
        gk = sb("gk", [128, 96])
        go = sb("go", [128, 128])
        negA = sb("negA", [128, 4])
        dtb = sb("dtb", [128, 4])
        eps_t = sb("eps_t", [128, 1])
        one_t = sb("one_t", [128, 1])

        O.ld("c0", ident_f[:], c_ident, [], ["ident_f"])
        O.ld("c1", U_f[:], c_U, [], ["U_f"])
        O.ld("c2", mLs[:], c_mLs, [], ["mLs"])
        O.ld("c3", mU[:], c_mU, [], ["mU"])
        O.ld("c4", lvl[:], c_lvl, [], ["lvl"])
        O.ld("c5", gkv[:], kv_a_norm.partition_broadcast(128), [], ["gkv"])
        O.ld("c6", gq[:], q_norm.partition_broadcast(128), [], ["gq"])
        O.ld("c7", gk[:], k_norm.partition_broadcast(128), [], ["gk"])
        O.ld("c8", go[:], gdn_out_norm.partition_broadcast(128), [], ["go"])
        O.ld("c9", negA[:], a_log.partition_broadcast(128), [], ["negA"])
        O.ld("c10", dtb[:], dt_bias.partition_broadcast(128), [], ["dtb"])
        tmpT = sb("tmpT", [48, 128])

        def ld_T(dst, src2d, k, key):
            O.ld("ldT", tmpT[:k, :], src2d, [], ["tmpT"])
            O.tr(PS[:, 0:k], tmpT[:k, :], ident_f[:k, :k], ["tmpT", "ident_f"], ["b0"])
            O.cp("dve", dst, PS[:, 0:k], ["b0"], [key])

        ld_T(gA[:, :], attn_norm.rearrange("(k p) -> k p", p=128), 8, "gA")
        ld_T(gQ[:, :], q_a_norm.rearrange("(k p) -> k p", p=128), 3, "gQ")
        ld_T(cw[:, :, :].rearrange("p w c -> p (w c)"), conv_w.rearrange("w (c p) -> (w c) p", p=128), 48, "cw")
        O.cp("dve", ident_b[:], ident_f[:], ["ident_f"], ["ident_b"])
        O.mset("dve", ones_f[:], 1.0, ["ones_f"])
        O.mset("dve", eps_t[:], EPS, ["eps_t"])
        O.mset("dve", one_t[:], 1.0, ["one_t"])
        O.act(negA[:], negA[:], AF.Exp, ["negA"], ["negA"])
        O.tsm("dve", negA[:], negA[:], -1.0, ["negA"], ["negA"])
        for kc in range(8):
            O.ld("stg", stg[:], w_in[kc * 128:(kc + 1) * 128, :], [], ["stg"])
            O.tsm("dve" if kc % 2 == 0 else "pool", w_in_b[:, kc, :], stg[:], gA[:, kc:kc + 1], ["stg", "gA"], ["w_in_b"])
        for kc in range(3):
            O.ld("stg", stg[:, 0:768], w_uq[kc * 128:(kc + 1) * 128, :], [], ["stg"])
            O.tsm("dve", w_uq_b[:, kc, :], stg[:, 0:768], gQ[:, kc:kc + 1], ["stg", "gQ"], ["w_uq_b"])
        for kc in range(2):
            O.ld("stg", stg[:, 0:512], w_uk[kc * 128:(kc + 1) * 128, :], [], ["stg"])
            O.cp("dve", w_uk_b[:, kc, :], stg[:, 0:512], ["stg"], ["w_uk_b"])
            O.ld("stg", stg[:, 0:512], w_uv[kc * 128:(kc + 1) * 128, :], [], ["stg"])
            O.cp("pool", w_uv_b[:, kc, :], stg[:, 0:512], ["stg"], ["w_uv_b"])
        for w4 in range(4):
            for c in range(12):
                O.tsm("dve" if c % 2 == 0 else "pool", convd[:, w4 * 12 + c, :], ident_f[:], cw[:, w4, c:c + 1],
                      ["ident_f", "cw"], ["convd"])

        xt2 = [sb("xt%d" % i, [128, 1024]) for i in range(2)]
        junk = sb("junk", [128, 1024], BF16)
        scF = sb("scF", [128, 768])
        scB = sb("scB", [128, 512])
        sc2 = sb("sc2", [128, 1024])
        st1 = sb("st1", [128, 64])
        xn = sb("xn", [128, 1024], BF16)
        xT = sb("xT", [128, 8, 128], BF16)
        zTh = sb("zTh", [128, 12, 131], BF16)
        sg2 = [sb("sg%d" % i, [128, 512]) for i in range(2)]
        hs2 = [sb("hs%d" % i, [128, 32]) for i in range(2)]
        k_tm2 = [sb("k_tm%d" % i, [128, 4, 128]) for i in range(2)]
        kbg2 = [sb("kbg%d" % i, [128, 4, 128], BF16) for i in range(2)]
        vb2 = [sb("vb%d" % i, [128, 4, 128], BF16) for i in range(2)]
        gkT2 = [sb("gkT%d" % i, [128, 4, 128], BF16) for i in range(2)]
        gqT2 = [sb("gqT%d" % i, [128, 4, 128], BF16) for i in range(2)]
        cst = sb("cst", [128, 1536])
        qln = sb("qln", [128, 384], BF16)
        qlT = sb("qlT", [128, 3, 128], BF16)
        qtmp = sb("qtmp", [128, 8, 96])
        rtmp = sb("rtmp", [128, 8, 64])
        q_pad = sb("q_pad", [128, 8, 128], BF16)
        qT = sb("qT", [128, 8, 128], BF16)
        lat_n = sb("lat_n", [128, 256])
        lat_b = sb("lat_b", [128, 256], BF16)
        latT = sb("latT", [128, 2, 128], BF16)
        rope_raw = sb("rope_raw", [128, 32])
        cs_t = sb("cs_t", [128, 32])
        rg = sb("rg", [128, 32])
        rr = sb("rr", [128, 32])
        t4 = sb("t4", [128, 64])
        k_pad = sb("k_pad", [128, 8, 128], BF16)
        kTt = sb("kTt", [128, 8, 128], BF16)
        v_b = sb("v_b", [128, 8, 65], BF16)
        csT = sb("csT", [128, 12, 128], BF16)
        qkv_tm = sb("qkv_tm", [128, 12, 128])
        e_dec = sb("e_dec", [128, 4])
        e_last = sb("e_last", [128, 4])
        rs8 = sb("rs8", [128, 8])
        UG = sb("UG", [128, 4, 128])
        dtmp = sb("dtmp", [128, 4, 128])
        MQs = sb("MQs", [128, 4, 128])
        MQT = sb("MQT", [128, 4, 128])
        A_b = sb("A_b", [128, 4, 128], BF16)
        At_b = sb("At_b", [128, 4, 128], BF16)
        TTb = sb("TTb", [128, 4, 2, 128], BF16)
        TTf = sb("TTf", [128, 4, 2, 128])
        X_b = sb("X_b", [128, 4, 128], BF16)
        k_tmb = sb("k_tmb", [128, 4, 128], BF16)
        q_tm = sb("q_tm", [128, 4, 128], BF16)
        kdec = sb("kdec", [128, 4, 128], BF16)
        u_sb = sb("u_sb", [128, 4, 128])
        wT_b = sb("wT_b", [128, 4, 128], BF16)
        vnb = sb("vnb", [128, 4, 128], BF16)
        qkT_b = sb("qkT_b", [128, 4, 128], BF16)
        oi = sb("oi", [128, 4, 128])
        o_sb = sb("o_sb", [128, 4, 128])
        Sg = sb("Sg", [128, 4, 128])
        Sgb = sb("Sgb", [128, 4, 128], BF16)
        gm_b = sb("gm_b", [128, 512], BF16)

        O.mset("dve", q_pad[:], 0.0, ["q_pad"])
        O.mset("dve", k_pad[:], 0.0, ["k_pad"])
        O.mset("dve", v_b[:], 1.0, ["v_b"])

        def rsq(out, in_, scale, n, r, w):
            O.act(out, in_, AF.Ln, r + ["eps_t"], w, scale=scale, bias=eps_t[:n, 0:1])
            O.act(out, out, AF.Exp, w, w, scale=-0.5)

        def rope_ops(o1, o2, x1, x2, cos, sin, ta, tb, tc, td, r, w, tk):
            O.tt("pool", ta, x1, cos, ALU.mult, r, [tk + "a"])
            O.tt("pool", tb, x2, sin, ALU.mult, r, [tk + "b"])
            O.tt("pool", tc, x2, cos, ALU.mult, r, [tk + "c"])
            O.tt("pool", td, x1, sin, ALU.mult, r, [tk + "d"])
            O.tt("pool", o1, ta, tb, ALU.subtract, [tk + "a", tk + "b"], w)
            O.tt("pool", o2, tc, td, ALU.add, [tk + "c", tk + "d"], w)

        tvF = PSB[:, 0:1024].rearrange("p (a b) -> p a b", a=8)

        def kv_build(q, j, n, col0, pos0):
            O.ld("cs", cs_t[:n, :], c_cs[pos0:pos0 + n, :], [], ["cs_t"])
            O.cp("act", lat_b[:n, :], lat_n[:n, :], ["lat_n"], ["lat_b"])
            for kc in range(2):
                O.tr(tvF[:, kc, :n], lat_b[:n, kc * 128:(kc + 1) * 128], ident_b[:n, :n], ["lat_b", "ident_b"], ["b0"])
            O.cp("dve", latT[:, :, :n], tvF[:, 0:2, :n], ["b0"], ["latT"])
            for kc in range(2):
                O.mm(bank(3)[:n, :], latT[:, kc, :n], w_uk_b[:, kc, :], kc == 0, kc == 1, ["latT", "w_uk_b"], ["b3"])
            for kc in range(2):
                O.mm(bank(4)[:n, :], latT[:, kc, :n], w_uv_b[:, kc, :], kc == 0, kc == 1, ["latT", "w_uv_b"], ["b4"])
            yield
            O.cp("act", v_b[:n, :, 0:64], bank(4)[:n, :].rearrange("p (h d) -> p h d", h=8), ["b4"], ["v_b"])
            O.S.dma("sp", "vd", lambda e: e.dma_start(out=V_d[q, j, 0:n, :], in_=v_b[:n, :, :].rearrange("p h d -> p (h d)")),
                    ["v_b"], [])
            O.act(scF[:n, 0:512], bank(3)[:n, :], AF.Square, ["b3"], ["scF"])
            O.red(st1[:n, 0:8], scF[:n, 0:512].rearrange("p (h d) -> p h d", h=8), ["scF"], ["st_k"])
            O.act(junk[:n, 0:32], rope_raw[:n, :], AF.Square, ["rope_raw"], ["junk", "st_kr"], accum_out=st1[:n, 8:9])
            O.tsa("dve", st1[:n, 0:8], st1[:n, 0:8], st1[:n, 8:9], ["st_k", "st_kr"], ["st_k"])
            rsq(st1[:n, 0:8], st1[:n, 0:8], 1.0 / 96, n, ["st_k"], ["st_k"])
            yield
            kn3 = bank(3)[:n, :].rearrange("p (h d) -> p h d", h=8)
            O.tt("dve", rtmp[:n, :, :], kn3, st1[:n, 0:8].unsqueeze(2).to_broadcast([n, 8, 64]), ALU.mult,
                 ["b3", "st_k"], ["rtmp"])
            O.tt("pool", k_pad[:n, :, 0:64], rtmp[:n, :, :], gk[:n, 0:64].unsqueeze(1).to_broadcast([n, 8, 64]), ALU.mult,
                 ["rtmp", "gk"], ["k_pad"])
            O.tt("pool", rg[:n, :], rope_raw[:n, :], gk[:n, 64:96], ALU.mult, ["rope_raw", "gk"], ["rg"])
            rope_ops(rr[:n, 0:16], rr[:n, 16:32], rg[:n, 0:16], rg[:n, 16:32], cs_t[:n, 0:16], cs_t[:n, 16:32],
                     t4[:n, 0:16], t4[:n, 16:32], t4[:n, 32:48], t4[:n, 48:64], ["rg", "cs_t"], ["rr"], "t4")
            O.tt("dve", k_pad[:n, :, 64:96], rr[:n, :].unsqueeze(1).to_broadcast([n, 8, 32]),
                 st1[:n, 0:8].unsqueeze(2).to_broadcast([n, 8, 32]), ALU.mult, ["rr", "st_k"], ["k_pad"])
            yield
            for h in range(8):
                O.tr(tvF[:, h, :n], k_pad[:n, h, :], ident_b[:n, :n], ["k_pad", "ident_b"], ["b0"])
            O.cp("act", kTt[:, :, :n], tvF[:, :, :n], ["b0"], ["kTt"])
            O.S.dma("sp", "ktd", lambda e: e.dma_start(out=KT_d[q, :, :, col0:col0 + n], in_=kTt[:, :, :n]), ["kTt"], [])
            yield

        def cache_kv(q, j, n, row0):
            O.ld("lat", lat_n[:n, :], clat[row0:row0 + n, :], [], ["lat_n"])
            O.ld("rop", rope_raw[:n, :], crope[row0:row0 + n, :], [], ["rope_raw"])
            yield from kv_build(q, j, n, row0, row0)

        def load_x(c):
            O.ld("xt%d" % c["par"], xt2[c["par"]][:c["n"], :], c["x_src"], [], ["xt%d" % c["par"]])

        def front(c, nxt):
            q, j, qi, n, pos0 = c["q"], c["j"], c["qi"], c["n"], c["pos0"]
            need_out, first, last, is_sample, par = c["need_out"], c["first"], c["last"], c["is_sample"], c["par"]
            xt = xt2[par]
            xk, zk, sgk, hk = "xt%d" % par, "zTh", "sg%d" % par, "hs%d" % par
            hs, k_tm, kbg, vb, gkT, gqT = hs2[par], k_tm2[par], kbg2[par], vb2[par], gkT2[par], gqT2[par]
            beta, g_t, gc, ngc, e_gc, bge = (hs[:, 0:4], hs[:, 4:8], hs[:, 8:12], hs[:, 12:16], hs[:, 16:20], hs[:, 20:24])
            if c["load_self"]:
                load_x(c)
            if nxt is not None:
                load_x(nxt)
            O.act(junk[:n, 0:1024], xt[:n, :], AF.Square, [xk], ["junk", "st_x"], accum_out=st1[:n, 16:17])
            rsq(st1[:n, 16:17], st1[:n, 16:17], 1.0 / 1024, n, ["st_x"], ["st_x"])
            O.asc(xn[:n, :], xt[:n, :], st1[:n, 16:17], [xk, "st_x"], ["xn"])
            yield
            for kc in range(8):
                O.tr(tvF[:, kc, :n], xn[:n, kc * 128:(kc + 1) * 128], ident_b[:n, :n], ["xn", "ident_b"], ["b0"])
            O.cp("dve", xT[:, :, :n], tvF[:, :, :n], ["b0"], ["xT"])
            yield
            Z1a, Z1b, Z3, Z4 = bank(1), bank(2, 160), bank(3), bank(2, 8, 256)
            for (dst, c0, c1, key) in ((Z1a, 0, 512, "b1"), (Z1b, 512, 672, "b2"), (Z3, 2208, 2720, "b3"), (Z4, 2720, 2728, "b2x")):
                for kc in range(8):
                    O.mm(dst[:n, :], xT[:, kc, :n], w_in_b[:, kc, c0:c1], kc == 0, kc == 7, ["xT", "w_in_b"], [key])
            yield
            if need_out:
                sg = sg2[par]
                O.act(sg[:n, :], Z3[:n, :], AF.Exp, ["b3"], [sgk], scale=-1.0)
                O.tsa("dve", sg[:n, :], sg[:n, :], 1.0, [sgk], [sgk])
                O.recip(sg[:n, :], sg[:n, :], [sgk], [sgk])
                O.tt("dve", sg[:n, :], sg[:n, :], Z3[:n, :], ALU.mult, [sgk, "b3"], [sgk])
            if first:
                if is_sample:
                    ld_T(cst[:, 0:36], sconv.rearrange("w (c p) -> (w c) p", p=128), 36, "cst")
                    for w3 in range(3):
                        O.cp("dve", zTh[:, :, w3], cst[:, w3 * 12:(w3 + 1) * 12], ["cst"], [zk])
                else:
                    O.mset("dve", zTh[:, :, 0:3], 0.0, [zk])
            else:
                pn = c["prev_n"]
                O.cp("pool", zTh[:, :, 0:3], zTh[:, :, pn:pn + 3], [zk], [zk])
            yield
            for g in range(3):
                bk = 4 if g % 2 == 0 else 3
                zc = bank(bk)[:, 0:4 * n].rearrange("p (c t) -> p c t", c=4)
                for c4 in range(4):
                    cc = 4 * g + c4
                    for kc in range(8):
                        O.mm(zc[:, c4, :], w_in_b[:, kc, 672 + cc * 128: 672 + (cc + 1) * 128], xT[:, kc, :n], kc == 0, kc == 7,
                             ["xT", "w_in_b"], ["b%d" % bk])
                O.cp("act", zTh[:, 4 * g:4 * g + 4, 3:3 + n], zc[:, :, :], ["b%d" % bk], [zk])
                yield
            if last:
                for c3 in range(3):
                    bk = 3 if c3 % 2 == 0 else 4
                    for kc in range(8):
                        O.mm(bank(bk)[:3, :], xT[:, kc, n - 3:n], w_in_b[:, kc, 672 + c3 * 512: 672 + (c3 + 1) * 512],
                             kc == 0, kc == 7, ["xT", "w_in_b"], ["b%d" % bk])
                    O.cp("act", cst[:3, c3 * 512:(c3 + 1) * 512], bank(bk)[:3, :], ["b%d" % bk], ["cst"])
                conv_out = c["conv_out"]
                O.S.dma("sp", "cso", lambda e: e.dma_start(out=conv_out, in_=cst[:3, :]), ["cst"], [])
                yield
            if need_out:
                O.act(junk[:n, 0:384], Z1a[:n, 0:384], AF.Square, ["b1"], ["junk", "st_q"], accum_out=st1[:n, 17:18])
                rsq(st1[:n, 17:18], st1[:n, 17:18], 1.0 / 384, n, ["st_q"], ["st_q"])
                O.act(qln[:n, :], Z1a[:n, 0:384], AF.Copy, ["b1", "st_q"], ["qln"], scale=st1[:n, 17:18])
                for kc in range(3):
                    O.tr(tvF[:, kc, :n], qln[:n, kc * 128:(kc + 1) * 128], ident_b[:n, :n], ["qln", "ident_b"], ["b0"])
                O.cp("dve", qlT[:, :, :n], tvF[:, 0:3, :n], ["b0"], ["qlT"])
                qr = PS[:, 3 * 512: 3 * 512 + 768]
                for (c0, c1) in ((0, 512), (512, 768)):
                    for kc in range(3):
                        O.mm(qr[:n, c0:c1], qlT[:, kc, :n], w_uq_b[:, kc, c0:c1], kc == 0, kc == 2, ["qlT", "w_uq_b"], ["b3", "b4"])
                yield
                O.act(scF[:n, 0:768], qr[:n, :], AF.Square, ["b3", "b4"], ["scF"])
                O.red(st1[:n, 24:32], scF[:n, 0:768].rearrange("p (h d) -> p h d", h=8), ["scF"], ["st_qh"])
                rsq(st1[:n, 24:32], st1[:n, 24:32], 1.0 / 96, n, ["st_qh"], ["st_qh"])
                qr3 = qr[:n, :].rearrange("p (h d) -> p h d", h=8)
                O.tt("dve", qtmp[:n, :, :], qr3, st1[:n, 24:32].unsqueeze(2).to_broadcast([n, 8, 96]), ALU.mult,
                     ["b3", "b4", "st_qh"], ["qtmp"])
                O.tt("pool", qtmp[:n, :, :], qtmp[:n, :, :], gq[:n, :].unsqueeze(1).to_broadcast([n, 8, 96]), ALU.mult,
                     ["qtmp", "gq"], ["qtmp"])
                O.ld("cs", cs_t[:n, :], c_cs[pos0:pos0 + n, :], [], ["cs_t"])
                cosb = cs_t[:n, 0:16].unsqueeze(1).to_broadcast([n, 8, 16])
                sinb = cs_t[:n, 16:32].unsqueeze(1).to_broadcast([n, 8, 16])
                rope_ops(q_pad[:n, :, 64:80], q_pad[:n, :, 80:96], qtmp[:n, :, 64:80], qtmp[:n, :, 80:96], cosb, sinb,
                         rtmp[:n, :, 0:16], rtmp[:n, :, 16:32], rtmp[:n, :, 32:48], rtmp[:n, :, 48:64],
                         ["qtmp", "cs_t"], ["q_pad"], "rtmp")
                O.cp("dve", q_pad[:n, :, 0:64], qtmp[:n, :, 0:64], ["qtmp"], ["q_pad"])
                yield
                for h in range(8):
                    O.tr(tvF[:, h, :n], q_pad[:n, h, :], ident_b[:n, :n], ["q_pad", "ident_b"], ["b0"])
                O.cp("act", qT[:, :, :n], tvF[:, :, :n], ["b0"], ["qT"])
                O.S.dma("sp", "qtd", lambda e: e.dma_start(out=qT_d[q, qi, :, :, 0:n], in_=qT[:, :, :n]), ["qT"], [])
                yield
            kvl = PS[:, 512 + 384: 512 + 640]
            O.act(junk[:n, 0:256], kvl[:n, :], AF.Square, ["b1", "b2"], ["junk", "st_kv"], accum_out=st1[:n, 18:19])
            rsq(st1[:n, 18:19], st1[:n, 18:19], 1.0 / 256, n, ["st_kv"], ["st_kv"])
            O.stt("dve", lat_n[:n, :], kvl[:n, :], st1[:n, 18:19], gkv[:n, :], ALU.mult, ALU.mult,
                  ["b1", "b2", "st_kv", "gkv"], ["lat_n"])
            O.cp("act", rope_raw[:n, :], Z1b[:n, 128:160], ["b2"], ["rope_raw"])
            lat_out, rope_out = c["lat_out"], c["rope_out"]
            O.S.dma("sp", "lato", lambda e: e.dma_start(out=lat_out, in_=lat_n[:n, :]), ["lat_n"], [])
            O.S.dma("sp", "ropeo", lambda e: e.dma_start(out=rope_out, in_=rope_raw[:n, :]), ["rope_raw"], [])
            yield
            yield from kv_build(q, j, n, pos0, pos0)
            O.act(beta[:n, :], Z4[:n, 0:4], AF.Exp, ["b2x"], [hk], scale=-1.0)
            O.tsa("dve", beta[:n, :], beta[:n, :], 1.0, [hk], [hk])
            O.recip(beta[:n, :], beta[:n, :], [hk], [hk])
            O.tt("dve", g_t[:n, :], Z4[:n, 4:8], dtb[:n, :], ALU.add, ["b2x", "dtb"], [hk])
            O.act(g_t[:n, :], g_t[:n, :], AF.Exp, [hk], [hk])
            O.act(g_t[:n, :], g_t[:n, :], AF.Ln, [hk, "one_t"], [hk], bias=one_t[:n, 0:1])
            O.tt("dve", g_t[:n, :], g_t[:n, :], negA[:n, :], ALU.mult, [hk, "negA"], [hk])
            gcp = bank(2, 4, 300)
            O.mm(gcp[:n, :], U_f[:n, :n], g_t[:n, :], True, True, ["U_f", hk], ["b2y"])
            O.cp("dve", gc[:n, :], gcp[:n, :], ["b2y"], [hk])
            O.tsm("dve", ngc[:n, :], gc[:n, :], -1.0, [hk], [hk])
            O.act(e_gc[:n, :], gc[:n, :], AF.Exp, [hk], [hk])
            O.tt("dve", bge[:n, :], beta[:n, :], e_gc[:n, :], ALU.mult, [hk], [hk])
            yield
            g_lo = 0 if need_out else 1
            c_lo = 4 * g_lo
            for g in range(g_lo, 3):
                bk = 3 if g % 2 == 0 else 4
                cps = bank(bk)[:, 0:4 * n].rearrange("p (c t) -> p c t", c=4)
                for c4 in range(4):
                    cc = 4 * g + c4
                    for w4 in range(4):
                        O.mm(cps[:, c4, :], convd[:, w4 * 12 + cc, :], zTh[:, cc, w4:w4 + n], w4 == 0, w4 == 3,
                             ["convd", zk], ["b%d" % bk])
                sv = scB[:, 0:4 * n].rearrange("p (c t) -> p c t", c=4)
                O.act(sv, cps, AF.Exp, ["b%d" % bk], ["scB"], scale=-1.0)
                O.tsa("dve", sv, sv, 1.0, ["scB"], ["scB"])
                O.recip(sv, sv, ["scB"], ["scB"])
                O.tt("dve", csT[:, 4 * g:4 * g + 4, :n], cps, sv, ALU.mult, ["b%d" % bk, "scB"], ["csT"])
                yield
            tvg = PSB[:, 0:512].rearrange("p (a b) -> p a b", a=4)
            for g in range(g_lo, 3):
                for c4 in range(4):
                    O.tr(tvg[:n, c4, :], csT[:, 4 * g + c4, :n], ident_b[:, :], ["csT", "ident_b"], ["b0"])
                O.cp("act", qkv_tm[:n, 4 * g:4 * g + 4, :], tvg[:n, :, :], ["b0"], ["qkv_tm"])
            yield
            sc2v = sc2[:, :].rearrange("p (c t) -> p c t", c=8)
            O.act(sc2v[:n, c_lo:8, :], qkv_tm[:n, c_lo:8, :], AF.Square, ["qkv_tm"], ["sc2"])
            O.red(rs8[:n, c_lo:8], sc2v[:n, c_lo:8, :], ["sc2"], ["rs8"])
            O.act(rs8[:n, c_lo:8], rs8[:n, c_lo:8], AF.Ln, ["rs8", "eps_t"], ["rs8"], bias=eps_t[:n, 0:1])
            O.act(rs8[:n, c_lo:8], rs8[:n, c_lo:8], AF.Exp, ["rs8"], ["rs8"], scale=-0.5)
            if need_out:
                O.tsm("dve", rs8[:n, 0:4], rs8[:n, 0:4], 128.0 ** -0.5, ["rs8"], ["rs8"])
                O.tt("dve", q_tm[:n, :, :], qkv_tm[:n, 0:4, :], rs8[:n, 0:4].unsqueeze(2).to_broadcast([n, 4, 128]), ALU.mult,
                     ["qkv_tm", "rs8"], ["q_tm"])
            O.tt("dve", k_tm[:n, :, :], qkv_tm[:n, 4:8, :], rs8[:n, 4:8].unsqueeze(2).to_broadcast([n, 4, 128]), ALU.mult,
                 ["qkv_tm", "rs8"], ["k_tm%d" % par])
            O.tt("dve", k_tmb[:n, :, :], qkv_tm[:n, 4:8, :], rs8[:n, 4:8].unsqueeze(2).to_broadcast([n, 4, 128]), ALU.mult,
                 ["qkv_tm", "rs8"], ["k_tmb"])
            yield
            for h in range(4):
                O.tsm("dve", kbg[:n, h, :], k_tm[:n, h, :], bge[:n, h:h + 1], ["k_tm%d" % par, hk], ["kbg%d" % par])
                O.tsm("pool", vb[:n, h, :], qkv_tm[:n, 8 + h, :], beta[:n, h:h + 1], ["qkv_tm", hk], ["vb%d" % par])
            for h in range(4):
                O.tr(tvF[:, h, :n], k_tmb[:n, h, :], ident_b[:n, :n], ["k_tmb", "ident_b"], ["b0"])
            if need_out:
                for h in range(4):
                    O.tr(tvF[:, 4 + h, :n], q_tm[:n, h, :], ident_b[:n, :n], ["q_tm", "ident_b"], ["b0"])
                O.cp("act", gqT[:, :, :n], tvF[:, 4:8, :n], ["b0"], ["gqT%d" % par])
            O.cp("dve", gkT[:, :, :n], tvF[:, 0:4, :n], ["b0"], ["gkT%d" % par])
            yield

        tvB = PSB[:, 5 * 1024: 6 * 1024].rearrange("p (a b) -> p a b", a=8)

        def back(c):
            q, j, qi, n, pos0 = c["q"], c["j"], c["qi"], c["n"], c["pos0"]
            need_out, first, last, is_sample, par = c["need_out"], c["first"], c["last"], c["is_sample"], c["par"]
            sgk, hk = "sg%d" % par, "hs%d" % par
            hs, k_tm, kbg, vb, gkT, gqT = hs2[par], k_tm2[par], kbg2[par], vb2[par], gkT2[par], gqT2[par]
            beta, g_t, gc, ngc, e_gc, bge = (hs[:, 0:4], hs[:, 4:8], hs[:, 8:12], hs[:, 12:16], hs[:, 16:20], hs[:, 20:24])
            kk, kbk, vbk, gkk, gqk = "k_tm%d" % par, "kbg%d" % par, "vb%d" % par, "gkT%d" % par, "gqT%d" % par
            O.tt("dve", UG[:n, :, :n], U_f[:n, :n].unsqueeze(1).to_broadcast([n, 4, n]),
                 g_t[:n, :].unsqueeze(2).to_broadcast([n, 4, n]), ALU.mult, ["U_f", hk], ["UG"])
            Grow = bank(6)[:, 0:4 * n].rearrange("p (h t) -> p h t", h=4)
            O.mm(Grow, ones_f[:n, :], UG[:n, :, :n], True, True, ["ones_f", "UG"], ["b6"])
            yield
            O.tt("dve", e_dec[:n, :], Grow[:n, :, n - 1], gc[:n, :], ALU.subtract, ["b6", hk], ["e_dec"])
            O.act(e_dec[:n, :], e_dec[:n, :], AF.Exp, ["e_dec"], ["e_dec"])
            O.act(e_last[:, :], Grow[:, :, n - 1], AF.Exp, ["b6"], ["e_last"])
            yield
            for h in range(4):
                O.tt("dve", dtmp[:n, h, :n], Grow[:n, h, :], mLs[:n, :n], ALU.subtract, ["b6", "mLs"], ["dtmp"])
            for h in range(4):
                O.act(MQs[:n, h, :n], dtmp[:n, h, :n], AF.Exp, ["dtmp", hk], ["MQs"], bias=gc[:n, h:h + 1], scale=-1.0)
            yield
            if need_out:
                for h in range(4):
                    O.tt("dve", dtmp[:n, h, :n], Grow[:n, h, :], mU[:n, :n], ALU.add, ["b6", "mU"], ["dtmp"])
                for h in range(4):
                    O.act(MQT[:n, h, :n], dtmp[:n, h, :n], AF.Exp, ["dtmp", hk], ["MQT"], bias=ngc[:n, h:h + 1])
                yield
            for h in range(4):
                O.tsm("pool", kdec[:n, h, :], k_tm[:n, h, :], e_dec[:n, h:h + 1], [kk, "e_dec"], ["kdec"])
            Gp = bank(6)[:, 0:4 * n].rearrange("p (h t) -> p h t", h=4)
            for h in range(4):
                O.mm(Gp[:n, h, :], gkT[:, h, :n], gkT[:, h, :n], True, True, [gkk], ["b6"])
            yield
            for h in range(4):
                O.tsm("dve", dtmp[:n, h, :n], Gp[:n, h, :], beta[:n, h:h + 1], ["b6", hk], ["dtmp"])
            O.tt("dve", A_b[:n, :, :n], dtmp[:n, :, :n], MQs[:n, :, :n], ALU.mult, ["dtmp", "MQs"], ["A_b"])
            yield
            tv7 = PSB[:, 7 * 1024: 8 * 1024].rearrange("p (a b) -> p a b", a=8)
            for h in range(4):
                O.tr(tv7[:n, h, :n], A_b[:n, h, :n], ident_b[:n, :n], ["A_b", "ident_b"], ["b7"])
            O.cp("act", At_b[:n, :, :n], tv7[:n, 0:4, :n], ["b7"], ["At_b"])
            for s2 in range(2):
                O.cp("pool", TTb[:n, :, s2, :n], ident_f[:n, :n].unsqueeze(1).to_broadcast([n, 4, n]), ["ident_f"], ["TTb"])
                O.cp("pool", TTf[:n, :, s2, :n], ident_f[:n, :n].unsqueeze(1).to_broadcast([n, 4, n]), ["ident_f"], ["TTf"])
            yield
            Xp = bank(5)[:, 0:4 * n].rearrange("p (h t) -> p h t", h=4)
            Yp = PS[:, 6 * 512: 8 * 512].rearrange("p (h s t) -> p h s t", h=4, s=2)
            l = 0
            while (1 << l) < n:
                for h in range(4):
                    O.mm(Xp[:n, h, :], At_b[:n, h, :n], TTb[:n, h, 0, :n], True, True, ["At_b", "TTb"], ["b5"])
                for h in range(4):
                    O.tt("dve", X_b[:n, h, :n], Xp[:n, h, :], lvl[:n, l, :n], ALU.mult, ["b5", "lvl"], ["X_b"])
                yield
                for h in range(4):
                    O.mm(Yp[:n, h, 0, :n], TTb[:n, h, 1, :n], X_b[:n, h, :n], True, True, ["TTb", "X_b"], ["b6", "b7"])
                    O.mm(Yp[:n, h, 1, :n], X_b[:n, h, :n], TTb[:n, h, 1, :n], True, True, ["TTb", "X_b"], ["b6", "b7"])
                O.tt("dve", TTf[:n, :, :, :n], TTf[:n, :, :, :n], Yp[:n, :, :, :n], ALU.subtract, ["TTf", "b6", "b7"], ["TTf"])
                O.cp("act", TTb[:n, :, :, :n], TTf[:n, :, :, :n], ["TTf"], ["TTb"])
                l += 1
                yield
            up_ = bank(5)[:, :].rearrange("p (h t) -> p h t", h=4)
            wTp = bank(6)[:, 0:4 * n].rearrange("p (h t) -> p h t", h=4)
            for h in range(4):
                O.mm(up_[:n, h, :], TTb[:n, h, 1, :n], vb[:n, h, :], True, True, ["TTb", vbk], ["b5"])
            for h in range(4):
                O.mm(wTp[:, h, :], kbg[:n, h, :], TTb[:n, h, 1, :n], True, True, ["TTb", kbk], ["b6"])
            yield
            O.cp("act", u_sb[:n, :, :], up_[:n, :, :], ["b5"], ["u_sb"])
            O.cp("act", wT_b[:, :, :n], wTp[:, :, :], ["b6"], ["wT_b"])
            yield
            if first:
                if is_sample:
                    O.ld("sg", Sg[:, :, :], sgdn.rearrange("h k v -> k h v"), [], ["Sg"])
                else:
                    O.mset("pool", Sg[:, :, :], 0.0, ["Sg"])
                O.cp("act", Sgb[:, :, :], Sg[:, :, :], ["Sg"], ["Sgb"])
            wSp = bank(7)[:, :].rearrange("p (h t) -> p h t", h=4)
            qSp = bank(5)[:, :].rearrange("p (h t) -> p h t", h=4)
            qkTp = bank(6)[:, 0:4 * n].rearrange("p (h t) -> p h t", h=4)
            for h in range(4):
                O.mm(wSp[:n, h, :], wT_b[:, h, :n], Sgb[:, h, :], True, True, ["wT_b", "Sgb"], ["b7"])
            O.tt("dve", vnb[:n, :, :], u_sb[:n, :, :], wSp[:n, :, :], ALU.subtract, ["u_sb", "b7"], ["vnb"])
            yield
            if need_out:
                for h in range(4):
                    O.mm(qSp[:n, h, :], gqT[:, h, :n], Sgb[:, h, :], True, True, [gqk, "Sgb"], ["b5"])
                for h in range(4):
                    O.mm(qkTp[:n, h, :], gkT[:, h, :n], gqT[:, h, :n], True, True, [gkk, gqk], ["b6"])
                for h in range(4):
                    O.tsm("dve", oi[:n, h, :], qSp[:n, h, :], e_gc[:n, h:h + 1], ["b5", hk], ["oi"])
                O.tt("dve", qkT_b[:n, :, :n], qkTp[:n, :, :], MQT[:n, :, :n], ALU.mult, ["b6", "MQT"], ["qkT_b"])
                o2p = bank(7)[:, :].rearrange("p (h t) -> p h t", h=4)
                for h in range(4):
                    O.mm(o2p[:n, h, :], qkT_b[:n, h, :n], vnb[:n, h, :], True, True, ["qkT_b", "vnb"], ["b7"])
                O.tt("dve", o_sb[:n, :, :], oi[:n, :, :], o2p[:n, :, :], ALU.add, ["oi", "b7"], ["o_sb"])
            yield
            dSp = bank(5)[:, :].rearrange("p (h t) -> p h t", h=4)
            for h in range(4):
                O.mm(dSp[:, h, :], kdec[:n, h, :], vnb[:n, h, :], True, True, ["kdec", "vnb"], ["b5"])
            for h in range(4):
                O.tsm("dve", Sg[:, h, :], Sg[:, h, :], e_last[:, h:h + 1], ["Sg", "e_last"], ["Sg"])
            O.tt("dve", Sg[:, :, :], Sg[:, :, :], dSp[:, :, :], ALU.add, ["Sg", "b5"], ["Sg"])
            O.cp("act", Sgb[:, :, :], Sg[:, :, :], ["Sg"], ["Sgb"])
            if last:
                S_out = c["S_out"]
                O.S.dma("sp", "so", lambda e: e.dma_start(out=S_out.rearrange("h k v -> k h v"), in_=Sg[:, :, :]), ["Sg"], [])
            if need_out:
                sg = sg2[par]
                O.act(sc2[:n, 0:512], o_sb[:n, :, :].rearrange("p h d -> p (h d)"), AF.Square, ["o_sb"], ["sc2"])
                O.red(st1[:n, 40:44], sc2[:n, 0:512].rearrange("p (h d) -> p h d", h=4), ["sc2"], ["st_o"])
                rsq(st1[:n, 40:44], st1[:n, 40:44], 1.0 / 128, n, ["st_o"], ["st_o"])
                for h in range(4):
                    O.tsm("dve", o_sb[:n, h, :], o_sb[:n, h, :], st1[:n, 40 + h:41 + h], ["o_sb", "st_o"], ["o_sb"])
                O.tt("pool", o_sb[:n, :, :], o_sb[:n, :, :], go[:n, :].unsqueeze(1).to_broadcast([n, 4, 128]), ALU.mult,
                     ["o_sb", "go"], ["o_sb"])
                O.tt("dve", gm_b[:n, :], o_sb[:n, :, :].rearrange("p h d -> p (h d)"), sg[:n, :], ALU.mult, ["o_sb", sgk], ["gm_b"])
                O.S.dma("sp", "gmd", lambda e: e.dma_start(out=gm_d[q, qi, 0:n, :], in_=gm_b[:n, :]), ["gm_b"], [])
            yield

        jobs = []
        if _STOP != 'prep':
            for s in range(NP):
                jobs.append(dict(q=s, j=0, qi=0, n=16, pos0=0, x_src=meta, need_out=False, first=True, last=False, is_sample=False,
                                 lat_out=p_lat[s, 0:16, :], rope_out=p_rope[s, 0:16, :], conv_out=None, S_out=None, prev_n=0))
                for i in range(NFT):
                    jobs.append(dict(q=s, j=i + 1, qi=i, n=128, pos0=16 + 128 * i, x_src=xp[s, 128 * i:128 * (i + 1), :],
                                     need_out=True, first=False, last=(i == NFT - 1), is_sample=False,
                                     lat_out=p_lat[s, 16 + 128 * i:16 + 128 * (i + 1), :],
                                     rope_out=p_rope[s, 16 + 128 * i:16 + 128 * (i + 1), :],
                                     conv_out=p_conv[s], S_out=p_S[s], prev_n=(16 if i == 0 else 128)))
            jobs.append(dict(q=NP, j=CFT + 1, qi=0, n=64, pos0=LC, x_src=xs, need_out=True, first=True, last=True, is_sample=True,
                             lat_out=s_lat, rope_out=s_rope, conv_out=s_conv, S_out=s_S, prev_n=0))
        for k, c in enumerate(jobs):
            c["par"] = k % 2
            c["load_self"] = (k == 0)
        cache_jobs = [(NP, 0, 16, 0)] + [(NP, i + 1, 128, 16 + 128 * i) for i in range(CFT)]
        if _STOP == 'prep':
            cache_jobs = []

        def run_all(g):
            for _ in g:
                pass

        def interleave(ga, gb):
            da = db = False
            while not (da and db):
                if not da:
                    try:
                        next(ga)
                    except StopIteration:
                        da = True
                if not db:
                    try:
                        next(gb)
                    except StopIteration:
                        db = True

        def front_plus(k):
            yield from front(jobs[k], jobs[k + 1] if k + 1 < len(jobs) else None)
            if cache_jobs and (k % 2 == 1 or len(jobs) - k <= len(cache_jobs)):
                cj = cache_jobs.pop(0)
                yield from cache_kv(*cj)

        if jobs:
            run_all(front_plus(0))
            for k in range(len(jobs)):
                if k + 1 < len(jobs):
                    interleave(back(jobs[k]), front_plus(k + 1))
                else:
                    run_all(back(jobs[k]))
        while cache_jobs:
            run_all(cache_kv(*cache_jobs.pop(0)))
        S.emit()
    if _STOP in ('prep', '1a'):
        return nc

    nc.all_engine_barrier()

    with ExitStack() as st:
        def sb(name, shape, dt=F32):
            return st.enter_context(nc.sbuf_tensor(name, shape, dt))

        PS = st.enter_context(nc.psum_tensor("psb", [128, 4096], F32))
        PSB = PS[:].bitcast(BF16)

        def bank(b, w=512, off=0):
            return PS[:, b * 512 + off: b * 512 + off + w]

        S = Sched(nc, "b")
        O = Ops(S)
        KTc = sb("KTc", [128, 8, NCMAX], BF16)
        Vc = sb("Vc", [128, NKT, 520], BF16)
        w_out_b = sb("w_out_b", [128, 8, 1024], BF16)
        stg = sb("stgb", [128, 1024])
        gM = sb("gM", [128, 4])
        ident_f = sb("ident_fb", [128, 128])
        ident_b = sb("ident_bb", [128, 128], BF16)
        rowa = sb("rowa", [1, 128], BF16)
        rowb = sb("rowb", [1, 128], BF16)
        eps_t = sb("eps_tb", [128, 1])
        qTt2 = [sb("qTt%d" % i, [128, 8, 128], BF16) for i in range(2)]
        mix_b2 = [sb("mix_b%d" % i, [128, 1024], BF16) for i in range(2)]
        mixT = sb("mixT", [128, 8, 128], BF16)
        xt2 = [sb("xtb%d" % i, [128, 1024]) for i in range(2)]
        x1s2 = [sb("x1s%d" % i, [128, 1024]) for i in range(2)]
        P_sb = sb("P_sb", [128, 3, 4, 128], BF16)
        attn_tm = sb("attn_tm", [128, 512])
        junk = sb("junkb", [128, 512], BF16)
        st1 = sb("st1b", [128, 16])

        O.ld("c0", ident_f[:], c_ident, [], ["ident_f"])
        O.cp("dve", ident_b[:], ident_f[:], ["ident_f"], ["ident_b"])
        O.mset("dve", eps_t[:], EPS, ["eps_t"])
        O.mset("dve", rowa[:, :], 1.0, ["rowa"])
        O.mset("dve", rowa[:, 0:64], 0.0, ["rowa"])
        O.mset("dve", rowb[:, :], -30000.0, ["rowb"])
        O.mset("dve", rowb[:, 64:128], 0.0, ["rowb"])
        tmpT = sb("tmpTb", [8, 128])
        O.ld("ldT", tmpT[:4, :], mla_out_norm.rearrange("(k p) -> k p", p=128), [], ["tmpT"])
        O.tr(PS[:, 0:4], tmpT[:4, :], ident_f[:4, :4], ["tmpT", "ident_f"], ["bs0"])
        O.cp("dve", gM[:, :], PS[:, 0:4], ["bs0"], ["gM"])
        for kc in range(8):
            O.ld("stg", stg[:], w_out[kc * 128:(kc + 1) * 128, :], [], ["stg"])
            if kc < 4:
                O.tsm("dve", w_out_b[:, kc, :], stg[:], gM[:, kc:kc + 1], ["stg", "gM"], ["w_out_b"])
            else:
                O.cp("dve", w_out_b[:, kc, :], stg[:], ["stg"], ["w_out_b"])

        SBANK = (0, 1, 7)
        tcount = [0]

        def attn_seq(q, tiles):
            units = []
            for ti, (qi, n, x_src, keytiles, diag_j, x1_idx) in enumerate(tiles):
                groups = []
                cur = []
                for kt in keytiles:
                    if kt[1] != 128:
                        if cur:
                            groups.append(cur)
                            cur = []
                        groups.append([kt])
                    else:
                        cur.append(kt)
                        if len(cur) == 4:
                            groups.append(cur)
                            cur = []
                if cur:
                    groups.append(cur)
                for h in range(8):
                    for gi, g in enumerate(groups):
                        units.append((ti, h, g, gi == 0, gi == len(groups) - 1))
            tpar = {}

            def prologue(ti):
                qi, n, x_src, keytiles, diag_j, x1_idx = tiles[ti]
                p = tcount[0] % 2
                tcount[0] += 1
                tpar[ti] = p
                O.ld("qt%d" % p, qTt2[p][:, :, :n], qT_d[q, qi, :, :, 0:n], [], ["qTt%d" % p])
                O.ld("gm%d" % p, mix_b2[p][:n, 512:1024], gm_d[q, qi, 0:n, :], [], ["mix_g%d" % p])
                O.ld("xt%d" % p, xt2[p][:n, :], x_src, [], ["xt%d" % p])

            def qk(ui):
                ti, h, g, fg, lg = units[ui]
                qi, n, x_src, keytiles, diag_j, x1_idx = tiles[ti]
                if ti not in tpar:
                    prologue(ti)
                p = tpar[ti]
                par = ui % 3
                Sp = bank(SBANK[par])[:, :].rearrange("p (s t) -> p s t", s=4)
                for s_, (j, nk, col0) in enumerate(g):
                    dg = (j == diag_j)
                    O.mm(Sp[:nk, s_, :n], KTc[0:96, h, col0:col0 + nk], qTt2[p][0:96, h, :n], True, not dg,
                         ["KTc", "qTt%d" % p], ["bs%d" % par])
                    if dg:
                        O.mm(Sp[:nk, s_, :n], rowa[0:1, :nk], rowb[0:1, :n], False, True, ["rowa", "rowb"], ["bs%d" % par])

            def expv(ui):
                ti, h, g, fg, lg = units[ui]
                qi, n, x_src, keytiles, diag_j, x1_idx = tiles[ti]
                p = tpar[ti]
                par = ui % 3
                Sp = bank(SBANK[par])[:, :].rearrange("p (s t) -> p s t", s=4)
                Op = bank(2 + (h % 2))
                ok = "bo%d" % (h % 2)
                nk0 = g[0][1]
                O.act(P_sb[:nk0, par, 0:len(g), :n], Sp[:nk0, 0:len(g), :n], AF.Exp, ["bs%d" % par], ["P%d" % par], scale=ATT_SCALE)
                for s_, (j, nk, col0) in enumerate(g):
                    O.mm(Op[:n, 0:65], P_sb[:nk, par, s_, :n], Vc[:nk, j, h * 65:(h + 1) * 65], fg and s_ == 0,
                         lg and s_ == len(g) - 1, ["P%d" % par, "Vc"], [ok])
                if lg:
                    O.recip(st1[:n, h:h + 1], Op[:n, 64:65], [ok], ["st_r%d" % h])
                    O.tsm("dve", attn_tm[:n, h * 64:(h + 1) * 64], Op[:n, 0:64], st1[:n, h:h + 1], [ok, "st_r%d" % h], ["attn_tm"])
                    if h == 7:
                        epilogue(ti)

            def epilogue(ti):
                qi, n, x_src, keytiles, diag_j, x1_idx = tiles[ti]
                p = tpar[ti]
                mix_b = mix_b2[p]
                O.act(junk[:n, :], attn_tm[:n, :], AF.Square, ["attn_tm"], ["junk", "st_a"], accum_out=st1[:n, 8:9])
                O.act(st1[:n, 8:9], st1[:n, 8:9], AF.Ln, ["st_a", "eps_t"], ["st_a"], scale=1.0 / 512, bias=eps_t[:n, 0:1])
                O.act(st1[:n, 8:9], st1[:n, 8:9], AF.Exp, ["st_a"], ["st_a"], scale=-0.5)
                O.asc(mix_b[:n, 0:512], attn_tm[:n, :], st1[:n, 8:9], ["attn_tm", "st_a"], ["mix_a%d" % p])
                tv = PSB[:, 4 * 1024: 5 * 1024].rearrange("p (a b) -> p a b", a=8)
                for c in range(8):
                    O.tr(tv[:, c, :n], mix_b[:n, c * 128:(c + 1) * 128], ident_b[:n, :n],
                         ["mix_a%d" % p, "mix_g%d" % p, "ident_b"], ["b4"])
                O.cp("dve", mixT[:, :, :n], tv[:, :, :n], ["b4"], ["mixT"])
                x1p = PS[:, 5 * 512: 7 * 512]
                for half in range(2):
                    for c in range(8):
                        O.mm(x1p[:n, half * 512:(half + 1) * 512], mixT[:, c, :n], w_out_b[:, c, half * 512:(half + 1) * 512],
                             c == 0, c == 7, ["mixT", "w_out_b"], ["b56"])
                O.tt("dve", x1s2[p][:n, :], x1p[:n, :], xt2[p][:n, :], ALU.add, ["b56", "xt%d" % p], ["x1s%d" % p])
                O.S.dma("pool", "x1o%d" % p, lambda e: e.dma_start(out=x1_d[x1_idx, 0:n, :], in_=x1s2[p][:n, :]), ["x1s%d" % p], [])

            LOOK = 2
            for ui in range(min(LOOK, len(units))):
                qk(ui)
            for ui in range(len(units)):
                if ui + LOOK < len(units):
                    qk(ui + LOOK)
                expv(ui)

        def load_cache(q, ncols, kts):
            for h in range(8):
                O.ld("ktc", KTc[:, h, 0:ncols], KT_d[q, :, h, 0:ncols], [], ["KTc"])
            full = [j for (j, nk, c0) in kts if nk == 128]
            if full:
                for j0 in range(full[0], full[-1] + 1, 4):
                    j1 = min(j0 + 4, full[-1] + 1)
                    O.ld("vc", Vc[:, j0:j1, :], V_d[q, j0:j1, :, :].rearrange("t p c -> p t c"), [], ["Vc"])
            for (j, nk, c0) in kts:
                if nk != 128:
                    O.ld("vc", Vc[:nk, j, :], V_d[q, j, 0:nk, :], [], ["Vc"])

        for s in range(NP):
            load_cache(s, LP, [(0, 16, 0)] + [(jj + 1, 128, 16 + 128 * jj) for jj in range(NFT)])
            tl = []
            for i in range(NFT):
                kts = [(0, 16, 0)] + [(jj + 1, 128, 16 + 128 * jj) for jj in range(i + 1)]
                tl.append((i, 128, xp[s, 128 * i:128 * (i + 1), :], kts, i + 1, s * NFT + i))
            attn_seq(s, tl)
        kts = [(0, 16, 0)] + [(jj + 1, 128, 16 + 128 * jj) for jj in range(CFT)] + [(CFT + 1, 64, LC)]
        load_cache(NP, LC + 64, kts)
        attn_seq(NP, [(0, 64, xs, kts, -1, NP * NFT)])
        S.emit()
    if _STOP == '1b':
        return nc

    nc.all_engine_barrier()

    with ExitStack() as st:
        def sb(name, shape, dt=F32):
            return st.enter_context(nc.sbuf_tensor(name, shape, dt))

        PS = st.enter_context(nc.psum_tensor("psc", [128, 4096], F32))
        PSB = PS[:].bitcast(BF16)
        S = Sched(nc, "c")
        O = Ops(S)
        w_up_b = sb("w_up_b", [128, 8, 4096], BF16)
        w_dn_b = sb("w_dn_b", [128, 32, 1024], BF16)
        gP = sb("gP", [128, 8])
        ident_f = sb("ident_fc", [128, 128])
        ident_b = sb("ident_bc", [128, 128], BF16)
        eps_t = sb("eps_tc", [128, 1])
        x1t2 = [sb("x1t%d" % i, [128, 2, 1024]) for i in range(2)]
        hn2 = [sb("hn%d" % i, [128, 1024], BF16) for i in range(2)]
        hT2 = [sb("hT%d" % i, [128, 8, 256], BF16) for i in range(2)]
        rl = sb("rl", [128, 2, 512])
        upT = sb("upT", [128, 32, 256], BF16)
        ys2 = [sb("ys%d" % i, [128, 1024]) for i in range(2)]
        junk = sb("junkc", [128, 1024], BF16)
        st1 = sb("st1c", [128, 8])

        O.ld("c0", ident_f[:], c_ident, [], ["ident_f"])
        O.cp("dve", ident_b[:], ident_f[:], ["ident_f"], ["ident_b"])
        O.mset("dve", eps_t[:], EPS, ["eps_t"])
        tmpT = sb("tmpTc", [8, 128])
        O.ld("ldT", tmpT[:8, :], mlp_norm.rearrange("(k p) -> k p", p=128), [], ["tmpT"])
        O.tr(PS[:, 0:8], tmpT[:8, :], ident_f[:8, :8], ["tmpT", "ident_f"], ["b0"])
        O.cp("dve", gP[:, :], PS[:, 0:8], ["b0"], ["gP"])
        for kc in range(8):
            sp_ = kc % 2
            stg = x1t2[sp_][:, :, :].rearrange("p a d -> p (a d)")
            for hf in range(2):
                O.ld("stg%d" % sp_, stg[:, :], w_up[kc * 128:(kc + 1) * 128, hf * 2048:(hf + 1) * 2048], [], ["x1t%d" % sp_])
                O.tsm("dve" if hf == 0 else "pool", w_up_b[:, kc, hf * 2048:(hf + 1) * 2048], stg[:, :], gP[:, kc:kc + 1],
                      ["x1t%d" % sp_, "gP"], ["w_up_b"])
        for c2 in range(16):
            sp_ = c2 % 2
            O.ld("stg%d" % sp_, x1t2[sp_][:, :, :], w_down[c2 * 256:(c2 + 1) * 256, :].rearrange("(f p) d -> p f d", p=128),
                 [], ["x1t%d" % sp_])
            O.cp("dve" if sp_ == 0 else "pool", w_dn_b[:, 2 * c2:2 * c2 + 2, :], x1t2[sp_][:, :, :], ["x1t%d" % sp_], ["w_dn_b"])

        def mlp_front_a(bi, subs):
            p = bi % 2
            for si, (idx, n, y_out) in enumerate(subs):
                O.ld("x1%d" % p, x1t2[p][:n, si, :], x1_d[idx, 0:n, :], [], ["x1t%d" % p])
            for si, (idx, n, y_out) in enumerate(subs):
                O.act(junk[:n, :], x1t2[p][:n, si, :], AF.Square, ["x1t%d" % p], ["junk", "st"], accum_out=st1[:n, si:si + 1])
                O.act(st1[:n, si:si + 1], st1[:n, si:si + 1], AF.Ln, ["st", "eps_t"], ["st"], scale=1.0 / 1024, bias=eps_t[:n, 0:1])
                O.act(st1[:n, si:si + 1], st1[:n, si:si + 1], AF.Exp, ["st"], ["st"], scale=-0.5)
                O.asc(hn2[si][:n, :], x1t2[p][:n, si, :], st1[:n, si:si + 1], ["x1t%d" % p, "st"], ["hn%d" % si])

        def mlp_front_b(bi, subs):
            p = bi % 2
            for si, (idx, n, y_out) in enumerate(subs):
                tv = PSB[:, 0:1024].rearrange("p (a b) -> p a b", a=8)
                for kc in range(8):
                    O.tr(tv[:, kc, :n], hn2[si][:n, kc * 128:(kc + 1) * 128], ident_b[:n, :n], ["hn%d" % si, "ident_b"], ["b0"])
                O.cp("dve", hT2[p][:, :, si * 128:si * 128 + n], tv[:, :, :n], ["b0"], ["hT%d" % p])

        def mlp_body(bi, subs, mid=None):
            p = bi % 2
            nt = sum(n for (_, n, _) in subs) if len(subs) == 1 else 256
            for fc in range(32):
                par = fc % 2
                reg = PS[:, (1 + par) * 512:(1 + par) * 512 + nt]
                rk = "bu%d" % par
                for kc in range(8):
                    O.mm(reg, w_up_b[:, kc, fc * 128:(fc + 1) * 128], hT2[p][:, kc, 0:nt], kc == 0, kc == 7,
                         ["w_up_b", "hT%d" % p], [rk])
                O.act(rl[:, par, 0:nt], reg, AF.Relu, [rk], ["rl%d" % par])
                O.tt("dve", upT[:, fc, 0:nt], rl[:, par, 0:nt], rl[:, par, 0:nt], ALU.mult,
                     ["rl%d" % par], ["upT"])
            if mid is not None:
                mid()
            for si, (idx, n, y_out) in enumerate(subs):
                yb = 5 if si == 0 else 3
                yp = PS[:, yb * 512: (yb + 2) * 512]
                yk = "by%d" % si
                for half in range(2):
                    for fc in range(32):
                        O.mm(yp[:n, half * 512:(half + 1) * 512], upT[:, fc, si * 128:si * 128 + n],
                             w_dn_b[:, fc, half * 512:(half + 1) * 512], fc == 0, fc == 31, ["upT", "w_dn_b"], [yk])
                ys = ys2[si]
                O.tt("dve", ys[:n, :], yp[:n, :], x1t2[p][:n, si, :], ALU.add, [yk, "x1t%d" % p], ["ys%d" % si])
                O.S.dma("pool", "yo%d" % si, lambda e, ys=ys, n=n, y_out=y_out: e.dma_start(out=y_out, in_=ys[:n, :]),
                        ["ys%d" % si], [])

        blocks = []
        flat = []
        for s in range(NP):
            for i in range(NFT):
                flat.append((s * NFT + i, 128, y_p[s, 128 * i:128 * (i + 1), :]))
        for i in range(0, len(flat), 2):
            blocks.append(flat[i:i + 2])
        blocks.append([(NP * NFT, 64, y_s)])
        mlp_front_a(0, blocks[0])
        mlp_front_b(0, blocks[0])
        for bi in range(len(blocks)):
            if bi + 1 < len(blocks):
                mlp_front_a(bi + 1, blocks[bi + 1])
                mlp_body(bi, blocks[bi], mid=lambda bi=bi: mlp_front_b(bi + 1, blocks[bi + 1]))
            else:
                mlp_body(bi, blocks[bi])
        S.emit()
    return nc


_NC_CACHE = {}


def run_cores(per_core_inputs, NP, SEQ, PAST):
    key = (NP, SEQ, PAST)
    if key not in _NC_CACHE:
        _NC_CACHE[key] = build_nc(NP, SEQ, PAST)
    nc = _NC_CACHE[key]
    res = run_bass_kernel_spmd(nc, per_core_inputs, core_ids=list(range(len(per_core_inputs))))
    return res.results


def make_core_inputs(c, NP, inputs, consts):
    f = lambda a: np.ascontiguousarray(np.asarray(a), dtype=np.float32)
    d = {
        "xp": f(inputs["x_prompt"][NP * c:NP * (c + 1)]),
        "meta": f(inputs["meta_tokens"]),
        "xs": f(inputs["x_sample"][c]),
        "clat": f(inputs["cache_kv_latent"][0, c]),
        "crope": f(inputs["cache_k_rope"][0, c]),
        "sgdn": f(inputs["state_gdn"][0, c]),
        "sconv": f(inputs["state_conv"][0, c]),
    }
    for k in ("w_in", "w_uq", "w_uk", "w_uv", "w_out", "w_up", "w_down", "attn_norm", "q_a_norm", "kv_a_norm", "q_norm",
              "k_norm", "mla_out_norm", "conv_w", "a_log", "dt_bias", "gdn_out_norm", "mlp_norm"):
        d[k] = f(inputs[k][0])
    d.update(consts)
    return d


def kernel(**inputs):
    NCORES = 8
    B, SEQ = inputs["x_prompt"].shape[:2]
    NP = B // NCORES
    PAST = inputs["cache_kv_latent"].shape[2] - 16
    LP = 16 + SEQ
    consts = host_consts(max(LP, 16 + PAST + 64))
    ins = [make_core_inputs(c, NP, inputs, consts) for c in range(NCORES)]
    res = run_cores(ins, NP, SEQ, PAST)
    cat = lambda k: np.concatenate([r[k] for r in res], axis=0)
    stk = lambda k: np.stack([r[k] for r in res], axis=0)
    y_p = cat("y_p")
    y_s = stk("y_s")
    return (y_p, y_s, cat("p_lat")[None], cat("p_rope")[None], cat("p_S")[None], cat("p_conv")[None],
            stk("s_lat")[None], stk("s_rope")[None], stk("s_S")[None], stk("s_conv")[None])
```

```python
import numpy as np
import ml_dtypes
from contextlib import ExitStack
import concourse.bass as bass
import concourse.mybir as mybir
from concourse.bass_utils import run_bass_kernel_spmd

F32 = mybir.dt.float32
BF16 = mybir.dt.bfloat16
AF = mybir.ActivationFunctionType
ALU = mybir.AluOpType
AX = mybir.AxisListType
EPS = 1e-6
NEG = -30000.0
ENGS = ("sp", "act", "dve", "pool", "pe")


class Sched:
    def __init__(self, nc, tag):
        self.nc = nc
        self.tag = tag
        self.lists = {e: [] for e in ENGS}
        self.last_writer = {}
        self.readers = {}
        self.dma_count = {}
        self.waited = {e: {} for e in ENGS}

    def _deps(self, eng, reads, writes):
        deps = []
        for k in reads:
            w = self.last_writer.get(k)
            if w is not None:
                deps.append((w, True))
        for k in writes:
            w = self.last_writer.get(k)
            if w is not None:
                deps.append((w, False))
            for t in self.readers.get(k, {}).values():
                deps.append((t, False))
        out = []
        wd = self.waited[eng]
        for t, raw in deps:
            if t[0] == "E" and t[1] == eng and eng == "pe":
                continue
            sid = (t[0], t[1])
            if wd.get(sid, -1) >= t[2]:
                continue
            wd[sid] = t[2]
            out.append(t)
        return out

    def _commit(self, token, stream, reads, writes):
        for k in reads:
            self.readers.setdefault(k, {})[stream] = token
        for k in writes:
            self.last_writer[k] = token
            self.readers[k] = {}

    def op(self, eng, fn, reads=(), writes=()):
        deps = self._deps(eng, reads, writes)
        idx = len(self.lists[eng])
        self.lists[eng].append({"fn": fn, "deps": deps, "flag": False, "dma": None})
        self._commit(("E", eng, idx), eng, reads, writes)

    def dma(self, eng, key, fn, reads=(), writes=(), n=1):
        deps = self._deps(eng, reads, writes)
        c = self.dma_count.get(key, 0) + 16 * n
        self.dma_count[key] = c
        self.lists[eng].append({"fn": fn, "deps": deps, "flag": False, "dma": key})
        self._commit(("D", key, c), "D" + key, reads, writes)

    def emit(self):
        nc = self.nc
        for e in ENGS:
            for rec in self.lists[e]:
                for t in rec["deps"]:
                    if t[0] == "E":
                        self.lists[t[1]][t[2]]["flag"] = True
        val = {}
        for e in ENGS:
            c = 0
            v = []
            for rec in self.lists[e]:
                if rec["flag"] and rec["dma"] is None:
                    c += 1
                v.append(c)
            val[e] = v
        with ExitStack() as st:
            esem = {e: st.enter_context(nc.semaphore(self.tag + "s_" + e)) for e in ENGS}
            dsem = {k: st.enter_context(nc.semaphore(self.tag + "d_" + k)) for k in self.dma_count}
            block = st.enter_context(nc.Block())
            final = dict(self.dma_count)

            def run(e, engine):
                for rec in self.lists[e]:
                    for t in rec["deps"]:
                        if t[0] == "E":
                            engine.wait_ge(esem[t[1]], val[t[1]][t[2]])
                        else:
                            engine.wait_ge(dsem[t[1]], t[2])
                    r = rec["fn"](engine)
                    if rec["dma"] is not None:
                        if not isinstance(r, (list, tuple)):
                            r = [r]
                        for ins in r:
                            ins.then_inc(dsem[rec["dma"]], 16)
                    elif rec["flag"]:
                        r.then_inc(esem[e], 1)
                if e == "sp":
                    for k, c in final.items():
                        engine.wait_ge(dsem[k], c)

            @block.sync
            def _(eng):
                run("sp", eng)

            @block.scalar
            def _(eng):
                run("act", eng)

            @block.vector
            def _(eng):
                run("dve", eng)

            @block.gpsimd
            def _(eng):
                run("pool", eng)

            @block.tensor
            def _(eng):
                run("pe", eng)


class Ops:
    def __init__(self, S):
        self.S = S

    def act(self, out, in_, func, r, w, **kw):
        self.S.op("act", lambda e: e.activation(out=out, in_=in_, func=func, **kw), r, w)

    def tt(self, eng, out, in0, in1, op, r, w):
        self.S.op(eng, lambda e: e.tensor_tensor(out=out, in0=in0, in1=in1, op=op), r, w)

    def stt(self, eng, out, in0, scalar, in1, op0, op1, r, w):
        self.S.op(eng, lambda e: e.scalar_tensor_tensor(out=out, in0=in0, scalar=scalar, in1=in1, op0=op0, op1=op1), r, w)

    def tsm(self, eng, out, in0, s1, r, w):
        self.S.op(eng, lambda e: e.tensor_scalar_mul(out=out, in0=in0, scalar1=s1), r, w)

    def asc(self, out, in_, sc, r, w):
        self.S.op("act", lambda e: e.activation(out=out, in_=in_, func=AF.Copy, scale=sc), r, w)

    def tsa(self, eng, out, in0, s1, r, w):
        self.S.op(eng, lambda e: e.tensor_scalar_add(out=out, in0=in0, scalar1=s1), r, w)

    def cp(self, eng, out, in_, r, w):
        if eng == "act":
            self.S.op("act", lambda e: e.copy(out=out, in_=in_), r, w)
        else:
            self.S.op(eng, lambda e: e.tensor_copy(out=out, in_=in_), r, w)

    def recip(self, out, in_, r, w):
        self.S.op("dve", lambda e: e.reciprocal(out=out, in_=in_), r, w)

    def red(self, out, in_, r, w):
        self.S.op("dve", lambda e: e.tensor_reduce(out=out, in_=in_, axis=AX.X, op=ALU.add), r, w)

    def mset(self, eng, ap, v, w):
        self.S.op(eng, lambda e: e.memset(ap, v), (), w)

    def mm(self, out, lhsT, rhs, start, stop, r, w):
        self.S.op("pe", lambda e: e.matmul(out, lhsT=lhsT, rhs=rhs, start=start, stop=stop), r, w)

    def tr(self, out, in_, ident, r, w):
        self.S.op("pe", lambda e: e.transpose(out=out, in_=in_, identity=ident), r, w)

    def ld(self, key, out, in_, r, w, slow=False):
        if slow:
            self.S.dma("sp", key, lambda e: e.dma_start(out=out, in_=in_, allow_slow_non_contiguous=True), r, w)
        else:
            self.S.dma("sp", key, lambda e: e.dma_start(out=out, in_=in_), r, w)


def host_consts(npos):
    c = {}
    c["c_ident"] = np.eye(128, dtype=np.float32)
    p = np.arange(128)[:, None]
    q = np.arange(128)[None, :]
    c["c_U"] = (p <= q).astype(np.float32)
    c["c_mLs"] = np.where(p > q, 0.0, NEG).astype(np.float32)
    c["c_mU"] = np.where(q >= p, 0.0, NEG).astype(np.float32)
    lv = np.zeros((128, 7, 128), np.float32)
    for l in range(7):
        b = 1 << l
        lv[:, l, :] = ((p // (2 * b) == q // (2 * b)) & (p % (2 * b) >= b) & (q % (2 * b) < b)).astype(np.float32)
    c["c_lvl"] = lv
    half = 16
    inv_freq = (np.float32(10000.0) ** (-np.arange(half, dtype=np.float32) / np.float32(half))).astype(np.float32)
    ang = np.arange(npos, dtype=np.float32)[:, None] * inv_freq[None, :]
    c["c_cs"] = np.concatenate([np.cos(ang), np.sin(ang)], axis=1).astype(np.float32)
    return c


import os
_STOP = os.environ.get('KSTOP', '')
_CUT = float(os.environ.get('KCUT', '99'))


def build_nc(NP, SEQ, PAST):
    NFT = SEQ // 128
    LP = 16 + SEQ
    CFT = PAST // 128
    LC = 16 + PAST
    NSEQ = NP + 1
    NCMAX = max(LP, LC + 64)
    NKT = max(NFT + 1, CFT + 2)
    NQT = max(NFT, 1)
    NX1 = NP * NFT + 1
    NPOS = max(LP, LC + 64)

    nc = bass.Bass("TRN2", target_bir_lowering=False)

    def din(name, shape):
        return nc.dram_tensor(name, shape, F32, kind="ExternalInput").ap()

    def dout(name, shape):
        return nc.dram_tensor(name, shape, F32, kind="ExternalOutput").ap()

    xp = din("xp", [NP, SEQ, 1024])
    meta = din("meta", [16, 1024])
    xs = din("xs", [64, 1024])
    clat = din("clat", [LC, 256])
    crope = din("crope", [LC, 32])
    sgdn = din("sgdn", [4, 128, 128])
    sconv = din("sconv", [3, 1536])
    w_in = din("w_in", [1024, 2728])
    w_uq = din("w_uq", [384, 768])
    w_uk = din("w_uk", [256, 512])
    w_uv = din("w_uv", [256, 512])
    w_out = din("w_out", [1024, 1024])
    w_up = din("w_up", [1024, 4096])
    w_down = din("w_down", [4096, 1024])
    attn_norm = din("attn_norm", [1024])
    q_a_norm = din("q_a_norm", [384])
    kv_a_norm = din("kv_a_norm", [256])
    q_norm = din("q_norm", [96])
    k_norm = din("k_norm", [96])
    mla_out_norm = din("mla_out_norm", [512])
    conv_w = din("conv_w", [4, 1536])
    a_log = din("a_log", [4])
    dt_bias = din("dt_bias", [4])
    gdn_out_norm = din("gdn_out_norm", [128])
    mlp_norm = din("mlp_norm", [1024])
    c_ident = din("c_ident", [128, 128])
    c_U = din("c_U", [128, 128])
    c_mLs = din("c_mLs", [128, 128])
    c_mU = din("c_mU", [128, 128])
    c_lvl = din("c_lvl", [128, 7, 128])
    c_cs = din("c_cs", [NPOS, 32])

    y_p = dout("y_p", [NP, SEQ, 1024])
    y_s = dout("y_s", [64, 1024])
    p_lat = dout("p_lat", [NP, LP, 256])
    p_rope = dout("p_rope", [NP, LP, 32])
    p_S = dout("p_S", [NP, 4, 128, 128])
    p_conv = dout("p_conv", [NP, 3, 1536])
    s_lat = dout("s_lat", [64, 256])
    s_rope = dout("s_rope", [64, 32])
    s_S = dout("s_S", [4, 128, 128])
    s_conv = dout("s_conv", [3, 1536])

    KT_d = nc.dram_tensor("KT_d", [NSEQ, 128, 8, NCMAX], BF16, kind="Internal").ap()
    V_d = nc.dram_tensor("V_d", [NSEQ, NKT, 128, 520], BF16, kind="Internal").ap()
    qT_d = nc.dram_tensor("qT_d", [NSEQ, NQT, 128, 8, 128], BF16, kind="Internal").ap()
    gm_d = nc.dram_tensor("gm_d", [NSEQ, NQT, 128, 512], BF16, kind="Internal").ap()
    x1_d = nc.dram_tensor("x1_d", [NX1, 128, 1024], F32, kind="Internal").ap()

    ATT_SCALE = 96.0 ** -0.5

    with ExitStack() as st:
        def sb(name, shape, dt=F32):
            return st.enter_context(nc.sbuf_tensor(name, shape, dt))

        PS = st.enter_context(nc.psum_tensor("ps", [128, 4096], F32))
        PSB = PS[:].bitcast(BF16)

        def bank(b, w=512, off=0):
            return PS[:, b * 512 + off: b * 512 + off + w]

        S = Sched(nc, "a")
        O = Ops(S)

        w_in_b = sb("w_in_b", [128, 8, 2728], BF16)
        w_uq_b = sb("w_uq_b", [128, 3, 768], BF16)
        w_uk_b = sb("w_uk_b", [128, 2, 512], BF16)
        w_uv_b = sb("w_uv_b", [128, 2, 512], BF16)
        convd = sb("convd", [128, 48, 128], BF16)
        stg = sb("stg", [128, 2728], F32)
        gA = sb("gA", [128, 8])
        gQ = sb("gQ", [128, 3])
        cw = sb("cw", [128, 4, 12])
        ident_f = sb("ident_f", [128, 128])
        ident_b = sb("ident_b", [128, 128], BF16)
        U_f = sb("U_f", [128, 128])
        ones_f = sb("ones_f", [128, 128])
        mLs = sb("mLs", [128, 128])
        mU = sb("mU", [128, 128])
        lvl = sb("lvl", [128, 7, 128])
        gkv = sb("gkv", [128, 256])
        gq = sb("gq", [128, 96])
        gk = sb("gk", [128, 96])
        go = sb("go", [128, 128])
        negA = sb("negA", [128, 4])
        dtb = sb("dtb", [128, 4])
        eps_t = sb("eps_t", [128, 1])
        one_t = sb("one_t", [128, 1])

        O.ld("c0", ident_f[:], c_ident, [], ["ident_f"])
        O.ld("c1", U_f[:], c_U, [], ["U_f"])
        O.ld("c2", mLs[:], c_mLs, [], ["mLs"])
        O.ld("c3", mU[:], c_mU, [], ["mU"])
        O.ld("c4", lvl[:], c_lvl, [], ["lvl"])
        O.ld("c5", gkv[:], kv_a_norm.partition_broadcast(128), [], ["gkv"])
        O.ld("c6", gq[:], q_norm.partition_broadcast(128), [], ["gq"])
        O.ld("c7", gk[:], k_norm.partition_broadcast(128), [], ["gk"])
        O.ld("c8", go[:], gdn_out_norm.partition_broadcast(128), [], ["go"])
        O.ld("c9", negA[:], a_log.partition_broadcast(128), [], ["negA"])
        O.ld("c10", dtb[:], dt_bias.partition_broadcast(128), [], ["dtb"])
        tmpT = sb("tmpT", [48, 128])

        def ld_T(dst, src2d, k, key):
            O.ld("ldT", tmpT[:k, :], src2d, [], ["tmpT"])
            O.tr(PS[:, 0:k], tmpT[:k, :], ident_f[:k, :k], ["tmpT", "ident_f"], ["b0"])
            O.cp("dve", dst, PS[:, 0:k], ["b0"], [key])

        ld_T(gA[:, :], attn_norm.rearrange("(k p) -> k p", p=128), 8, "gA")
        ld_T(gQ[:, :], q_a_norm.rearrange("(k p) -> k p", p=128), 3, "gQ")
        ld_T(cw[:, :, :].rearrange("p w c -> p (w c)"), conv_w.rearrange("w (c p) -> (w c) p", p=128), 48, "cw")
        O.cp("dve", ident_b[:], ident_f[:], ["ident_f"], ["ident_b"])
        O.mset("dve", ones_f[:], 1.0, ["ones_f"])
        O.mset("dve", eps_t[:], EPS, ["eps_t"])
        O.mset("dve", one_t[:], 1.0, ["one_t"])
        O.act(negA[:], negA[:], AF.Exp, ["negA"], ["negA"])
        O.tsm("dve", negA[:], negA[:], -1.0, ["negA"], ["negA"])
        for kc in range(8):
            O.ld("stg", stg[:], w_in[kc * 128:(kc + 1) * 128, :], [], ["stg"])
            O.tsm("dve", w_in_b[:, kc, :], stg[:], gA[:, kc:kc + 1], ["stg", "gA"], ["w_in_b"])
        for kc in range(3):
            O.ld("stg", stg[:, 0:768], w_uq[kc * 128:(kc + 1) * 128, :], [], ["stg"])
            O.tsm("dve", w_uq_b[:, kc, :], stg[:, 0:768], gQ[:, kc:kc + 1], ["stg", "gQ"], ["w_uq_b"])
        for kc in range(2):
            O.ld("stg", stg[:, 0:512], w_uk[kc * 128:(kc + 1) * 128, :], [], ["stg"])
            O.cp("dve", w_uk_b[:, kc, :], stg[:, 0:512], ["stg"], ["w_uk_b"])
            O.ld("stg", stg[:, 0:512], w_uv[kc * 128:(kc + 1) * 128, :], [], ["stg"])
            O.cp("dve", w_uv_b[:, kc, :], stg[:, 0:512], ["stg"], ["w_uv_b"])
        for w4 in range(4):
            for c in range(12):
                O.tsm("dve", convd[:, w4 * 12 + c, :], ident_f[:], cw[:, w4, c:c + 1],
                      ["ident_f", "cw"], ["convd"])

        xt2 = [sb("xt%d" % i, [128, 1024]) for i in range(2)]
        junk = sb("junk", [128, 1024], BF16)
        scF = sb("scF", [128, 768])
        scB = sb("scB", [128, 512])
        sc2 = sb("sc2", [128, 1024])
        st1 = sb("st1", [128, 64])
        xn = sb("xn", [128, 1024], BF16)
        xT = sb("xT", [128, 8, 128], BF16)
        zTh = sb("zTh", [128, 12, 131], BF16)
        sg2 = [sb("sg%d" % i, [128, 512]) for i in range(2)]
        hs2 = [sb("hs%d" % i, [128, 32]) for i in range(2)]
        k_tm2 = [sb("k_tm%d" % i, [128, 4, 128]) for i in range(2)]
        kbg2 = [sb("kbg%d" % i, [128, 4, 128], BF16) for i in range(2)]
        vb2 = [sb("vb%d" % i, [128, 4, 128], BF16) for i in range(2)]
        gkT2 = [sb("gkT%d" % i, [128, 4, 128], BF16) for i in range(2)]
        gqT2 = [sb("gqT%d" % i, [128, 4, 128], BF16) for i in range(2)]
        cst = sb("cst", [128, 1536])
        qln = sb("qln", [128, 384], BF16)
        qlT = sb("qlT", [128, 3, 128], BF16)
        qtmp = sb("qtmp", [128, 8, 96])
        rtmp = sb("rtmp", [128, 8, 64])
        q_pad = sb("q_pad", [128, 8, 128], BF16)
        qT = sb("qT", [128, 8, 128], BF16)
        lat_n = sb("lat_n", [128, 256])
        lat_b = sb("lat_b", [128, 256], BF16)
        latT = sb("latT", [128, 2, 128], BF16)
        rope_raw = sb("rope_raw", [128, 32])
        cs_t = sb("cs_t", [128, 32])
        rg = sb("rg", [128, 32])
        rr = sb("rr", [128, 32])
        t4 = sb("t4", [128, 64])
        k_pad = sb("k_pad", [128, 8, 128], BF16)
        kTt = sb("kTt", [128, 8, 128], BF16)
        v_b = sb("v_b", [128, 8, 65], BF16)
        csT = sb("csT", [128, 12, 128], BF16)
        qkv_tm = sb("qkv_tm", [128, 12, 128])
        e_dec = sb("e_dec", [128, 4])
        e_last = sb("e_last", [128, 4])
        rs8 = sb("rs8", [128, 8])
        UG = sb("UG", [128, 4, 128])
        dtmp = sb("dtmp", [128, 4, 128])
        MQs = sb("MQs", [128, 4, 128])
        MQT = sb("MQT", [128, 4, 128])
        A_b = sb("A_b", [128, 4, 128], BF16)
        At_b = sb("At_b", [128, 4, 128], BF16)
        TTb = sb("TTb", [128, 4, 2, 128], BF16)
        TTf = sb("TTf", [128, 4, 2, 128])
        X_b = sb("X_b", [128, 4, 128], BF16)
        k_tmb = sb("k_tmb", [128, 4, 128], BF16)
        q_tm = sb("q_tm", [128, 4, 128], BF16)
        kdec = sb("kdec", [128, 4, 128], BF16)
        u_sb = sb("u_sb", [128, 4, 128])
        wT_b = sb("wT_b", [128, 4, 128], BF16)
        vnb = sb("vnb", [128, 4, 128], BF16)
        qkT_b = sb("qkT_b", [128, 4, 128], BF16)
        oi = sb("oi", [128, 4, 128])
        o_sb = sb("o_sb", [128, 4, 128])
        Sg = sb("Sg", [128, 4, 128])
        Sgb = sb("Sgb", [128, 4, 128], BF16)
        gm_b = sb("gm_b", [128, 512], BF16)

        O.mset("dve", q_pad[:], 0.0, ["q_pad"])
        O.mset("dve", k_pad[:], 0.0, ["k_pad"])
        O.mset("dve", v_b[:], 1.0, ["v_b"])

        def rsq(out, in_, scale, n, r, w):
            O.act(out, in_, AF.Ln, r + ["eps_t"], w, scale=scale, bias=eps_t[:n, 0:1])
            O.act(out, out, AF.Exp, w, w, scale=-0.5)

        def rope_ops(o1, o2, x1, x2, cos, sin, ta, tb, tc, td, r, w, tk):
            O.tt("pool", ta, x1, cos, ALU.mult, r, [tk + "a"])
            O.tt("pool", tb, x2, sin, ALU.mult, r, [tk + "b"])
            O.tt("pool", tc, x2, cos, ALU.mult, r, [tk + "c"])
            O.tt("pool", td, x1, sin, ALU.mult, r, [tk + "d"])
            O.tt("pool", o1, ta, tb, ALU.subtract, [tk + "a", tk + "b"], w)
            O.tt("pool", o2, tc, td, ALU.add, [tk + "c", tk + "d"], w)

        tvF = PSB[:, 0:1024].rearrange("p (a b) -> p a b", a=8)

        def kv_build(q, j, n, col0, pos0):
            O.ld("cs", cs_t[:n, :], c_cs[pos0:pos0 + n, :], [], ["cs_t"])
            O.cp("act", lat_b[:n, :], lat_n[:n, :], ["lat_n"], ["lat_b"])
            for kc in range(2):
                O.tr(tvF[:, kc, :n], lat_b[:n, kc * 128:(kc + 1) * 128], ident_b[:n, :n], ["lat_b", "ident_b"], ["b0"])
            O.cp("dve", latT[:, :, :n], tvF[:, 0:2, :n], ["b0"], ["latT"])
            for kc in range(2):
                O.mm(bank(3)[:n, :], latT[:, kc, :n], w_uk_b[:, kc, :], kc == 0, kc == 1, ["latT", "w_uk_b"], ["b3"])
            for kc in range(2):
                O.mm(bank(4)[:n, :], latT[:, kc, :n], w_uv_b[:, kc, :], kc == 0, kc == 1, ["latT", "w_uv_b"], ["b4"])
            yield
            O.cp("act", v_b[:n, :, 0:64], bank(4)[:n, :].rearrange("p (h d) -> p h d", h=8), ["b4"], ["v_b"])
            O.S.dma("sp", "vd", lambda e: e.dma_start(out=V_d[q, j, 0:n, :], in_=v_b[:n, :, :].rearrange("p h d -> p (h d)")),
                    ["v_b"], [])
            O.act(scF[:n, 0:512], bank(3)[:n, :], AF.Square, ["b3"], ["scF"])
            O.red(st1[:n, 0:8], scF[:n, 0:512].rearrange("p (h d) -> p h d", h=8), ["scF"], ["st_k"])
            O.act(junk[:n, 0:32], rope_raw[:n, :], AF.Square, ["rope_raw"], ["junk", "st_kr"], accum_out=st1[:n, 8:9])
            O.tsa("dve", st1[:n, 0:8], st1[:n, 0:8], st1[:n, 8:9], ["st_k", "st_kr"], ["st_k"])
            rsq(st1[:n, 0:8], st1[:n, 0:8], 1.0 / 96, n, ["st_k"], ["st_k"])
            yield
            kn3 = bank(3)[:n, :].rearrange("p (h d) -> p h d", h=8)
            O.tt("dve", rtmp[:n, :, :], kn3, st1[:n, 0:8].unsqueeze(2).to_broadcast([n, 8, 64]), ALU.mult,
                 ["b3", "st_k"], ["rtmp"])
            O.tt("pool", k_pad[:n, :, 0:64], rtmp[:n, :, :], gk[:n, 0:64].unsqueeze(1).to_broadcast([n, 8, 64]), ALU.mult,
                 ["rtmp", "gk"], ["k_pad"])
            O.tt("pool", rg[:n, :], rope_raw[:n, :], gk[:n, 64:96], ALU.mult, ["rope_raw", "gk"], ["rg"])
            rope_ops(rr[:n, 0:16], rr[:n, 16:32], rg[:n, 0:16], rg[:n, 16:32], cs_t[:n, 0:16], cs_t[:n, 16:32],
                     t4[:n, 0:16], t4[:n, 16:32], t4[:n, 32:48], t4[:n, 48:64], ["rg", "cs_t"], ["rr"], "t4")
            O.tt("dve", k_pad[:n, :, 64:96], rr[:n, :].unsqueeze(1).to_broadcast([n, 8, 32]),
                 st1[:n, 0:8].unsqueeze(2).to_broadcast([n, 8, 32]), ALU.mult, ["rr", "st_k"], ["k_pad"])
            yield
            for h in range(8):
                O.tr(tvF[:, h, :n], k_pad[:n, h, :], ident_b[:n, :n], ["k_pad", "ident_b"], ["b0"])
            O.cp("act", kTt[:, :, :n], tvF[:, :, :n], ["b0"], ["kTt"])
            O.S.dma("sp", "ktd", lambda e: e.dma_start(out=KT_d[q, :, :, col0:col0 + n], in_=kTt[:, :, :n]), ["kTt"], [])
            yield

        def cache_kv(q, j, n, row0):
            O.ld("lat", lat_n[:n, :], clat[row0:row0 + n, :], [], ["lat_n"])
            O.ld("rop", rope_raw[:n, :], crope[row0:row0 + n, :], [], ["rope_raw"])
            yield from kv_build(q, j, n, row0, row0)

        def load_x(c):
            O.ld("xt%d" % c["par"], xt2[c["par"]][:c["n"], :], c["x_src"], [], ["xt%d" % c["par"]])

        def front(c, nxt):
            q, j, qi, n, pos0 = c["q"], c["j"], c["qi"], c["n"], c["pos0"]
            need_out, first, last, is_sample, par = c["need_out"], c["first"], c["last"], c["is_sample"], c["par"]
            xt = xt2[par]
            xk, zk, sgk, hk = "xt%d" % par, "zTh", "sg%d" % par, "hs%d" % par
            hs, k_tm, kbg, vb, gkT, gqT = hs2[par], k_tm2[par], kbg2[par], vb2[par], gkT2[par], gqT2[par]
            beta, g_t, gc, ngc, e_gc, bge = (hs[:, 0:4], hs[:, 4:8], hs[:, 8:12], hs[:, 12:16], hs[:, 16:20], hs[:, 20:24])
            if c["load_self"]:
                load_x(c)
            if nxt is not None:
                load_x(nxt)
            O.act(junk[:n, 0:1024], xt[:n, :], AF.Square, [xk], ["junk", "st_x"], accum_out=st1[:n, 16:17])
            rsq(st1[:n, 16:17], st1[:n, 16:17], 1.0 / 1024, n, ["st_x"], ["st_x"])
            O.asc(xn[:n, :], xt[:n, :], st1[:n, 16:17], [xk, "st_x"], ["xn"])
            yield
            for kc in range(8):
                O.tr(tvF[:, kc, :n], xn[:n, kc * 128:(kc + 1) * 128], ident_b[:n, :n], ["xn", "ident_b"], ["b0"])
            O.cp("dve", xT[:, :, :n], tvF[:, :, :n], ["b0"], ["xT"])
            yield
            Z1a, Z1b, Z3, Z4 = bank(1), bank(2, 160), bank(3), bank(2, 8, 256)
            for (dst, c0, c1, key) in ((Z1a, 0, 512, "b1"), (Z1b, 512, 672, "b2"), (Z3, 2208, 2720, "b3"), (Z4, 2720, 2728, "b2x")):
                for kc in range(8):
                    O.mm(dst[:n, :], xT[:, kc, :n], w_in_b[:, kc, c0:c1], kc == 0, kc == 7, ["xT", "w_in_b"], [key])
            yield
            if need_out:
                sg = sg2[par]
                O.act(sg[:n, :], Z3[:n, :], AF.Exp, ["b3"], [sgk], scale=-1.0)
                O.tsa("dve", sg[:n, :], sg[:n, :], 1.0, [sgk], [sgk])
                O.recip(sg[:n, :], sg[:n, :], [sgk], [sgk])
                O.tt("dve", sg[:n, :], sg[:n, :], Z3[:n, :], ALU.mult, [sgk, "b3"], [sgk])
            if first:
                if is_sample:
                    ld_T(cst[:, 0:36], sconv.rearrange("w (c p) -> (w c) p", p=128), 36, "cst")
                    for w3 in range(3):
                        O.cp("dve", zTh[:, :, w3], cst[:, w3 * 12:(w3 + 1) * 12], ["cst"], [zk])
                else:
                    O.mset("dve", zTh[:, :, 0:3], 0.0, [zk])
            else:
                pn = c["prev_n"]
                O.cp("pool", zTh[:, :, 0:3], zTh[:, :, pn:pn + 3], [zk], [zk])
            yield
            for g in range(3):
                bk = 4 if g % 2 == 0 else 3
                zc = bank(bk)[:, 0:4 * n].rearrange("p (c t) -> p c t", c=4)
                for c4 in range(4):
                    cc = 4 * g + c4
                    for kc in range(8):
                        O.mm(zc[:, c4, :], w_in_b[:, kc, 672 + cc * 128: 672 + (cc + 1) * 128], xT[:, kc, :n], kc == 0, kc == 7,
                             ["xT", "w_in_b"], ["b%d" % bk])
                O.cp("act", zTh[:, 4 * g:4 * g + 4, 3:3 + n], zc[:, :, :], ["b%d" % bk], [zk])
                yield
            if last:
                for c3 in range(3):
                    bk = 3 if c3 % 2 == 0 else 4
                    for kc in range(8):
                        O.mm(bank(bk)[:3, :], xT[:, kc, n - 3:n], w_in_b[:, kc, 672 + c3 * 512: 672 + (c3 + 1) * 512],
                             kc == 0, kc == 7, ["xT", "w_in_b"], ["b%d" % bk])
                    O.cp("act", cst[:3, c3 * 512:(c3 + 1) * 512], bank(bk)[:3, :], ["b%d" % bk], ["cst"])
                conv_out = c["conv_out"]
                O.S.dma("sp", "cso", lambda e: e.dma_start(out=conv_out, in_=cst[:3, :]), ["cst"], [])
                yield
            if need_out:
                O.act(junk[:n, 0:384], Z1a[:n, 0:384], AF.Square, ["b1"], ["junk", "st_q"], accum_out=st1[:n, 17:18])
                rsq(st1[:n, 17:18], st1[:n, 17:18], 1.0 / 384, n, ["st_q"], ["st_q"])
                O.act(qln[:n, :], Z1a[:n, 0:384], AF.Copy, ["b1", "st_q"], ["qln"], scale=st1[:n, 17:18])
                for kc in range(3):
                    O.tr(tvF[:, kc, :n], qln[:n, kc * 128:(kc + 1) * 128], ident_b[:n, :n], ["qln", "ident_b"], ["b0"])
                O.cp("dve", qlT[:, :, :n], tvF[:, 0:3, :n], ["b0"], ["qlT"])
                qr = PS[:, 3 * 512: 3 * 512 + 768]
                for (c0, c1) in ((0, 512), (512, 768)):
                    for kc in range(3):
                        O.mm(qr[:n, c0:c1], qlT[:, kc, :n], w_uq_b[:, kc, c0:c1], kc == 0, kc == 2, ["qlT", "w_uq_b"], ["b3", "b4"])
                yield
                O.act(scF[:n, 0:768], qr[:n, :], AF.Square, ["b3", "b4"], ["scF"])
                O.red(st1[:n, 24:32], scF[:n, 0:768].rearrange("p (h d) -> p h d", h=8), ["scF"], ["st_qh"])
                rsq(st1[:n, 24:32], st1[:n, 24:32], 1.0 / 96, n, ["st_qh"], ["st_qh"])
                qr3 = qr[:n, :].rearrange("p (h d) -> p h d", h=8)
                O.tt("dve", qtmp[:n, :, :], qr3, st1[:n, 24:32].unsqueeze(2).to_broadcast([n, 8, 96]), ALU.mult,
                     ["b3", "b4", "st_qh"], ["qtmp"])
                O.tt("pool", qtmp[:n, :, :], qtmp[:n, :, :], gq[:n, :].unsqueeze(1).to_broadcast([n, 8, 96]), ALU.mult,
                     ["qtmp", "gq"], ["qtmp"])
                O.ld("cs", cs_t[:n, :], c_cs[pos0:pos0 + n, :], [], ["cs_t"])
                cosb = cs_t[:n, 0:16].unsqueeze(1).to_broadcast([n, 8, 16])
                sinb = cs_t[:n, 16:32].unsqueeze(1).to_broadcast([n, 8, 16])
                rope_ops(q_pad[:n, :, 64:80], q_pad[:n, :, 80:96], qtmp[:n, :, 64:80], qtmp[:n, :, 80:96], cosb, sinb,
                         rtmp[:n, :, 0:16], rtmp[:n, :, 16:32], rtmp[:n, :, 32:48], rtmp[:n, :, 48:64],
                         ["qtmp", "cs_t"], ["q_pad"], "rtmp")
                O.cp("dve", q_pad[:n, :, 0:64], qtmp[:n, :, 0:64], ["qtmp"], ["q_pad"])
                yield
                for h in range(8):
                    O.tr(tvF[:, h, :n], q_pad[:n, h, :], ident_b[:n, :n], ["q_pad", "ident_b"], ["b0"])
                O.cp("act", qT[:, :, :n], tvF[:, :, :n], ["b0"], ["qT"])
                O.S.dma("sp", "qtd", lambda e: e.dma_start(out=qT_d[q, qi, :, :, 0:n], in_=qT[:, :, :n]), ["qT"], [])
                yield
            kvl = PS[:, 512 + 384: 512 + 640]
            O.act(junk[:n, 0:256], kvl[:n, :], AF.Square, ["b1", "b2"], ["junk", "st_kv"], accum_out=st1[:n, 18:19])
            rsq(st1[:n, 18:19], st1[:n, 18:19], 1.0 / 256, n, ["st_kv"], ["st_kv"])
            O.stt("dve", lat_n[:n, :], kvl[:n, :], st1[:n, 18:19], gkv[:n, :], ALU.mult, ALU.mult,
                  ["b1", "b2", "st_kv", "gkv"], ["lat_n"])
            O.cp("act", rope_raw[:n, :], Z1b[:n, 128:160], ["b2"], ["rope_raw"])
            lat_out, rope_out = c["lat_out"], c["rope_out"]
            O.S.dma("sp", "lato", lambda e: e.dma_start(out=lat_out, in_=lat_n[:n, :]), ["lat_n"], [])
            O.S.dma("sp", "ropeo", lambda e: e.dma_start(out=rope_out, in_=rope_raw[:n, :]), ["rope_raw"], [])
            yield
            yield from kv_build(q, j, n, pos0, pos0)
            O.act(beta[:n, :], Z4[:n, 0:4], AF.Exp, ["b2x"], [hk], scale=-1.0)
            O.tsa("dve", beta[:n, :], beta[:n, :], 1.0, [hk], [hk])
            O.recip(beta[:n, :], beta[:n, :], [hk], [hk])
            O.tt("dve", g_t[:n, :], Z4[:n, 4:8], dtb[:n, :], ALU.add, ["b2x", "dtb"], [hk])
            O.act(g_t[:n, :], g_t[:n, :], AF.Exp, [hk], [hk])
            O.act(g_t[:n, :], g_t[:n, :], AF.Ln, [hk, "one_t"], [hk], bias=one_t[:n, 0:1])
            O.tt("dve", g_t[:n, :], g_t[:n, :], negA[:n, :], ALU.mult, [hk, "negA"], [hk])
            gcp = bank(2, 4, 300)
            O.mm(gcp[:n, :], U_f[:n, :n], g_t[:n, :], True, True, ["U_f", hk], ["b2y"])
            O.cp("dve", gc[:n, :], gcp[:n, :], ["b2y"], [hk])
            O.tsm("dve", ngc[:n, :], gc[:n, :], -1.0, [hk], [hk])
            O.act(e_gc[:n, :], gc[:n, :], AF.Exp, [hk], [hk])
            O.tt("dve", bge[:n, :], beta[:n, :], e_gc[:n, :], ALU.mult, [hk], [hk])
            yield
            g_lo = 0 if need_out else 1
            c_lo = 4 * g_lo
            for g in range(g_lo, 3):
                bk = 3 if g % 2 == 0 else 4
                cps = bank(bk)[:, 0:4 * n].rearrange("p (c t) -> p c t", c=4)
                for c4 in range(4):
                    cc = 4 * g + c4
                    for w4 in range(4):
                        O.mm(cps[:, c4, :], convd[:, w4 * 12 + cc, :], zTh[:, cc, w4:w4 + n], w4 == 0, w4 == 3,
                             ["convd", zk], ["b%d" % bk])
                sv = scB[:, 0:4 * n].rearrange("p (c t) -> p c t", c=4)
                O.act(sv, cps, AF.Exp, ["b%d" % bk], ["scB"], scale=-1.0)
                O.tsa("dve", sv, sv, 1.0, ["scB"], ["scB"])
                O.recip(sv, sv, ["scB"], ["scB"])
                O.tt("dve", csT[:, 4 * g:4 * g + 4, :n], cps, sv, ALU.mult, ["b%d" % bk, "scB"], ["csT"])
                yield
            tvg = PSB[:, 0:512].rearrange("p (a b) -> p a b", a=4)
            for g in range(g_lo, 3):
                for c4 in range(4):
                    O.tr(tvg[:n, c4, :], csT[:, 4 * g + c4, :n], ident_b[:, :], ["csT", "ident_b"], ["b0"])
                O.cp("act", qkv_tm[:n, 4 * g:4 * g + 4, :], tvg[:n, :, :], ["b0"], ["qkv_tm"])
            yield
            sc2v = sc2[:, :].rearrange("p (c t) -> p c t", c=8)
            O.act(sc2v[:n, c_lo:8, :], qkv_tm[:n, c_lo:8, :], AF.Square, ["qkv_tm"], ["sc2"])
            O.red(rs8[:n, c_lo:8], sc2v[:n, c_lo:8, :], ["sc2"], ["rs8"])
            O.act(rs8[:n, c_lo:8], rs8[:n, c_lo:8], AF.Ln, ["rs8", "eps_t"], ["rs8"], bias=eps_t[:n, 0:1])
            O.act(rs8[:n, c_lo:8], rs8[:n, c_lo:8], AF.Exp, ["rs8"], ["rs8"], scale=-0.5)
            if need_out:
                O.tsm("dve", rs8[:n, 0:4], rs8[:n, 0:4], 128.0 ** -0.5, ["rs8"], ["rs8"])
                O.tt("dve", q_tm[:n, :, :], qkv_tm[:n, 0:4, :], rs8[:n, 0:4].unsqueeze(2).to_broadcast([n, 4, 128]), ALU.mult,
                     ["qkv_tm", "rs8"], ["q_tm"])
            O.tt("dve", k_tm[:n, :, :], qkv_tm[:n, 4:8, :], rs8[:n, 4:8].unsqueeze(2).to_broadcast([n, 4, 128]), ALU.mult,
                 ["qkv_tm", "rs8"], ["k_tm%d" % par])
            O.tt("dve", k_tmb[:n, :, :], qkv_tm[:n, 4:8, :], rs8[:n, 4:8].unsqueeze(2).to_broadcast([n, 4, 128]), ALU.mult,
                 ["qkv_tm", "rs8"], ["k_tmb"])
            yield
            for h in range(4):
                O.tsm("dve", kbg[:n, h, :], k_tm[:n, h, :], bge[:n, h:h + 1], ["k_tm%d" % par, hk], ["kbg%d" % par])
                O.tsm("pool", vb[:n, h, :], qkv_tm[:n, 8 + h, :], beta[:n, h:h + 1], ["qkv_tm", hk], ["vb%d" % par])
            for h in range(4):
                O.tr(tvF[:, h, :n], k_tmb[:n, h, :], ident_b[:n, :n], ["k_tmb", "ident_b"], ["b0"])
            if need_out:
                for h in range(4):
                    O.tr(tvF[:, 4 + h, :n], q_tm[:n, h, :], ident_b[:n, :n], ["q_tm", "ident_b"], ["b0"])
                O.cp("act", gqT[:, :, :n], tvF[:, 4:8, :n], ["b0"], ["gqT%d" % par])
            O.cp("dve", gkT[:, :, :n], tvF[:, 0:4, :n], ["b0"], ["gkT%d" % par])
            yield

        tvB = PSB[:, 5 * 1024: 6 * 1024].rearrange("p (a b) -> p a b", a=8)

        def back(c):
            q, j, qi, n, pos0 = c["q"], c["j"], c["qi"], c["n"], c["pos0"]
            need_out, first, last, is_sample, par = c["need_out"], c["first"], c["last"], c["is_sample"], c["par"]
            sgk, hk = "sg%d" % par, "hs%d" % par
            hs, k_tm, kbg, vb, gkT, gqT = hs2[par], k_tm2[par], kbg2[par], vb2[par], gkT2[par], gqT2[par]
            beta, g_t, gc, ngc, e_gc, bge = (hs[:, 0:4], hs[:, 4:8], hs[:, 8:12], hs[:, 12:16], hs[:, 16:20], hs[:, 20:24])
            kk, kbk, vbk, gkk, gqk = "k_tm%d" % par, "kbg%d" % par, "vb%d" % par, "gkT%d" % par, "gqT%d" % par
            O.tt("dve", UG[:n, :, :n], U_f[:n, :n].unsqueeze(1).to_broadcast([n, 4, n]),
                 g_t[:n, :].unsqueeze(2).to_broadcast([n, 4, n]), ALU.mult, ["U_f", hk], ["UG"])
            Grow = bank(6)[:, 0:4 * n].rearrange("p (h t) -> p h t", h=4)
            O.mm(Grow, ones_f[:n, :], UG[:n, :, :n], True, True, ["ones_f", "UG"], ["b6"])
            yield
            O.tt("dve", e_dec[:n, :], Grow[:n, :, n - 1], gc[:n, :], ALU.subtract, ["b6", hk], ["e_dec"])
            O.act(e_dec[:n, :], e_dec[:n, :], AF.Exp, ["e_dec"], ["e_dec"])
            O.act(e_last[:, :], Grow[:, :, n - 1], AF.Exp, ["b6"], ["e_last"])
            yield
            for h in range(4):
                O.tt("dve", dtmp[:n, h, :n], Grow[:n, h, :], mLs[:n, :n], ALU.subtract, ["b6", "mLs"], ["dtmp"])
            for h in range(4):
                O.act(MQs[:n, h, :n], dtmp[:n, h, :n], AF.Exp, ["dtmp", hk], ["MQs"], bias=gc[:n, h:h + 1], scale=-1.0)
            yield
            if need_out:
                for h in range(4):
                    O.tt("dve", dtmp[:n, h, :n], Grow[:n, h, :], mU[:n, :n], ALU.add, ["b6", "mU"], ["dtmp"])
                for h in range(4):
                    O.act(MQT[:n, h, :n], dtmp[:n, h, :n], AF.Exp, ["dtmp", hk], ["MQT"], bias=ngc[:n, h:h + 1])
                yield
            for h in range(4):
                O.tsm("pool", kdec[:n, h, :], k_tm[:n, h, :], e_dec[:n, h:h + 1], [kk, "e_dec"], ["kdec"])
            Gp = bank(6)[:, 0:4 * n].rearrange("p (h t) -> p h t", h=4)
            for h in range(4):
                O.mm(Gp[:n, h, :], gkT[:, h, :n], gkT[:, h, :n], True, True, [gkk], ["b6"])
            yield
            for h in range(4):
                O.tsm("dve", dtmp[:n, h, :n], Gp[:n, h, :], beta[:n, h:h + 1], ["b6", hk], ["dtmp"])
            O.tt("dve", A_b[:n, :, :n], dtmp[:n, :, :n], MQs[:n, :, :n], ALU.mult, ["dtmp", "MQs"], ["A_b"])
            yield
            tv7 = PSB[:, 7 * 1024: 8 * 1024].rearrange("p (a b) -> p a b", a=8)
            for h in range(4):
                O.tr(tv7[:n, h, :n], A_b[:n, h, :n], ident_b[:n, :n], ["A_b", "ident_b"], ["b7"])
            O.cp("act", At_b[:n, :, :n], tv7[:n, 0:4, :n], ["b7"], ["At_b"])
            for s2 in range(2):
                O.cp("pool", TTb[:n, :, s2, :n], ident_f[:n, :n].unsqueeze(1).to_broadcast([n, 4, n]), ["ident_f"], ["TTb"])
                O.cp("pool", TTf[:n, :, s2, :n], ident_f[:n, :n].unsqueeze(1).to_broadcast([n, 4, n]), ["ident_f"], ["TTf"])
            yield
            Xp = bank(5)[:, 0:4 * n].rearrange("p (h t) -> p h t", h=4)
            Yp = PS[:, 6 * 512: 8 * 512].rearrange("p (h s t) -> p h s t", h=4, s=2)
            l = 0
            while (1 << l) < n:
                for h in range(4):
                    O.mm(Xp[:n, h, :], At_b[:n, h, :n], TTb[:n, h, 0, :n], True, True, ["At_b", "TTb"], ["b5"])
                for h in range(4):
                    O.tt("dve", X_b[:n, h, :n], Xp[:n, h, :], lvl[:n, l, :n], ALU.mult, ["b5", "lvl"], ["X_b"])
                yield
                for h in range(4):
                    O.mm(Yp[:n, h, 0, :n], TTb[:n, h, 1, :n], X_b[:n, h, :n], True, True, ["TTb", "X_b"], ["b6", "b7"])
                    O.mm(Yp[:n, h, 1, :n], X_b[:n, h, :n], TTb[:n, h, 1, :n], True, True, ["TTb", "X_b"], ["b6", "b7"])
                O.tt("dve", TTf[:n, :, :, :n], TTf[:n, :, :, :n], Yp[:n, :, :, :n], ALU.subtract, ["TTf", "b6", "b7"], ["TTf"])
                O.cp("act", TTb[:n, :, :, :n], TTf[:n, :, :, :n], ["TTf"], ["TTb"])
                l += 1
                yield
            up_ = bank(5)[:, :].rearrange("p (h t) -> p h t", h=4)
            wTp = bank(6)[:, 0:4 * n].rearrange("p (h t) -> p h t", h=4)
            for h in range(4):
                O.mm(up_[:n, h, :], TTb[:n, h, 1, :n], vb[:n, h, :], True, True, ["TTb", vbk], ["b5"])
            for h in range(4):
                O.mm(wTp[:, h, :], kbg[:n, h, :], TTb[:n, h, 1, :n], True, True, ["TTb", kbk], ["b6"])
            yield
            O.cp("act", u_sb[:n, :, :], up_[:n, :, :], ["b5"], ["u_sb"])
            O.cp("act", wT_b[:, :, :n], wTp[:, :, :], ["b6"], ["wT_b"])
            yield
            if first:
                if is_sample:
                    O.ld("sg", Sg[:, :, :], sgdn.rearrange("h k v -> k h v"), [], ["Sg"])
                else:
                    O.mset("pool", Sg[:, :, :], 0.0, ["Sg"])
                O.cp("act", Sgb[:, :, :], Sg[:, :, :], ["Sg"], ["Sgb"])
            wSp = bank(7)[:, :].rearrange("p (h t) -> p h t", h=4)
            qSp = bank(5)[:, :].rearrange("p (h t) -> p h t", h=4)
            qkTp = bank(6)[:, 0:4 * n].rearrange("p (h t) -> p h t", h=4)
            for h in range(4):
                O.mm(wSp[:n, h, :], wT_b[:, h, :n], Sgb[:, h, :], True, True, ["wT_b", "Sgb"], ["b7"])
            O.tt("dve", vnb[:n, :, :], u_sb[:n, :, :], wSp[:n, :, :], ALU.subtract, ["u_sb", "b7"], ["vnb"])
            yield
            if need_out:
                for h in range(4):
                    O.mm(qSp[:n, h, :], gqT[:, h, :n], Sgb[:, h, :], True, True, [gqk, "Sgb"], ["b5"])
                for h in range(4):
                    O.mm(qkTp[:n, h, :], gkT[:, h, :n], gqT[:, h, :n], True, True, [gkk, gqk], ["b6"])
                for h in range(4):
                    O.tsm("dve", oi[:n, h, :], qSp[:n, h, :], e_gc[:n, h:h + 1], ["b5", hk], ["oi"])
                O.tt("dve", qkT_b[:n, :, :n], qkTp[:n, :, :], MQT[:n, :, :n], ALU.mult, ["b6", "MQT"], ["qkT_b"])
                o2p = bank(7)[:, :].rearrange("p (h t) -> p h t", h=4)
                for h in range(4):
                    O.mm(o2p[:n, h, :], qkT_b[:n, h, :n], vnb[:n, h, :], True, True, ["qkT_b", "vnb"], ["b7"])
                O.tt("dve", o_sb[:n, :, :], oi[:n, :, :], o2p[:n, :, :], ALU.add, ["oi", "b7"], ["o_sb"])
            yield
            dSp = bank(5)[:, :].rearrange("p (h t) -> p h t", h=4)
            for h in range(4):
                O.mm(dSp[:, h, :], kdec[:n, h, :], vnb[:n, h, :], True, True, ["kdec", "vnb"], ["b5"])
            for h in range(4):
                O.tsm("dve", Sg[:, h, :], Sg[:, h, :], e_last[:, h:h + 1], ["Sg", "e_last"], ["Sg"])
            O.tt("dve", Sg[:, :, :], Sg[:, :, :], dSp[:, :, :], ALU.add, ["Sg", "b5"], ["Sg"])
            O.cp("act", Sgb[:, :, :], Sg[:, :, :], ["Sg"], ["Sgb"])
            if last:
                S_out = c["S_out"]
                O.S.dma("sp", "so", lambda e: e.dma_start(out=S_out.rearrange("h k v -> k h v"), in_=Sg[:, :, :]), ["Sg"], [])
            if need_out:
                sg = sg2[par]
                O.act(sc2[:n, 0:512], o_sb[:n, :, :].rearrange("p h d -> p (h d)"), AF.Square, ["o_sb"], ["sc2"])
                O.red(st1[:n, 40:44], sc2[:n, 0:512].rearrange("p (h d) -> p h d", h=4), ["sc2"], ["st_o"])
                rsq(st1[:n, 40:44], st1[:n, 40:44], 1.0 / 128, n, ["st_o"], ["st_o"])
                for h in range(4):
                    O.tsm("dve", o_sb[:n, h, :], o_sb[:n, h, :], st1[:n, 40 + h:41 + h], ["o_sb", "st_o"], ["o_sb"])
                O.tt("pool", o_sb[:n, :, :], o_sb[:n, :, :], go[:n, :].unsqueeze(1).to_broadcast([n, 4, 128]), ALU.mult,
                     ["o_sb", "go"], ["o_sb"])
                O.tt("dve", gm_b[:n, :], o_sb[:n, :, :].rearrange("p h d -> p (h d)"), sg[:n, :], ALU.mult, ["o_sb", sgk], ["gm_b"])
                O.S.dma("sp", "gmd", lambda e: e.dma_start(out=gm_d[q, qi, 0:n, :], in_=gm_b[:n, :]), ["gm_b"], [])
            yield

        jobs = []
        if _STOP != 'prep':
            for s in range(NP):
                jobs.append(dict(q=s, j=0, qi=0, n=16, pos0=0, x_src=meta, need_out=False, first=True, last=False, is_sample=False,
                                 lat_out=p_lat[s, 0:16, :], rope_out=p_rope[s, 0:16, :], conv_out=None, S_out=None, prev_n=0))
                for i in range(NFT):
                    jobs.append(dict(q=s, j=i + 1, qi=i, n=128, pos0=16 + 128 * i, x_src=xp[s, 128 * i:128 * (i + 1), :],
                                     need_out=True, first=False, last=(i == NFT - 1), is_sample=False,
                                     lat_out=p_lat[s, 16 + 128 * i:16 + 128 * (i + 1), :],
                                     rope_out=p_rope[s, 16 + 128 * i:16 + 128 * (i + 1), :],
                                     conv_out=p_conv[s], S_out=p_S[s], prev_n=(16 if i == 0 else 128)))
            jobs.append(dict(q=NP, j=CFT + 1, qi=0, n=64, pos0=LC, x_src=xs, need_out=True, first=True, last=True, is_sample=True,
                             lat_out=s_lat, rope_out=s_rope, conv_out=s_conv, S_out=s_S, prev_n=0))
        for k, c in enumerate(jobs):
            c["par"] = k % 2
            c["load_self"] = (k == 0)
        cache_jobs = [(NP, 0, 16, 0)] + [(NP, i + 1, 128, 16 + 128 * i) for i in range(CFT)]
        if _STOP == 'prep':
            cache_jobs = []

        def run_all(g):
            for _ in g:
                pass

        def interleave(ga, gb):
            da = db = False
            while not (da and db):
                if not da:
                    try:
                        next(ga)
                    except StopIteration:
                        da = True
                if not db:
                    try:
                        next(gb)
                    except StopIteration:
                        db = True

        def front_plus(k):
            yield from front(jobs[k], jobs[k + 1] if k + 1 < len(jobs) else None)
            if cache_jobs and (k % 2 == 1 or len(jobs) - k <= len(cache_jobs)):
                cj = cache_jobs.pop(0)
                yield from cache_kv(*cj)

        if jobs:
            run_all(front_plus(0))
            for k in range(len(jobs)):
                if k + 1 < len(jobs):
                    interleave(back(jobs[k]), front_plus(k + 1))
                else:
                    run_all(back(jobs[k]))
        while cache_jobs:
            run_all(cache_kv(*cache_jobs.pop(0)))
        S.emit()
    if _STOP in ('prep', '1a'):
        return nc

    nc.all_engine_barrier()

    with ExitStack() as st:
        def sb(name, shape, dt=F32):
            return st.enter_context(nc.sbuf_tensor(name, shape, dt))

        PS = st.enter_context(nc.psum_tensor("psb", [128, 4096], F32))
        PSB = PS[:].bitcast(BF16)

        def bank(b, w=512, off=0):
            return PS[:, b * 512 + off: b * 512 + off + w]

        S = Sched(nc, "b")
        O = Ops(S)
        KTc = sb("KTc", [128, 8, NCMAX], BF16)
        Vc = sb("Vc", [128, NKT, 520], BF16)
        w_out_b = sb("w_out_b", [128, 8, 1024], BF16)
        stg = sb("stgb", [128, 1024])
        gM = sb("gM", [128, 4])
        ident_f = sb("ident_fb", [128, 128])
        ident_b = sb("ident_bb", [128, 128], BF16)
        rowa = sb("rowa", [1, 128], BF16)
        rowb = sb("rowb", [1, 128], BF16)
        eps_t = sb("eps_tb", [128, 1])
        qTt2 = [sb("qTt%d" % i, [128, 8, 128], BF16) for i in range(2)]
        mix_b2 = [sb("mix_b%d" % i, [128, 1024], BF16) for i in range(2)]
        mixT = sb("mixT", [128, 8, 128], BF16)
        xt2 = [sb("xtb%d" % i, [128, 1024]) for i in range(2)]
        x1s2 = [sb("x1s%d" % i, [128, 1024]) for i in range(2)]
        P_sb = sb("P_sb", [128, 3, 4, 128], BF16)
        attn_tm = sb("attn_tm", [128, 512])
        junk = sb("junkb", [128, 512], BF16)
        st1 = sb("st1b", [128, 16])

        O.ld("c0", ident_f[:], c_ident, [], ["ident_f"])
        O.cp("dve", ident_b[:], ident_f[:], ["ident_f"], ["ident_b"])
        O.mset("dve", eps_t[:], EPS, ["eps_t"])
        O.mset("dve", rowa[:, :], 1.0, ["rowa"])
        O.mset("dve", rowa[:, 0:64], 0.0, ["rowa"])
        O.mset("dve", rowb[:, :], -30000.0, ["rowb"])
        O.mset("dve", rowb[:, 64:128], 0.0, ["rowb"])
        tmpT = sb("tmpTb", [8, 128])
        O.ld("ldT", tmpT[:4, :], mla_out_norm.rearrange("(k p) -> k p", p=128), [], ["tmpT"])
        O.tr(PS[:, 0:4], tmpT[:4, :], ident_f[:4, :4], ["tmpT", "ident_f"], ["bs0"])
        O.cp("dve", gM[:, :], PS[:, 0:4], ["bs0"], ["gM"])
        for kc in range(8):
            O.ld("stg", stg[:], w_out[kc * 128:(kc + 1) * 128, :], [], ["stg"])
            if kc < 4:
                O.tsm("dve", w_out_b[:, kc, :], stg[:], gM[:, kc:kc + 1], ["stg", "gM"], ["w_out_b"])
            else:
                O.cp("dve", w_out_b[:, kc, :], stg[:], ["stg"], ["w_out_b"])

        SBANK = (0, 1, 7)
        tcount = [0]

        def attn_seq(q, tiles):
            units = []
            for ti, (qi, n, x_src, keytiles, diag_j, x1_idx) in enumerate(tiles):
                groups = []
                cur = []
                for kt in keytiles:
                    if kt[1] != 128:
                        if cur:
                            groups.append(cur)
                            cur = []
                        groups.append([kt])
                    else:
                        cur.append(kt)
                        if len(cur) == 4:
                            groups.append(cur)
                            cur = []
                if cur:
                    groups.append(cur)
                for h in range(8):
                    for gi, g in enumerate(groups):
                        units.append((ti, h, g, gi == 0, gi == len(groups) - 1))
            tpar = {}

            def prologue(ti):
                qi, n, x_src, keytiles, diag_j, x1_idx = tiles[ti]
                p = tcount[0] % 2
                tcount[0] += 1
                tpar[ti] = p
                O.ld("qt%d" % p, qTt2[p][:, :, :n], qT_d[q, qi, :, :, 0:n], [], ["qTt%d" % p])
                O.ld("gm%d" % p, mix_b2[p][:n, 512:1024], gm_d[q, qi, 0:n, :], [], ["mix_g%d" % p])
                O.ld("xt%d" % p, xt2[p][:n, :], x_src, [], ["xt%d" % p])

            def qk(ui):
                ti, h, g, fg, lg = units[ui]
                qi, n, x_src, keytiles, diag_j, x1_idx = tiles[ti]
                if ti not in tpar:
                    prologue(ti)
                p = tpar[ti]
                par = ui % 3
                Sp = bank(SBANK[par])[:, :].rearrange("p (s t) -> p s t", s=4)
                for s_, (j, nk, col0) in enumerate(g):
                    dg = (j == diag_j)
                    O.mm(Sp[:nk, s_, :n], KTc[0:96, h, col0:col0 + nk], qTt2[p][0:96, h, :n], True, not dg,
                         ["KTc", "qTt%d" % p], ["bs%d" % par])
                    if dg:
                        O.mm(Sp[:nk, s_, :n], rowa[0:1, :nk], rowb[0:1, :n], False, True, ["rowa", "rowb"], ["bs%d" % par])

            def expv(ui):
                ti, h, g, fg, lg = units[ui]
                qi, n, x_src, keytiles, diag_j, x1_idx = tiles[ti]
                p = tpar[ti]
                par = ui % 3
                Sp = bank(SBANK[par])[:, :].rearrange("p (s t) -> p s t", s=4)
                Op = bank(2 + (h % 2))
                ok = "bo%d" % (h % 2)
                nk0 = g[0][1]
                O.act(P_sb[:nk0, par, 0:len(g), :n], Sp[:nk0, 0:len(g), :n], AF.Exp, ["bs%d" % par], ["P%d" % par], scale=ATT_SCALE)
                for s_, (j, nk, col0) in enumerate(g):
                    O.mm(Op[:n, 0:65], P_sb[:nk, par, s_, :n], Vc[:nk, j, h * 65:(h + 1) * 65], fg and s_ == 0,
                         lg and s_ == len(g) - 1, ["P%d" % par, "Vc"], [ok])
                if lg:
                    O.recip(st1[:n, h:h + 1], Op[:n, 64:65], [ok], ["st_r%d" % h])
                    O.tsm("dve", attn_tm[:n, h * 64:(h + 1) * 64], Op[:n, 0:64], st1[:n, h:h + 1], [ok, "st_r%d" % h], ["attn_tm"])
                    if h == 7:
                        epilogue(ti)

            def epilogue(ti):
                qi, n, x_src, keytiles, diag_j, x1_idx = tiles[ti]
                p = tpar[ti]
                mix_b = mix_b2[p]
                O.act(junk[:n, :], attn_tm[:n, :], AF.Square, ["attn_tm"], ["junk", "st_a"], accum_out=st1[:n, 8:9])
                O.act(st1[:n, 8:9], st1[:n, 8:9], AF.Ln, ["st_a", "eps_t"], ["st_a"], scale=1.0 / 512, bias=eps_t[:n, 0:1])
                O.act(st1[:n, 8:9], st1[:n, 8:9], AF.Exp, ["st_a"], ["st_a"], scale=-0.5)
                O.asc(mix_b[:n, 0:512], attn_tm[:n, :], st1[:n, 8:9], ["attn_tm", "st_a"], ["mix_a%d" % p])
                tv = PSB[:, 4 * 1024: 5 * 1024].rearrange("p (a b) -> p a b", a=8)
                for c in range(8):
                    O.tr(tv[:, c, :n], mix_b[:n, c * 128:(c + 1) * 128], ident_b[:n, :n],
                         ["mix_a%d" % p, "mix_g%d" % p, "ident_b"], ["b4"])
                O.cp("dve", mixT[:, :, :n], tv[:, :, :n], ["b4"], ["mixT"])
                x1p = PS[:, 5 * 512: 7 * 512]
                for half in range(2):
                    for c in range(8):
                        O.mm(x1p[:n, half * 512:(half + 1) * 512], mixT[:, c, :n], w_out_b[:, c, half * 512:(half + 1) * 512],
                             c == 0, c == 7, ["mixT", "w_out_b"], ["b56"])
                O.tt("dve", x1s2[p][:n, :], x1p[:n, :], xt2[p][:n, :], ALU.add, ["b56", "xt%d" % p], ["x1s%d" % p])
                O.S.dma("pool", "x1o%d" % p, lambda e: e.dma_start(out=x1_d[x1_idx, 0:n, :], in_=x1s2[p][:n, :]), ["x1s%d" % p], [])

            LOOK = 2
            for ui in range(min(LOOK, len(units))):
                qk(ui)
            for ui in range(len(units)):
                if ui + LOOK < len(units):
                    qk(ui + LOOK)
                expv(ui)

        def load_cache(q, ncols, kts):
            for h in range(8):
                O.ld("ktc", KTc[:, h, 0:ncols], KT_d[q, :, h, 0:ncols], [], ["KTc"])
            full = [j for (j, nk, c0) in kts if nk == 128]
            if full:
                for j0 in range(full[0], full[-1] + 1, 4):
                    j1 = min(j0 + 4, full[-1] + 1)
                    O.ld("vc", Vc[:, j0:j1, :], V_d[q, j0:j1, :, :].rearrange("t p c -> p t c"), [], ["Vc"])
            for (j, nk, c0) in kts:
                if nk != 128:
                    O.ld("vc", Vc[:nk, j, :], V_d[q, j, 0:nk, :], [], ["Vc"])

        for s in range(NP):
            load_cache(s, LP, [(0, 16, 0)] + [(jj + 1, 128, 16 + 128 * jj) for jj in range(NFT)])
            tl = []
            for i in range(NFT):
                kts = [(0, 16, 0)] + [(jj + 1, 128, 16 + 128 * jj) for jj in range(i + 1)]
                tl.append((i, 128, xp[s, 128 * i:128 * (i + 1), :], kts, i + 1, s * NFT + i))
            attn_seq(s, tl)
        kts = [(0, 16, 0)] + [(jj + 1, 128, 16 + 128 * jj) for jj in range(CFT)] + [(CFT + 1, 64, LC)]
        load_cache(NP, LC + 64, kts)
        attn_seq(NP, [(0, 64, xs, kts, -1, NP * NFT)])
        S.emit()
    if _STOP == '1b':
        return nc

    nc.all_engine_barrier()

    with ExitStack() as st:
        def sb(name, shape, dt=F32):
            return st.enter_context(nc.sbuf_tensor(name, shape, dt))

        PS = st.enter_context(nc.psum_tensor("psc", [128, 4096], F32))
        PSB = PS[:].bitcast(BF16)
        S = Sched(nc, "c")
        O = Ops(S)
        w_up_b = sb("w_up_b", [128, 8, 4096], BF16)
        w_dn_b = sb("w_dn_b", [128, 32, 1024], BF16)
        gP = sb("gP", [128, 8])
        ident_f = sb("ident_fc", [128, 128])
        ident_b = sb("ident_bc", [128, 128], BF16)
        eps_t = sb("eps_tc", [128, 1])
        x1t2 = [sb("x1t%d" % i, [128, 2, 1024]) for i in range(2)]
        hn2 = [sb("hn%d" % i, [128, 1024], BF16) for i in range(2)]
        hT2 = [sb("hT%d" % i, [128, 8, 256], BF16) for i in range(2)]
        rl = sb("rl", [128, 3, 512])
        upT = sb("upT", [128, 32, 256], BF16)
        ys2 = [sb("ys%d" % i, [128, 1024]) for i in range(2)]
        junk = sb("junkc", [128, 1024], BF16)
        st1 = sb("st1c", [128, 8])

        O.ld("c0", ident_f[:], c_ident, [], ["ident_f"])
        O.cp("dve", ident_b[:], ident_f[:], ["ident_f"], ["ident_b"])
        O.mset("dve", eps_t[:], EPS, ["eps_t"])
        tmpT = sb("tmpTc", [8, 128])
        O.ld("ldT", tmpT[:8, :], mlp_norm.rearrange("(k p) -> k p", p=128), [], ["tmpT"])
        O.tr(PS[:, 0:8], tmpT[:8, :], ident_f[:8, :8], ["tmpT", "ident_f"], ["b0"])
        O.cp("dve", gP[:, :], PS[:, 0:8], ["b0"], ["gP"])
        for kc in range(8):
            sp_ = kc % 2
            stg = x1t2[sp_][:, :, :].rearrange("p a d -> p (a d)")
            for hf in range(2):
                O.ld("stg%d" % sp_, stg[:, :], w_up[kc * 128:(kc + 1) * 128, hf * 2048:(hf + 1) * 2048], [], ["x1t%d" % sp_])
                O.tsm("dve", w_up_b[:, kc, hf * 2048:(hf + 1) * 2048], stg[:, :], gP[:, kc:kc + 1],
                      ["x1t%d" % sp_, "gP"], ["w_up_b"])
        for c2 in range(16):
            sp_ = c2 % 2
            O.ld("stg%d" % sp_, x1t2[sp_][:, :, :], w_down[c2 * 256:(c2 + 1) * 256, :].rearrange("(f p) d -> p f d", p=128),
                 [], ["x1t%d" % sp_])
            O.cp("dve", w_dn_b[:, 2 * c2:2 * c2 + 2, :], x1t2[sp_][:, :, :], ["x1t%d" % sp_], ["w_dn_b"])

        def mlp_front_a(bi, subs):
            p = bi % 2
            for si, (idx, n, y_out) in enumerate(subs):
                O.ld("x1%d" % p, x1t2[p][:n, si, :], x1_d[idx, 0:n, :], [], ["x1t%d" % p])
            for si, (idx, n, y_out) in enumerate(subs):
                O.act(junk[:n, :], x1t2[p][:n, si, :], AF.Square, ["x1t%d" % p], ["junk", "st"], accum_out=st1[:n, si:si + 1])
                O.act(st1[:n, si:si + 1], st1[:n, si:si + 1], AF.Ln, ["st", "eps_t"], ["st"], scale=1.0 / 1024, bias=eps_t[:n, 0:1])
                O.act(st1[:n, si:si + 1], st1[:n, si:si + 1], AF.Exp, ["st"], ["st"], scale=-0.5)
                O.asc(hn2[si][:n, :], x1t2[p][:n, si, :], st1[:n, si:si + 1], ["x1t%d" % p, "st"], ["hn%d" % si])

        def mlp_front_b(bi, subs):
            p = bi % 2
            for si, (idx, n, y_out) in enumerate(subs):
                tv = PSB[:, 0:1024].rearrange("p (a b) -> p a b", a=8)
                for kc in range(8):
                    O.tr(tv[:, kc, :n], hn2[si][:n, kc * 128:(kc + 1) * 128], ident_b[:n, :n], ["hn%d" % si, "ident_b"], ["b0"])
                O.cp("dve", hT2[p][:, :, si * 128:si * 128 + n], tv[:, :, :n], ["b0"], ["hT%d" % p])

        def mlp_body(bi, subs, mid=None):
            p = bi % 2
            nt = sum(n for (_, n, _) in subs) if len(subs) == 1 else 256
            UPB = (1, 2, 7)
            for fc in range(32):
                par = fc % 3
                reg = PS[:, UPB[par] * 512:UPB[par] * 512 + nt]
                rk = "bu%d" % par
                for kc in range(8):
                    O.mm(reg, w_up_b[:, kc, fc * 128:(fc + 1) * 128], hT2[p][:, kc, 0:nt], kc == 0, kc == 7,
                         ["w_up_b", "hT%d" % p], [rk])
                O.act(rl[:, par, 0:nt], reg, AF.Relu, [rk], ["rl%d" % par])
                O.tt("dve", upT[:, fc, 0:nt], rl[:, par, 0:nt], rl[:, par, 0:nt], ALU.mult,
                     ["rl%d" % par], ["upT"])
            if mid is not None:
                mid()
            for si, (idx, n, y_out) in enumerate(subs):
                yb = 5 if si == 0 else 3
                yp = PS[:, yb * 512: (yb + 2) * 512]
                yk = "by%d" % si
                for half in range(2):
                    for fc in range(32):
                        O.mm(yp[:n, half * 512:(half + 1) * 512], upT[:, fc, si * 128:si * 128 + n],
                             w_dn_b[:, fc, half * 512:(half + 1) * 512], fc == 0, fc == 31, ["upT", "w_dn_b"], [yk])
                ys = ys2[si]
                O.tt("dve", ys[:n, :], yp[:n, :], x1t2[p][:n, si, :], ALU.add, [yk, "x1t%d" % p], ["ys%d" % si])
                O.S.dma("pool", "yo%d" % si, lambda e, ys=ys, n=n, y_out=y_out: e.dma_start(out=y_out, in_=ys[:n, :]),
                        ["ys%d" % si], [])

        blocks = []
        flat = []
        for s in range(NP):
            for i in range(NFT):
                flat.append((s * NFT + i, 128, y_p[s, 128 * i:128 * (i + 1), :]))
        for i in range(0, len(flat), 2):
            blocks.append(flat[i:i + 2])
        blocks.append([(NP * NFT, 64, y_s)])
        mlp_front_a(0, blocks[0])
        mlp_front_b(0, blocks[0])
        for bi in range(len(blocks)):
            if bi + 1 < len(blocks):
                mlp_front_a(bi + 1, blocks[bi + 1])
                mlp_body(bi, blocks[bi], mid=lambda bi=bi: mlp_front_b(bi + 1, blocks[bi + 1]))
            else:
                mlp_body(bi, blocks[bi])
        S.emit()
    return nc


_NC_CACHE = {}


def run_cores(per_core_inputs, NP, SEQ, PAST):
    key = (NP, SEQ, PAST)
    if key not in _NC_CACHE:
        _NC_CACHE[key] = build_nc(NP, SEQ, PAST)
    nc = _NC_CACHE[key]
    res = run_bass_kernel_spmd(nc, per_core_inputs, core_ids=list(range(len(per_core_inputs))))
    return res.results


def make_core_inputs(c, NP, inputs, consts):
    f = lambda a: np.ascontiguousarray(np.asarray(a), dtype=np.float32)
    d = {
        "xp": f(inputs["x_prompt"][NP * c:NP * (c + 1)]),
        "meta": f(inputs["meta_tokens"]),
        "xs": f(inputs["x_sample"][c]),
        "clat": f(inputs["cache_kv_latent"][0, c]),
        "crope": f(inputs["cache_k_rope"][0, c]),
        "sgdn": f(inputs["state_gdn"][0, c]),
        "sconv": f(inputs["state_conv"][0, c]),
    }
    for k in ("w_in", "w_uq", "w_uk", "w_uv", "w_out", "w_up", "w_down", "attn_norm", "q_a_norm", "kv_a_norm", "q_norm",
              "k_norm", "mla_out_norm", "conv_w", "a_log", "dt_bias", "gdn_out_norm", "mlp_norm"):
        d[k] = f(inputs[k][0])
    d.update(consts)
    return d


def kernel(**inputs):
    NCORES = 8
    B, SEQ = inputs["x_prompt"].shape[:2]
    NP = B // NCORES
    PAST = inputs["cache_kv_latent"].shape[2] - 16
    LP = 16 + SEQ
    consts = host_consts(max(LP, 16 + PAST + 64))
    ins = [make_core_inputs(c, NP, inputs, consts) for c in range(NCORES)]
    res = run_cores(ins, NP, SEQ, PAST)
    cat = lambda k: np.concatenate([r[k] for r in res], axis=0)
    stk = lambda k: np.stack([r[k] for r in res], axis=0)
    y_p = cat("y_p")
    y_s = stk("y_s")
    return (y_p, y_s, cat("p_lat")[None], cat("p_rope")[None], cat("p_S")[None], cat("p_conv")[None],
            stk("s_lat")[None], stk("s_rope")[None], stk("s_S")[None], stk("s_conv")[None])
```

```python
import numpy as np
import ml_dtypes
from contextlib import ExitStack
import concourse.bass as bass
import concourse.mybir as mybir
from concourse.bass_utils import run_bass_kernel_spmd

F32 = mybir.dt.float32
BF16 = mybir.dt.bfloat16
AF = mybir.ActivationFunctionType
ALU = mybir.AluOpType
AX = mybir.AxisListType
EPS = 1e-6
NEG = -30000.0
ENGS = ("sp", "act", "dve", "pool", "pe")


class Sched:
    def __init__(self, nc, tag):
        self.nc = nc
        self.tag = tag
        self.lists = {e: [] for e in ENGS}
        self.last_writer = {}
        self.readers = {}
        self.dma_count = {}
        self.waited = {e: {} for e in ENGS}

    def _deps(self, eng, reads, writes):
        deps = []
        for k in reads:
            w = self.last_writer.get(k)
            if w is not None:
                deps.append((w, True))
        for k in writes:
            w = self.last_writer.get(k)
            if w is not None:
                deps.append((w, False))
            for t in self.readers.get(k, {}).values():
                deps.append((t, False))
        out = []
        wd = self.waited[eng]
        for t, raw in deps:
            if t[0] == "E" and t[1] == eng and eng == "pe":
                continue
            sid = (t[0], t[1])
            if wd.get(sid, -1) >= t[2]:
                continue
            wd[sid] = t[2]
            out.append(t)
        return out

    def _commit(self, token, stream, reads, writes):
        for k in reads:
            self.readers.setdefault(k, {})[stream] = token
        for k in writes:
            self.last_writer[k] = token
            self.readers[k] = {}

    def op(self, eng, fn, reads=(), writes=()):
        deps = self._deps(eng, reads, writes)
        idx = len(self.lists[eng])
        self.lists[eng].append({"fn": fn, "deps": deps, "flag": False, "dma": None})
        self._commit(("E", eng, idx), eng, reads, writes)

    def dma(self, eng, key, fn, reads=(), writes=(), n=1):
        deps = self._deps(eng, reads, writes)
        c = self.dma_count.get(key, 0) + 16 * n
        self.dma_count[key] = c
        self.lists[eng].append({"fn": fn, "deps": deps, "flag": False, "dma": key})
        self._commit(("D", key, c), "D" + key, reads, writes)

    def emit(self):
        nc = self.nc
        for e in ENGS:
            for rec in self.lists[e]:
                for t in rec["deps"]:
                    if t[0] == "E":
                        self.lists[t[1]][t[2]]["flag"] = True
        val = {}
        for e in ENGS:
            c = 0
            v = []
            for rec in self.lists[e]:
                if rec["flag"] and rec["dma"] is None:
                    c += 1
                v.append(c)
            val[e] = v
        with ExitStack() as st:
            esem = {e: st.enter_context(nc.semaphore(self.tag + "s_" + e)) for e in ENGS}
            dsem = {k: st.enter_context(nc.semaphore(self.tag + "d_" + k)) for k in self.dma_count}
            block = st.enter_context(nc.Block())
            final = dict(self.dma_count)

            def run(e, engine):
                for rec in self.lists[e]:
                    for t in rec["deps"]:
                        if t[0] == "E":
                            engine.wait_ge(esem[t[1]], val[t[1]][t[2]])
                        else:
                            engine.wait_ge(dsem[t[1]], t[2])
                    r = rec["fn"](engine)
                    if rec["dma"] is not None:
                        if not isinstance(r, (list, tuple)):
                            r = [r]
                        for ins in r:
                            ins.then_inc(dsem[rec["dma"]], 16)
                    elif rec["flag"]:
                        r.then_inc(esem[e], 1)
                if e == "sp":
                    for k, c in final.items():
                        engine.wait_ge(dsem[k], c)

            @block.sync
            def _(eng):
                run("sp", eng)

            @block.scalar
            def _(eng):
                run("act", eng)

            @block.vector
            def _(eng):
                run("dve", eng)

            @block.gpsimd
            def _(eng):
                run("pool", eng)

            @block.tensor
            def _(eng):
                run("pe", eng)


class Ops:
    def __init__(self, S):
        self.S = S

    def act(self, out, in_, func, r, w, **kw):
        self.S.op("act", lambda e: e.activation(out=out, in_=in_, func=func, **kw), r, w)

    def tt(self, eng, out, in0, in1, op, r, w):
        self.S.op(eng, lambda e: e.tensor_tensor(out=out, in0=in0, in1=in1, op=op), r, w)

    def stt(self, eng, out, in0, scalar, in1, op0, op1, r, w):
        self.S.op(eng, lambda e: e.scalar_tensor_tensor(out=out, in0=in0, scalar=scalar, in1=in1, op0=op0, op1=op1), r, w)

    def tsm(self, eng, out, in0, s1, r, w):
        self.S.op(eng, lambda e: e.tensor_scalar_mul(out=out, in0=in0, scalar1=s1), r, w)

    def asc(self, out, in_, sc, r, w):
        self.S.op("act", lambda e: e.activation(out=out, in_=in_, func=AF.Copy, scale=sc), r, w)

    def tsa(self, eng, out, in0, s1, r, w):
        self.S.op(eng, lambda e: e.tensor_scalar_add(out=out, in0=in0, scalar1=s1), r, w)

    def cp(self, eng, out, in_, r, w):
        if eng == "act":
            self.S.op("act", lambda e: e.copy(out=out, in_=in_), r, w)
        else:
            self.S.op(eng, lambda e: e.tensor_copy(out=out, in_=in_), r, w)

    def recip(self, out, in_, r, w):
        self.S.op("dve", lambda e: e.reciprocal(out=out, in_=in_), r, w)

    def red(self, out, in_, r, w):
        self.S.op("dve", lambda e: e.tensor_reduce(out=out, in_=in_, axis=AX.X, op=ALU.add), r, w)

    def mset(self, eng, ap, v, w):
        self.S.op(eng, lambda e: e.memset(ap, v), (), w)

    def mm(self, out, lhsT, rhs, start, stop, r, w):
        self.S.op("pe", lambda e: e.matmul(out, lhsT=lhsT, rhs=rhs, start=start, stop=stop), r, w)

    def tr(self, out, in_, ident, r, w):
        self.S.op("pe", lambda e: e.transpose(out=out, in_=in_, identity=ident), r, w)

    def ld(self, key, out, in_, r, w, slow=False):
        if slow:
            self.S.dma("sp", key, lambda e: e.dma_start(out=out, in_=in_, allow_slow_non_contiguous=True), r, w)
        else:
            self.S.dma("sp", key, lambda e: e.dma_start(out=out, in_=in_), r, w)


def host_consts(npos):
    c = {}
    c["c_ident"] = np.eye(128, dtype=np.float32)
    p = np.arange(128)[:, None]
    q = np.arange(128)[None, :]
    c["c_U"] = (p <= q).astype(np.float32)
    c["c_mLs"] = np.where(p > q, 0.0, NEG).astype(np.float32)
    c["c_mU"] = np.where(q >= p, 0.0, NEG).astype(np.float32)
    lv = np.zeros((128, 7, 128), np.float32)
    for l in range(7):
        b = 1 << l
        lv[:, l, :] = ((p // (2 * b) == q // (2 * b)) & (p % (2 * b) >= b) & (q % (2 * b) < b)).astype(np.float32)
    c["c_lvl"] = lv
    half = 16
    inv_freq = (np.float32(10000.0) ** (-np.arange(half, dtype=np.float32) / np.float32(half))).astype(np.float32)
    ang = np.arange(npos, dtype=np.float32)[:, None] * inv_freq[None, :]
    c["c_cs"] = np.concatenate([np.cos(ang), np.sin(ang)], axis=1).astype(np.float32)
    return c


import os
_STOP = os.environ.get('KSTOP', '')
_CUT = float(os.environ.get('KCUT', '99'))


def build_nc(NP, SEQ, PAST):
    NFT = SEQ // 128
    LP = 16 + SEQ
    CFT = PAST // 128
    LC = 16 + PAST
    NSEQ = NP + 1
    NCMAX = max(LP, LC + 64)
    NKT = max(NFT + 1, CFT + 2)
    NQT = max(NFT, 1)
    NX1 = NP * NFT + 1
    NPOS = max(LP, LC + 64)

    nc = bass.Bass("TRN2", target_bir_lowering=False)

    def din(name, shape):
        return nc.dram_tensor(name, shape, F32, kind="ExternalInput").ap()

    def dout(name, shape):
        return nc.dram_tensor(name, shape, F32, kind="ExternalOutput").ap()

    xp = din("xp", [NP, SEQ, 1024])
    meta = din("meta", [16, 1024])
    xs = din("xs", [64, 1024])
    clat = din("clat", [LC, 256])
    crope = din("crope", [LC, 32])
    sgdn = din("sgdn", [4, 128, 128])
    sconv = din("sconv", [3, 1536])
    w_in = din("w_in", [1024, 2728])
    w_uq = din("w_uq", [384, 768])
    w_uk = din("w_uk", [256, 512])
    w_uv = din("w_uv", [256, 512])
    w_out = din("w_out", [1024, 1024])
    w_up = din("w_up", [1024, 4096])
    w_down = din("w_down", [4096, 1024])
    attn_norm = din("attn_norm", [1024])
    q_a_norm = din("q_a_norm", [384])
    kv_a_norm = din("kv_a_norm", [256])
    q_norm = din("q_norm", [96])
    k_norm = din("k_norm", [96])
    mla_out_norm = din("mla_out_norm", [512])
    conv_w = din("conv_w", [4, 1536])
    a_log = din("a_log", [4])
    dt_bias = din("dt_bias", [4])
    gdn_out_norm = din("gdn_out_norm", [128])
    mlp_norm = din("mlp_norm", [1024])
    c_ident = din("c_ident", [128, 128])
    c_U = din("c_U", [128, 128])
    c_mLs = din("c_mLs", [128, 128])
    c_mU = din("c_mU", [128, 128])
    c_lvl = din("c_lvl", [128, 7, 128])
    c_cs = din("c_cs", [NPOS, 32])

    y_p = dout("y_p", [NP, SEQ, 1024])
    y_s = dout("y_s", [64, 1024])
    p_lat = dout("p_lat", [NP, LP, 256])
    p_rope = dout("p_rope", [NP, LP, 32])
    p_S = dout("p_S", [NP, 4, 128, 128])
    p_conv = dout("p_conv", [NP, 3, 1536])
    s_lat = dout("s_lat", [64, 256])
    s_rope = dout("s_rope", [64, 32])
    s_S = dout("s_S", [4, 128, 128])
    s_conv = dout("s_conv", [3, 1536])

    KT_d = nc.dram_tensor("KT_d", [NSEQ, 128, 8, NCMAX], BF16, kind="Internal").ap()
    V_d = nc.dram_tensor("V_d", [NSEQ, NKT, 128, 520], BF16, kind="Internal").ap()
    qT_d = nc.dram_tensor("qT_d", [NSEQ, NQT, 128, 8, 128], BF16, kind="Internal").ap()
    gm_d = nc.dram_tensor("gm_d", [NSEQ, NQT, 128, 512], BF16, kind="Internal").ap()
    x1_d = nc.dram_tensor("x1_d", [NX1, 128, 1024], F32, kind="Internal").ap()

    ATT_SCALE = 96.0 ** -0.5

    with ExitStack() as st:
        def sb(name, shape, dt=F32):
            return st.enter_context(nc.sbuf_tensor(name, shape, dt))

        PS = st.enter_context(nc.psum_tensor("ps", [128, 4096], F32))
        PSB = PS[:].bitcast(BF16)

        def bank(b, w=512, off=0):
            return PS[:, b * 512 + off: b * 512 + off + w]

        S = Sched(nc, "a")
        O = Ops(S)

        w_in_b = sb("w_in_b", [128, 8, 2728], BF16)
        w_uq_b = sb("w_uq_b", [128, 3, 768], BF16)
        w_uk_b = sb("w_uk_b", [128, 2, 512], BF16)
        w_uv_b = sb("w_uv_b", [128, 2, 512], BF16)
        convd = sb("convd", [128, 48, 128], BF16)
        stg = sb("stg", [128, 2728], F32)
        gA = sb("gA", [128, 8])
        gQ = sb("gQ", [128, 3])
        cw = sb("cw", [128, 4, 12])
        ident_f = sb("ident_f", [128, 128])
        ident_b = sb("ident_b", [128, 128], BF16)
        U_f = sb("U_f", [128, 128])
        ones_f = sb("ones_f", [128, 128])
        mLs = sb("mLs", [128, 128])
        mU = sb("mU", [128, 128])
        lvl = sb("lvl", [128, 7, 128])
        gkv = sb("gkv", [128, 256])
        gq = sb("gq", [128, 96])
        gk = sb("gk", [128, 96])
        go = sb("go", [128, 128])
        negA = sb("negA", [128, 4])
        dtb = sb("dtb", [128, 4])
        eps_t = sb("eps_t", [128, 1])
        one_t = sb("one_t", [128, 1])

        O.ld("c0", ident_f[:], c_ident, [], ["ident_f"])
        O.ld("c1", U_f[:], c_U, [], ["U_f"])
        O.ld("c2", mLs[:], c_mLs, [], ["mLs"])
        O.ld("c3", mU[:], c_mU, [], ["mU"])
        O.ld("c4", lvl[:], c_lvl, [], ["lvl"])
        O.ld("c5", gkv[:], kv_a_norm.partition_broadcast(128), [], ["gkv"])
        O.ld("c6", gq[:], q_norm.partition_broadcast(128), [], ["gq"])
        O.ld("c7", gk[:], k_norm.partition_broadcast(128), [], ["gk"])
        O.ld("c8", go[:], gdn_out_norm.partition_broadcast(128), [], ["go"])
        O.ld("c9", negA[:], a_log.partition_broadcast(128), [], ["negA"])
        O.ld("c10", dtb[:], dt_bias.partition_broadcast(128), [], ["dtb"])
        tmpT = sb("tmpT", [48, 128])

        def ld_T(dst, src2d, k, key):
            O.ld("ldT", tmpT[:k, :], src2d, [], ["tmpT"])
            O.tr(PS[:, 0:k], tmpT[:k, :], ident_f[:k, :k], ["tmpT", "ident_f"], ["b0"])
            O.cp("dve", dst, PS[:, 0:k], ["b0"], [key])

        ld_T(gA[:, :], attn_norm.rearrange("(k p) -> k p", p=128), 8, "gA")
        ld_T(gQ[:, :], q_a_norm.rearrange("(k p) -> k p", p=128), 3, "gQ")
        ld_T(cw[:, :, :].rearrange("p w c -> p (w c)"), conv_w.rearrange("w (c p) -> (w c) p", p=128), 48, "cw")
        O.cp("dve", ident_b[:], ident_f[:], ["ident_f"], ["ident_b"])
        O.mset("dve", ones_f[:], 1.0, ["ones_f"])
        O.mset("dve", eps_t[:], EPS, ["eps_t"])
        O.mset("dve", one_t[:], 1.0, ["one_t"])
        O.act(negA[:], negA[:], AF.Exp, ["negA"], ["negA"])
        O.tsm("dve", negA[:], negA[:], -1.0, ["negA"], ["negA"])
        for kc in range(8):
            O.ld("stg", stg[:], w_in[kc * 128:(kc + 1) * 128, :], [], ["stg"])
            O.tsm("dve", w_in_b[:, kc, :], stg[:], gA[:, kc:kc + 1], ["stg", "gA"], ["w_in_b"])
        for kc in range(3):
            O.ld("stg", stg[:, 0:768], w_uq[kc * 128:(kc + 1) * 128, :], [], ["stg"])
            O.tsm("dve", w_uq_b[:, kc, :], stg[:, 0:768], gQ[:, kc:kc + 1], ["stg", "gQ"], ["w_uq_b"])
        for kc in range(2):
            O.ld("stg", stg[:, 0:512], w_uk[kc * 128:(kc + 1) * 128, :], [], ["stg"])
            O.cp("dve", w_uk_b[:, kc, :], stg[:, 0:512], ["stg"], ["w_uk_b"])
            O.ld("stg", stg[:, 0:512], w_uv[kc * 128:(kc + 1) * 128, :], [], ["stg"])
            O.cp("dve", w_uv_b[:, kc, :], stg[:, 0:512], ["stg"], ["w_uv_b"])
        for w4 in range(4):
            for c in range(12):
                O.tsm("dve", convd[:, w4 * 12 + c, :], ident_f[:], cw[:, w4, c:c + 1],
                      ["ident_f", "cw"], ["convd"])

        xt2 = [sb("xt%d" % i, [128, 1024]) for i in range(2)]
        junk = sb("junk", [128, 1024], BF16)
        scF = sb("scF", [128, 768])
        scB = sb("scB", [128, 512])
        sc2 = sb("sc2", [128, 1024])
        st1 = sb("st1", [128, 64])
        xn = sb("xn", [128, 1024], BF16)
        xT = sb("xT", [128, 8, 128], BF16)
        zTh = sb("zTh", [128, 12, 131], BF16)
        sg2 = [sb("sg%d" % i, [128, 512]) for i in range(2)]
        hs2 = [sb("hs%d" % i, [128, 32]) for i in range(2)]
        k_tm2 = [sb("k_tm%d" % i, [128, 4, 128]) for i in range(2)]
        kbg2 = [sb("kbg%d" % i, [128, 4, 128], BF16) for i in range(2)]
        vb2 = [sb("vb%d" % i, [128, 4, 128], BF16) for i in range(2)]
        gkT2 = [sb("gkT%d" % i, [128, 4, 128], BF16) for i in range(2)]
        gqT2 = [sb("gqT%d" % i, [128, 4, 128], BF16) for i in range(2)]
        cst = sb("cst", [128, 1536])
        qln = sb("qln", [128, 384], BF16)
        qlT = sb("qlT", [128, 3, 128], BF16)
        qtmp = sb("qtmp", [128, 8, 96])
        rtmp = sb("rtmp", [128, 8, 64])
        q_pad = sb("q_pad", [128, 8, 128], BF16)
        qT = sb("qT", [128, 8, 128], BF16)
        lat_n = sb("lat_n", [128, 256])
        lat_b = sb("lat_b", [128, 256], BF16)
        latT = sb("latT", [128, 2, 128], BF16)
        rope_raw = sb("rope_raw", [128, 32])
        cs_t = sb("cs_t", [128, 32])
        rg = sb("rg", [128, 32])
        rr = sb("rr", [128, 32])
        t4 = sb("t4", [128, 64])
        k_pad = sb("k_pad", [128, 8, 128], BF16)
        kTt = sb("kTt", [128, 8, 128], BF16)
        v_b = sb("v_b", [128, 8, 65], BF16)
        csT = sb("csT", [128, 12, 128], BF16)
        qkv_tm = sb("qkv_tm", [128, 12, 128])
        e_dec = sb("e_dec", [128, 4])
        e_last = sb("e_last", [128, 4])
        rs8 = sb("rs8", [128, 8])
        UG = sb("UG", [128, 4, 128])
        dtmp = sb("dtmp", [128, 4, 128])
        MQs = sb("MQs", [128, 4, 128])
        MQT = sb("MQT", [128, 4, 128])
        A_b = sb("A_b", [128, 4, 128], BF16)
        At_b = sb("At_b", [128, 4, 128], BF16)
        TTb = sb("TTb", [128, 4, 2, 128], BF16)
        TTf = sb("TTf", [128, 4, 2, 128])
        X_b = sb("X_b", [128, 4, 128], BF16)
        k_tmb = sb("k_tmb", [128, 4, 128], BF16)
        q_tm = sb("q_tm", [128, 4, 128], BF16)
        kdec = sb("kdec", [128, 4, 128], BF16)
        u_sb = sb("u_sb", [128, 4, 128])
        wT_b = sb("wT_b", [128, 4, 128], BF16)
        vnb = sb("vnb", [128, 4, 128], BF16)
        qkT_b = sb("qkT_b", [128, 4, 128], BF16)
        oi = sb("oi", [128, 4, 128])
        o_sb = sb("o_sb", [128, 4, 128])
        Sg = sb("Sg", [128, 4, 128])
        Sgb = sb("Sgb", [128, 4, 128], BF16)
        gm_b = sb("gm_b", [128, 512], BF16)

        O.mset("dve", q_pad[:], 0.0, ["q_pad"])
        O.mset("dve", k_pad[:], 0.0, ["k_pad"])
        O.mset("dve", v_b[:], 1.0, ["v_b"])

        def rsq(out, in_, scale, n, r, w):
            O.act(out, in_, AF.Ln, r + ["eps_t"], w, scale=scale, bias=eps_t[:n, 0:1])
            O.act(out, out, AF.Exp, w, w, scale=-0.5)

        def rope_ops(o1, o2, x1, x2, cos, sin, ta, tb, tc, td, r, w, tk):
            O.tt("pool", ta, x1, cos, ALU.mult, r, [tk + "a"])
            O.tt("pool", tb, x2, sin, ALU.mult, r, [tk + "b"])
            O.tt("pool", tc, x2, cos, ALU.mult, r, [tk + "c"])
            O.tt("pool", td, x1, sin, ALU.mult, r, [tk + "d"])
            O.tt("pool", o1, ta, tb, ALU.subtract, [tk + "a", tk + "b"], w)
            O.tt("pool", o2, tc, td, ALU.add, [tk + "c", tk + "d"], w)

        tvF = PSB[:, 0:1024].rearrange("p (a b) -> p a b", a=8)

        def kv_build(q, j, n, col0, pos0):
            O.ld("cs", cs_t[:n, :], c_cs[pos0:pos0 + n, :], [], ["cs_t"])
            O.cp("act", lat_b[:n, :], lat_n[:n, :], ["lat_n"], ["lat_b"])
            for kc in range(2):
                O.tr(tvF[:, kc, :n], lat_b[:n, kc * 128:(kc + 1) * 128], ident_b[:n, :n], ["lat_b", "ident_b"], ["b0"])
            O.cp("dve", latT[:, :, :n], tvF[:, 0:2, :n], ["b0"], ["latT"])
            for kc in range(2):
                O.mm(bank(3)[:n, :], latT[:, kc, :n], w_uk_b[:, kc, :], kc == 0, kc == 1, ["latT", "w_uk_b"], ["b3"])
            for kc in range(2):
                O.mm(bank(4)[:n, :], latT[:, kc, :n], w_uv_b[:, kc, :], kc == 0, kc == 1, ["latT", "w_uv_b"], ["b4"])
            yield
            O.cp("act", v_b[:n, :, 0:64], bank(4)[:n, :].rearrange("p (h d) -> p h d", h=8), ["b4"], ["v_b"])
            O.S.dma("sp", "vd", lambda e: e.dma_start(out=V_d[q, j, 0:n, :], in_=v_b[:n, :, :].rearrange("p h d -> p (h d)")),
                    ["v_b"], [])
            O.act(scF[:n, 0:512], bank(3)[:n, :], AF.Square, ["b3"], ["scF"])
            O.red(st1[:n, 0:8], scF[:n, 0:512].rearrange("p (h d) -> p h d", h=8), ["scF"], ["st_k"])
            O.act(junk[:n, 0:32], rope_raw[:n, :], AF.Square, ["rope_raw"], ["junk", "st_kr"], accum_out=st1[:n, 8:9])
            O.tsa("dve", st1[:n, 0:8], st1[:n, 0:8], st1[:n, 8:9], ["st_k", "st_kr"], ["st_k"])
            rsq(st1[:n, 0:8], st1[:n, 0:8], 1.0 / 96, n, ["st_k"], ["st_k"])
            yield
            kn3 = bank(3)[:n, :].rearrange("p (h d) -> p h d", h=8)
            O.tt("dve", rtmp[:n, :, :], kn3, st1[:n, 0:8].unsqueeze(2).to_broadcast([n, 8, 64]), ALU.mult,
                 ["b3", "st_k"], ["rtmp"])
            O.tt("dve", k_pad[:n, :, 0:64], gk[:n, 0:64].unsqueeze(1).to_broadcast([n, 8, 64]), rtmp[:n, :, :], ALU.mult,
                 ["rtmp", "gk"], ["k_pad"])
            O.tt("pool", rg[:n, :], rope_raw[:n, :], gk[:n, 64:96], ALU.mult, ["rope_raw", "gk"], ["rg"])
            rope_ops(rr[:n, 0:16], rr[:n, 16:32], rg[:n, 0:16], rg[:n, 16:32], cs_t[:n, 0:16], cs_t[:n, 16:32],
                     t4[:n, 0:16], t4[:n, 16:32], t4[:n, 32:48], t4[:n, 48:64], ["rg", "cs_t"], ["rr"], "t4")
            O.tt("dve", k_pad[:n, :, 64:96], rr[:n, :].unsqueeze(1).to_broadcast([n, 8, 32]),
                 st1[:n, 0:8].unsqueeze(2).to_broadcast([n, 8, 32]), ALU.mult, ["rr", "st_k"], ["k_pad"])
            yield
            for h in range(8):
                O.tr(tvF[:, h, :n], k_pad[:n, h, :], ident_b[:n, :n], ["k_pad", "ident_b"], ["b0"])
            O.cp("act", kTt[:, :, :n], tvF[:, :, :n], ["b0"], ["kTt"])
            O.S.dma("sp", "ktd", lambda e: e.dma_start(out=KT_d[q, :, :, col0:col0 + n], in_=kTt[:, :, :n]), ["kTt"], [])
            yield

        def cache_kv(q, j, n, row0):
            O.ld("lat", lat_n[:n, :], clat[row0:row0 + n, :], [], ["lat_n"])
            O.ld("rop", rope_raw[:n, :], crope[row0:row0 + n, :], [], ["rope_raw"])
            yield from kv_build(q, j, n, row0, row0)

        def load_x(c):
            O.ld("xt%d" % c["par"], xt2[c["par"]][:c["n"], :], c["x_src"], [], ["xt%d" % c["par"]])

        def front(c, nxt):
            q, j, qi, n, pos0 = c["q"], c["j"], c["qi"], c["n"], c["pos0"]
            need_out, first, last, is_sample, par = c["need_out"], c["first"], c["last"], c["is_sample"], c["par"]
            xt = xt2[par]
            xk, zk, sgk, hk = "xt%d" % par, "zTh", "sg%d" % par, "hs%d" % par
            hs, k_tm, kbg, vb, gkT, gqT = hs2[par], k_tm2[par], kbg2[par], vb2[par], gkT2[par], gqT2[par]
            beta, g_t, gc, ngc, e_gc, bge = (hs[:, 0:4], hs[:, 4:8], hs[:, 8:12], hs[:, 12:16], hs[:, 16:20], hs[:, 20:24])
            if c["load_self"]:
                load_x(c)
            if nxt is not None:
                load_x(nxt)
            O.act(junk[:n, 0:1024], xt[:n, :], AF.Square, [xk], ["junk", "st_x"], accum_out=st1[:n, 16:17])
            rsq(st1[:n, 16:17], st1[:n, 16:17], 1.0 / 1024, n, ["st_x"], ["st_x"])
            O.asc(xn[:n, :], xt[:n, :], st1[:n, 16:17], [xk, "st_x"], ["xn"])
            yield
            for kc in range(8):
                O.tr(tvF[:, kc, :n], xn[:n, kc * 128:(kc + 1) * 128], ident_b[:n, :n], ["xn", "ident_b"], ["b0"])
            O.cp("dve", xT[:, :, :n], tvF[:, :, :n], ["b0"], ["xT"])
            yield
            Z1a, Z1b, Z3, Z4 = bank(1), bank(2, 160), bank(3), bank(2, 8, 256)
            for (dst, c0, c1, key) in ((Z1a, 0, 512, "b1"), (Z1b, 512, 672, "b2"), (Z3, 2208, 2720, "b3"), (Z4, 2720, 2728, "b2x")):
                for kc in range(8):
                    O.mm(dst[:n, :], xT[:, kc, :n], w_in_b[:, kc, c0:c1], kc == 0, kc == 7, ["xT", "w_in_b"], [key])
            yield
            if need_out:
                sg = sg2[par]
                O.act(sg[:n, :], Z3[:n, :], AF.Exp, ["b3"], [sgk], scale=-1.0)
                O.tsa("dve", sg[:n, :], sg[:n, :], 1.0, [sgk], [sgk])
                O.recip(sg[:n, :], sg[:n, :], [sgk], [sgk])
                O.tt("dve", sg[:n, :], sg[:n, :], Z3[:n, :], ALU.mult, [sgk, "b3"], [sgk])
            if first:
                if is_sample:
                    ld_T(cst[:, 0:36], sconv.rearrange("w (c p) -> (w c) p", p=128), 36, "cst")
                    for w3 in range(3):
                        O.cp("dve", zTh[:, :, w3], cst[:, w3 * 12:(w3 + 1) * 12], ["cst"], [zk])
                else:
                    O.mset("dve", zTh[:, :, 0:3], 0.0, [zk])
            else:
                pn = c["prev_n"]
                O.cp("pool", zTh[:, :, 0:3], zTh[:, :, pn:pn + 3], [zk], [zk])
            yield
            for g in range(3):
                bk = 4 if g % 2 == 0 else 3
                zc = bank(bk)[:, 0:4 * n].rearrange("p (c t) -> p c t", c=4)
                for c4 in range(4):
                    cc = 4 * g + c4
                    for kc in range(8):
                        O.mm(zc[:, c4, :], w_in_b[:, kc, 672 + cc * 128: 672 + (cc + 1) * 128], xT[:, kc, :n], kc == 0, kc == 7,
                             ["xT", "w_in_b"], ["b%d" % bk])
                O.cp("act", zTh[:, 4 * g:4 * g + 4, 3:3 + n], zc[:, :, :], ["b%d" % bk], [zk])
                yield
            if last:
                for c3 in range(3):
                    bk = 3 if c3 % 2 == 0 else 4
                    for kc in range(8):
                        O.mm(bank(bk)[:3, :], xT[:, kc, n - 3:n], w_in_b[:, kc, 672 + c3 * 512: 672 + (c3 + 1) * 512],
                             kc == 0, kc == 7, ["xT", "w_in_b"], ["b%d" % bk])
                    O.cp("act", cst[:3, c3 * 512:(c3 + 1) * 512], bank(bk)[:3, :], ["b%d" % bk], ["cst"])
                conv_out = c["conv_out"]
                O.S.dma("sp", "cso", lambda e: e.dma_start(out=conv_out, in_=cst[:3, :]), ["cst"], [])
                yield
            if need_out:
                O.act(junk[:n, 0:384], Z1a[:n, 0:384], AF.Square, ["b1"], ["junk", "st_q"], accum_out=st1[:n, 17:18])
                rsq(st1[:n, 17:18], st1[:n, 17:18], 1.0 / 384, n, ["st_q"], ["st_q"])
                O.act(qln[:n, :], Z1a[:n, 0:384], AF.Copy, ["b1", "st_q"], ["qln"], scale=st1[:n, 17:18])
                for kc in range(3):
                    O.tr(tvF[:, kc, :n], qln[:n, kc * 128:(kc + 1) * 128], ident_b[:n, :n], ["qln", "ident_b"], ["b0"])
                O.cp("dve", qlT[:, :, :n], tvF[:, 0:3, :n], ["b0"], ["qlT"])
                qr = PS[:, 3 * 512: 3 * 512 + 768]
                for (c0, c1) in ((0, 512), (512, 768)):
                    for kc in range(3):
                        O.mm(qr[:n, c0:c1], qlT[:, kc, :n], w_uq_b[:, kc, c0:c1], kc == 0, kc == 2, ["qlT", "w_uq_b"], ["b3", "b4"])
                yield
                O.act(scF[:n, 0:768], qr[:n, :], AF.Square, ["b3", "b4"], ["scF"])
                O.red(st1[:n, 24:32], scF[:n, 0:768].rearrange("p (h d) -> p h d", h=8), ["scF"], ["st_qh"])
                rsq(st1[:n, 24:32], st1[:n, 24:32], 1.0 / 96, n, ["st_qh"], ["st_qh"])
                qr3 = qr[:n, :].rearrange("p (h d) -> p h d", h=8)
                O.tt("dve", qtmp[:n, :, :], qr3, st1[:n, 24:32].unsqueeze(2).to_broadcast([n, 8, 96]), ALU.mult,
                     ["b3", "b4", "st_qh"], ["qtmp"])
                O.tt("dve", qtmp[:n, :, :], gq[:n, :].unsqueeze(1).to_broadcast([n, 8, 96]), qtmp[:n, :, :], ALU.mult,
                     ["qtmp", "gq"], ["qtmp"])
                O.ld("cs", cs_t[:n, :], c_cs[pos0:pos0 + n, :], [], ["cs_t"])
                cosb = cs_t[:n, 0:16].unsqueeze(1).to_broadcast([n, 8, 16])
                sinb = cs_t[:n, 16:32].unsqueeze(1).to_broadcast([n, 8, 16])
                rope_ops(q_pad[:n, :, 64:80], q_pad[:n, :, 80:96], qtmp[:n, :, 64:80], qtmp[:n, :, 80:96], cosb, sinb,
                         rtmp[:n, :, 0:16], rtmp[:n, :, 16:32], rtmp[:n, :, 32:48], rtmp[:n, :, 48:64],
                         ["qtmp", "cs_t"], ["q_pad"], "rtmp")
                O.cp("dve", q_pad[:n, :, 0:64], qtmp[:n, :, 0:64], ["qtmp"], ["q_pad"])
                yield
                for h in range(8):
                    O.tr(tvF[:, h, :n], q_pad[:n, h, :], ident_b[:n, :n], ["q_pad", "ident_b"], ["b0"])
                O.cp("act", qT[:, :, :n], tvF[:, :, :n], ["b0"], ["qT"])
                O.S.dma("sp", "qtd", lambda e: e.dma_start(out=qT_d[q, qi, :, :, 0:n], in_=qT[:, :, :n]), ["qT"], [])
                yield
            kvl = PS[:, 512 + 384: 512 + 640]
            O.act(junk[:n, 0:256], kvl[:n, :], AF.Square, ["b1", "b2"], ["junk", "st_kv"], accum_out=st1[:n, 18:19])
            rsq(st1[:n, 18:19], st1[:n, 18:19], 1.0 / 256, n, ["st_kv"], ["st_kv"])
            O.stt("dve", lat_n[:n, :], kvl[:n, :], st1[:n, 18:19], gkv[:n, :], ALU.mult, ALU.mult,
                  ["b1", "b2", "st_kv", "gkv"], ["lat_n"])
            O.cp("act", rope_raw[:n, :], Z1b[:n, 128:160], ["b2"], ["rope_raw"])
            lat_out, rope_out = c["lat_out"], c["rope_out"]
            O.S.dma("sp", "lato", lambda e: e.dma_start(out=lat_out, in_=lat_n[:n, :]), ["lat_n"], [])
            O.S.dma("sp", "ropeo", lambda e: e.dma_start(out=rope_out, in_=rope_raw[:n, :]), ["rope_raw"], [])
            yield
            yield from kv_build(q, j, n, pos0, pos0)
            O.act(beta[:n, :], Z4[:n, 0:4], AF.Exp, ["b2x"], [hk], scale=-1.0)
            O.tsa("dve", beta[:n, :], beta[:n, :], 1.0, [hk], [hk])
            O.recip(beta[:n, :], beta[:n, :], [hk], [hk])
            O.tt("dve", g_t[:n, :], Z4[:n, 4:8], dtb[:n, :], ALU.add, ["b2x", "dtb"], [hk])
            O.act(g_t[:n, :], g_t[:n, :], AF.Exp, [hk], [hk])
            O.act(g_t[:n, :], g_t[:n, :], AF.Ln, [hk, "one_t"], [hk], bias=one_t[:n, 0:1])
            O.tt("dve", g_t[:n, :], g_t[:n, :], negA[:n, :], ALU.mult, [hk, "negA"], [hk])
            gcp = bank(2, 4, 300)
            O.mm(gcp[:n, :], U_f[:n, :n], g_t[:n, :], True, True, ["U_f", hk], ["b2y"])
            O.cp("dve", gc[:n, :], gcp[:n, :], ["b2y"], [hk])
            O.tsm("dve", ngc[:n, :], gc[:n, :], -1.0, [hk], [hk])
            O.act(e_gc[:n, :], gc[:n, :], AF.Exp, [hk], [hk])
            O.tt("dve", bge[:n, :], beta[:n, :], e_gc[:n, :], ALU.mult, [hk], [hk])
            yield
            g_lo = 0 if need_out else 1
            c_lo = 4 * g_lo
            for g in range(g_lo, 3):
                bk = 3 if g % 2 == 0 else 4
                cps = bank(bk)[:, 0:4 * n].rearrange("p (c t) -> p c t", c=4)
                for c4 in range(4):
                    cc = 4 * g + c4
                    for w4 in range(4):
                        O.mm(cps[:, c4, :], convd[:, w4 * 12 + cc, :], zTh[:, cc, w4:w4 + n], w4 == 0, w4 == 3,
                             ["convd", zk], ["b%d" % bk])
                sv = scB[:, 0:4 * n].rearrange("p (c t) -> p c t", c=4)
                O.act(sv, cps, AF.Exp, ["b%d" % bk], ["scB"], scale=-1.0)
                O.tsa("dve", sv, sv, 1.0, ["scB"], ["scB"])
                O.recip(sv, sv, ["scB"], ["scB"])
                O.tt("dve", csT[:, 4 * g:4 * g + 4, :n], cps, sv, ALU.mult, ["b%d" % bk, "scB"], ["csT"])
                yield
            tvg = PSB[:, 0:512].rearrange("p (a b) -> p a b", a=4)
            for g in range(g_lo, 3):
                for c4 in range(4):
                    O.tr(tvg[:n, c4, :], csT[:, 4 * g + c4, :n], ident_b[:, :], ["csT", "ident_b"], ["b0"])
                O.cp("act", qkv_tm[:n, 4 * g:4 * g + 4, :], tvg[:n, :, :], ["b0"], ["qkv_tm"])
            yield
            sc2v = sc2[:, :].rearrange("p (c t) -> p c t", c=8)
            O.act(sc2v[:n, c_lo:8, :], qkv_tm[:n, c_lo:8, :], AF.Square, ["qkv_tm"], ["sc2"])
            O.red(rs8[:n, c_lo:8], sc2v[:n, c_lo:8, :], ["sc2"], ["rs8"])
            O.act(rs8[:n, c_lo:8], rs8[:n, c_lo:8], AF.Ln, ["rs8", "eps_t"], ["rs8"], bias=eps_t[:n, 0:1])
            O.act(rs8[:n, c_lo:8], rs8[:n, c_lo:8], AF.Exp, ["rs8"], ["rs8"], scale=-0.5)
            if need_out:
                O.tsm("dve", rs8[:n, 0:4], rs8[:n, 0:4], 128.0 ** -0.5, ["rs8"], ["rs8"])
                O.tt("dve", q_tm[:n, :, :], qkv_tm[:n, 0:4, :], rs8[:n, 0:4].unsqueeze(2).to_broadcast([n, 4, 128]), ALU.mult,
                     ["qkv_tm", "rs8"], ["q_tm"])
            O.tt("dve", k_tm[:n, :, :], qkv_tm[:n, 4:8, :], rs8[:n, 4:8].unsqueeze(2).to_broadcast([n, 4, 128]), ALU.mult,
                 ["qkv_tm", "rs8"], ["k_tm%d" % par])
            O.tt("dve", k_tmb[:n, :, :], qkv_tm[:n, 4:8, :], rs8[:n, 4:8].unsqueeze(2).to_broadcast([n, 4, 128]), ALU.mult,
                 ["qkv_tm", "rs8"], ["k_tmb"])
            yield
            for h in range(4):
                O.tsm("dve", kbg[:n, h, :], k_tm[:n, h, :], bge[:n, h:h + 1], ["k_tm%d" % par, hk], ["kbg%d" % par])
                O.tsm("pool", vb[:n, h, :], qkv_tm[:n, 8 + h, :], beta[:n, h:h + 1], ["qkv_tm", hk], ["vb%d" % par])
            for h in range(4):
                O.tr(tvF[:, h, :n], k_tmb[:n, h, :], ident_b[:n, :n], ["k_tmb", "ident_b"], ["b0"])
            if need_out:
                for h in range(4):
                    O.tr(tvF[:, 4 + h, :n], q_tm[:n, h, :], ident_b[:n, :n], ["q_tm", "ident_b"], ["b0"])
                O.cp("act", gqT[:, :, :n], tvF[:, 4:8, :n], ["b0"], ["gqT%d" % par])
            O.cp("dve", gkT[:, :, :n], tvF[:, 0:4, :n], ["b0"], ["gkT%d" % par])
            yield

        tvB = PSB[:, 5 * 1024: 6 * 1024].rearrange("p (a b) -> p a b", a=8)

        def back(c):
            q, j, qi, n, pos0 = c["q"], c["j"], c["qi"], c["n"], c["pos0"]
            need_out, first, last, is_sample, par = c["need_out"], c["first"], c["last"], c["is_sample"], c["par"]
            sgk, hk = "sg%d" % par, "hs%d" % par
            hs, k_tm, kbg, vb, gkT, gqT = hs2[par], k_tm2[par], kbg2[par], vb2[par], gkT2[par], gqT2[par]
            beta, g_t, gc, ngc, e_gc, bge = (hs[:, 0:4], hs[:, 4:8], hs[:, 8:12], hs[:, 12:16], hs[:, 16:20], hs[:, 20:24])
            kk, kbk, vbk, gkk, gqk = "k_tm%d" % par, "kbg%d" % par, "vb%d" % par, "gkT%d" % par, "gqT%d" % par
            O.tt("dve", UG[:n, :, :n], U_f[:n, :n].unsqueeze(1).to_broadcast([n, 4, n]),
                 g_t[:n, :].unsqueeze(2).to_broadcast([n, 4, n]), ALU.mult, ["U_f", hk], ["UG"])
            Grow = bank(6)[:, 0:4 * n].rearrange("p (h t) -> p h t", h=4)
            O.mm(Grow, ones_f[:n, :], UG[:n, :, :n], True, True, ["ones_f", "UG"], ["b6"])
            yield
            O.tt("dve", e_dec[:n, :], Grow[:n, :, n - 1], gc[:n, :], ALU.subtract, ["b6", hk], ["e_dec"])
            O.act(e_dec[:n, :], e_dec[:n, :], AF.Exp, ["e_dec"], ["e_dec"])
            O.act(e_last[:, :], Grow[:, :, n - 1], AF.Exp, ["b6"], ["e_last"])
            yield
            for h in range(4):
                O.tt("dve", dtmp[:n, h, :n], Grow[:n, h, :], mLs[:n, :n], ALU.subtract, ["b6", "mLs"], ["dtmp"])
            for h in range(4):
                O.act(MQs[:n, h, :n], dtmp[:n, h, :n], AF.Exp, ["dtmp", hk], ["MQs"], bias=gc[:n, h:h + 1], scale=-1.0)
            yield
            if need_out:
                for h in range(4):
                    O.tt("dve", dtmp[:n, h, :n], Grow[:n, h, :], mU[:n, :n], ALU.add, ["b6", "mU"], ["dtmp"])
                for h in range(4):
                    O.act(MQT[:n, h, :n], dtmp[:n, h, :n], AF.Exp, ["dtmp", hk], ["MQT"], bias=ngc[:n, h:h + 1])
                yield
            for h in range(4):
                O.tsm("pool", kdec[:n, h, :], k_tm[:n, h, :], e_dec[:n, h:h + 1], [kk, "e_dec"], ["kdec"])
            Gp = bank(6)[:, 0:4 * n].rearrange("p (h t) -> p h t", h=4)
            for h in range(4):
                O.mm(Gp[:n, h, :], gkT[:, h, :n], gkT[:, h, :n], True, True, [gkk], ["b6"])
            yield
            for h in range(4):
                O.tsm("dve", dtmp[:n, h, :n], Gp[:n, h, :], beta[:n, h:h + 1], ["b6", hk], ["dtmp"])
            O.tt("dve", A_b[:n, :, :n], dtmp[:n, :, :n], MQs[:n, :, :n], ALU.mult, ["dtmp", "MQs"], ["A_b"])
            yield
            tv7 = PSB[:, 7 * 1024: 8 * 1024].rearrange("p (a b) -> p a b", a=8)
            for h in range(4):
                O.tr(tv7[:n, h, :n], A_b[:n, h, :n], ident_b[:n, :n], ["A_b", "ident_b"], ["b7"])
            O.cp("act", At_b[:n, :, :n], tv7[:n, 0:4, :n], ["b7"], ["At_b"])
            for s2 in range(2):
                O.cp("pool", TTb[:n, :, s2, :n], ident_f[:n, :n].unsqueeze(1).to_broadcast([n, 4, n]), ["ident_f"], ["TTb"])
                O.cp("pool", TTf[:n, :, s2, :n], ident_f[:n, :n].unsqueeze(1).to_broadcast([n, 4, n]), ["ident_f"], ["TTf"])
            yield
            Xp = bank(5)[:, 0:4 * n].rearrange("p (h t) -> p h t", h=4)
            Yp = PS[:, 6 * 512: 8 * 512].rearrange("p (h s t) -> p h s t", h=4, s=2)
            l = 0
            while (1 << l) < n:
                for h in range(4):
                    O.mm(Xp[:n, h, :], At_b[:n, h, :n], TTb[:n, h, 0, :n], True, True, ["At_b", "TTb"], ["b5"])
                for h in range(4):
                    O.tt("dve", X_b[:n, h, :n], Xp[:n, h, :], lvl[:n, l, :n], ALU.mult, ["b5", "lvl"], ["X_b"])
                yield
                for h in range(4):
                    O.mm(Yp[:n, h, 0, :n], TTb[:n, h, 1, :n], X_b[:n, h, :n], True, True, ["TTb", "X_b"], ["b6", "b7"])
                    O.mm(Yp[:n, h, 1, :n], X_b[:n, h, :n], TTb[:n, h, 1, :n], True, True, ["TTb", "X_b"], ["b6", "b7"])
                O.tt("dve", TTf[:n, :, :, :n], TTf[:n, :, :, :n], Yp[:n, :, :, :n], ALU.subtract, ["TTf", "b6", "b7"], ["TTf"])
                O.cp("act", TTb[:n, :, :, :n], TTf[:n, :, :, :n], ["TTf"], ["TTb"])
                l += 1
                yield
            up_ = bank(5)[:, :].rearrange("p (h t) -> p h t", h=4)
            wTp = bank(6)[:, 0:4 * n].rearrange("p (h t) -> p h t", h=4)
            for h in range(4):
                O.mm(up_[:n, h, :], TTb[:n, h, 1, :n], vb[:n, h, :], True, True, ["TTb", vbk], ["b5"])
            for h in range(4):
                O.mm(wTp[:, h, :], kbg[:n, h, :], TTb[:n, h, 1, :n], True, True, ["TTb", kbk], ["b6"])
            yield
            O.cp("act", u_sb[:n, :, :], up_[:n, :, :], ["b5"], ["u_sb"])
            O.cp("act", wT_b[:, :, :n], wTp[:, :, :], ["b6"], ["wT_b"])
            yield
            if first:
                if is_sample:
                    O.ld("sg", Sg[:, :, :], sgdn.rearrange("h k v -> k h v"), [], ["Sg"])
                else:
                    O.mset("pool", Sg[:, :, :], 0.0, ["Sg"])
                O.cp("act", Sgb[:, :, :], Sg[:, :, :], ["Sg"], ["Sgb"])
            wSp = bank(7)[:, :].rearrange("p (h t) -> p h t", h=4)
            qSp = bank(5)[:, :].rearrange("p (h t) -> p h t", h=4)
            qkTp = bank(6)[:, 0:4 * n].rearrange("p (h t) -> p h t", h=4)
            for h in range(4):
                O.mm(wSp[:n, h, :], wT_b[:, h, :n], Sgb[:, h, :], True, True, ["wT_b", "Sgb"], ["b7"])
            O.tt("dve", vnb[:n, :, :], u_sb[:n, :, :], wSp[:n, :, :], ALU.subtract, ["u_sb", "b7"], ["vnb"])
            yield
            if need_out:
                for h in range(4):
                    O.mm(qSp[:n, h, :], gqT[:, h, :n], Sgb[:, h, :], True, True, [gqk, "Sgb"], ["b5"])
                for h in range(4):
                    O.mm(qkTp[:n, h, :], gkT[:, h, :n], gqT[:, h, :n], True, True, [gkk, gqk], ["b6"])
                for h in range(4):
                    O.tsm("dve", oi[:n, h, :], qSp[:n, h, :], e_gc[:n, h:h + 1], ["b5", hk], ["oi"])
                O.tt("dve", qkT_b[:n, :, :n], qkTp[:n, :, :], MQT[:n, :, :n], ALU.mult, ["b6", "MQT"], ["qkT_b"])
                o2p = bank(7)[:, :].rearrange("p (h t) -> p h t", h=4)
                for h in range(4):
                    O.mm(o2p[:n, h, :], qkT_b[:n, h, :n], vnb[:n, h, :], True, True, ["qkT_b", "vnb"], ["b7"])
                O.tt("dve", o_sb[:n, :, :], oi[:n, :, :], o2p[:n, :, :], ALU.add, ["oi", "b7"], ["o_sb"])
            yield
            dSp = bank(5)[:, :].rearrange("p (h t) -> p h t", h=4)
            for h in range(4):
                O.mm(dSp[:, h, :], kdec[:n, h, :], vnb[:n, h, :], True, True, ["kdec", "vnb"], ["b5"])
            for h in range(4):
                O.tsm("dve", Sg[:, h, :], Sg[:, h, :], e_last[:, h:h + 1], ["Sg", "e_last"], ["Sg"])
            O.tt("dve", Sg[:, :, :], Sg[:, :, :], dSp[:, :, :], ALU.add, ["Sg", "b5"], ["Sg"])
            O.cp("act", Sgb[:, :, :], Sg[:, :, :], ["Sg"], ["Sgb"])
            if last:
                S_out = c["S_out"]
                O.S.dma("sp", "so", lambda e: e.dma_start(out=S_out.rearrange("h k v -> k h v"), in_=Sg[:, :, :]), ["Sg"], [])
            if need_out:
                sg = sg2[par]
                O.act(sc2[:n, 0:512], o_sb[:n, :, :].rearrange("p h d -> p (h d)"), AF.Square, ["o_sb"], ["sc2"])
                O.red(st1[:n, 40:44], sc2[:n, 0:512].rearrange("p (h d) -> p h d", h=4), ["sc2"], ["st_o"])
                rsq(st1[:n, 40:44], st1[:n, 40:44], 1.0 / 128, n, ["st_o"], ["st_o"])
                for h in range(4):
                    O.tsm("dve", o_sb[:n, h, :], o_sb[:n, h, :], st1[:n, 40 + h:41 + h], ["o_sb", "st_o"], ["o_sb"])
                O.tt("dve", o_sb[:n, :, :], go[:n, :].unsqueeze(1).to_broadcast([n, 4, 128]), o_sb[:n, :, :], ALU.mult,
                     ["o_sb", "go"], ["o_sb"])
                O.tt("dve", gm_b[:n, :], o_sb[:n, :, :].rearrange("p h d -> p (h d)"), sg[:n, :], ALU.mult, ["o_sb", sgk], ["gm_b"])
                O.S.dma("sp", "gmd", lambda e: e.dma_start(out=gm_d[q, qi, 0:n, :], in_=gm_b[:n, :]), ["gm_b"], [])
            yield

        jobs = []
        if _STOP != 'prep':
            for s in range(NP):
                jobs.append(dict(q=s, j=0, qi=0, n=16, pos0=0, x_src=meta, need_out=False, first=True, last=False, is_sample=False,
                                 lat_out=p_lat[s, 0:16, :], rope_out=p_rope[s, 0:16, :], conv_out=None, S_out=None, prev_n=0))
                for i in range(NFT):
                    jobs.append(dict(q=s, j=i + 1, qi=i, n=128, pos0=16 + 128 * i, x_src=xp[s, 128 * i:128 * (i + 1), :],
                                     need_out=True, first=False, last=(i == NFT - 1), is_sample=False,
                                     lat_out=p_lat[s, 16 + 128 * i:16 + 128 * (i + 1), :],
                                     rope_out=p_rope[s, 16 + 128 * i:16 + 128 * (i + 1), :],
                                     conv_out=p_conv[s], S_out=p_S[s], prev_n=(16 if i == 0 else 128)))
            jobs.append(dict(q=NP, j=CFT + 1, qi=0, n=64, pos0=LC, x_src=xs, need_out=True, first=True, last=True, is_sample=True,
                             lat_out=s_lat, rope_out=s_rope, conv_out=s_conv, S_out=s_S, prev_n=0))
        for k, c in enumerate(jobs):
            c["par"] = k % 2
            c["load_self"] = (k == 0)
        cache_jobs = [(NP, 0, 16, 0)] + [(NP, i + 1, 128, 16 + 128 * i) for i in range(CFT)]
        if _STOP == 'prep':
            cache_jobs = []

        def run_all(g):
            for _ in g:
                pass

        def interleave(ga, gb):
            da = db = False
            while not (da and db):
                if not da:
                    try:
                        next(ga)
                    except StopIteration:
                        da = True
                if not db:
                    try:
                        next(gb)
                    except StopIteration:
                        db = True

        def front_plus(k):
            yield from front(jobs[k], jobs[k + 1] if k + 1 < len(jobs) else None)
            if cache_jobs and (k % 2 == 1 or len(jobs) - k <= len(cache_jobs)):
                cj = cache_jobs.pop(0)
                yield from cache_kv(*cj)

        if jobs:
            run_all(front_plus(0))
            for k in range(len(jobs)):
                if k + 1 < len(jobs):
                    interleave(back(jobs[k]), front_plus(k + 1))
                else:
                    run_all(back(jobs[k]))
        while cache_jobs:
            run_all(cache_kv(*cache_jobs.pop(0)))
        S.emit()
    if _STOP in ('prep', '1a'):
        return nc

    nc.all_engine_barrier()

    with ExitStack() as st:
        def sb(name, shape, dt=F32):
            return st.enter_context(nc.sbuf_tensor(name, shape, dt))

        PS = st.enter_context(nc.psum_tensor("psb", [128, 4096], F32))
        PSB = PS[:].bitcast(BF16)

        def bank(b, w=512, off=0):
            return PS[:, b * 512 + off: b * 512 + off + w]

        S = Sched(nc, "b")
        O = Ops(S)
        KTc = sb("KTc", [128, 8, NCMAX], BF16)
        Vc = sb("Vc", [128, NKT, 520], BF16)
        w_out_b = sb("w_out_b", [128, 8, 1024], BF16)
        stg = sb("stgb", [128, 1024])
        gM = sb("gM", [128, 4])
        ident_f = sb("ident_fb", [128, 128])
        ident_b = sb("ident_bb", [128, 128], BF16)
        rowa = sb("rowa", [1, 128], BF16)
        rowb = sb("rowb", [1, 128], BF16)
        eps_t = sb("eps_tb", [128, 1])
        qTt2 = [sb("qTt%d" % i, [128, 8, 128], BF16) for i in range(2)]
        mix_b2 = [sb("mix_b%d" % i, [128, 1024], BF16) for i in range(2)]
        mixT = sb("mixT", [128, 8, 128], BF16)
        xt2 = [sb("xtb%d" % i, [128, 1024]) for i in range(2)]
        x1s2 = [sb("x1s%d" % i, [128, 1024]) for i in range(2)]
        P_sb = sb("P_sb", [128, 3, 4, 128], BF16)
        attn_tm = sb("attn_tm", [128, 512])
        junk = sb("junkb", [128, 512], BF16)
        st1 = sb("st1b", [128, 16])

        O.ld("c0", ident_f[:], c_ident, [], ["ident_f"])
        O.cp("dve", ident_b[:], ident_f[:], ["ident_f"], ["ident_b"])
        O.mset("dve", eps_t[:], EPS, ["eps_t"])
        O.mset("dve", rowa[:, :], 1.0, ["rowa"])
        O.mset("dve", rowa[:, 0:64], 0.0, ["rowa"])
        O.mset("dve", rowb[:, :], -30000.0, ["rowb"])
        O.mset("dve", rowb[:, 64:128], 0.0, ["rowb"])
        tmpT = sb("tmpTb", [8, 128])
        O.ld("ldT", tmpT[:4, :], mla_out_norm.rearrange("(k p) -> k p", p=128), [], ["tmpT"])
        O.tr(PS[:, 0:4], tmpT[:4, :], ident_f[:4, :4], ["tmpT", "ident_f"], ["bs0"])
        O.cp("dve", gM[:, :], PS[:, 0:4], ["bs0"], ["gM"])
        for kc in range(8):
            O.ld("stg", stg[:], w_out[kc * 128:(kc + 1) * 128, :], [], ["stg"])
            if kc < 4:
                O.tsm("dve", w_out_b[:, kc, :], stg[:], gM[:, kc:kc + 1], ["stg", "gM"], ["w_out_b"])
            else:
                O.cp("dve", w_out_b[:, kc, :], stg[:], ["stg"], ["w_out_b"])

        SBANK = (0, 1, 7)
        tcount = [0]

        def attn_seq(q, tiles):
            units = []
            for ti, (qi, n, x_src, keytiles, diag_j, x1_idx) in enumerate(tiles):
                groups = []
                cur = []
                for kt in keytiles:
                    if kt[1] != 128:
                        if cur:
                            groups.append(cur)
                            cur = []
                        groups.append([kt])
                    else:
                        cur.append(kt)
                        if len(cur) == 4:
                            groups.append(cur)
                            cur = []
                if cur:
                    groups.append(cur)
                for h in range(8):
                    for gi, g in enumerate(groups):
                        units.append((ti, h, g, gi == 0, gi == len(groups) - 1))
            tpar = {}

            def prologue(ti):
                qi, n, x_src, keytiles, diag_j, x1_idx = tiles[ti]
                p = tcount[0] % 2
                tcount[0] += 1
                tpar[ti] = p
                O.ld("qt%d" % p, qTt2[p][:, :, :n], qT_d[q, qi, :, :, 0:n], [], ["qTt%d" % p])
                O.ld("gm%d" % p, mix_b2[p][:n, 512:1024], gm_d[q, qi, 0:n, :], [], ["mix_g%d" % p])
                O.ld("xt%d" % p, xt2[p][:n, :], x_src, [], ["xt%d" % p])

            def qk(ui):
                ti, h, g, fg, lg = units[ui]
                qi, n, x_src, keytiles, diag_j, x1_idx = tiles[ti]
                if ti not in tpar:
                    prologue(ti)
                p = tpar[ti]
                par = ui % 3
                Sp = bank(SBANK[par])[:, :].rearrange("p (s t) -> p s t", s=4)
                for s_, (j, nk, col0) in enumerate(g):
                    dg = (j == diag_j)
                    O.mm(Sp[:nk, s_, :n], KTc[0:96, h, col0:col0 + nk], qTt2[p][0:96, h, :n], True, not dg,
                         ["KTc", "qTt%d" % p], ["bs%d" % par])
                    if dg:
                        O.mm(Sp[:nk, s_, :n], rowa[0:1, :nk], rowb[0:1, :n], False, True, ["rowa", "rowb"], ["bs%d" % par])

            def expv(ui):
                ti, h, g, fg, lg = units[ui]
                qi, n, x_src, keytiles, diag_j, x1_idx = tiles[ti]
                p = tpar[ti]
                par = ui % 3
                Sp = bank(SBANK[par])[:, :].rearrange("p (s t) -> p s t", s=4)
                Op = bank(2 + (h % 2))
                ok = "bo%d" % (h % 2)
                nk0 = g[0][1]
                O.act(P_sb[:nk0, par, 0:len(g), :n], Sp[:nk0, 0:len(g), :n], AF.Exp, ["bs%d" % par], ["P%d" % par], scale=ATT_SCALE)
                for s_, (j, nk, col0) in enumerate(g):
                    O.mm(Op[:n, 0:65], P_sb[:nk, par, s_, :n], Vc[:nk, j, h * 65:(h + 1) * 65], fg and s_ == 0,
                         lg and s_ == len(g) - 1, ["P%d" % par, "Vc"], [ok])
                if lg:
                    O.recip(st1[:n, h:h + 1], Op[:n, 64:65], [ok], ["st_r%d" % h])
                    O.tsm("dve", attn_tm[:n, h * 64:(h + 1) * 64], Op[:n, 0:64], st1[:n, h:h + 1], [ok, "st_r%d" % h], ["attn_tm"])
                    if h == 7:
                        epilogue(ti)

            def epilogue(ti):
                qi, n, x_src, keytiles, diag_j, x1_idx = tiles[ti]
                p = tpar[ti]
                mix_b = mix_b2[p]
                O.act(junk[:n, :], attn_tm[:n, :], AF.Square, ["attn_tm"], ["junk", "st_a"], accum_out=st1[:n, 8:9])
                O.act(st1[:n, 8:9], st1[:n, 8:9], AF.Ln, ["st_a", "eps_t"], ["st_a"], scale=1.0 / 512, bias=eps_t[:n, 0:1])
                O.act(st1[:n, 8:9], st1[:n, 8:9], AF.Exp, ["st_a"], ["st_a"], scale=-0.5)
                O.asc(mix_b[:n, 0:512], attn_tm[:n, :], st1[:n, 8:9], ["attn_tm", "st_a"], ["mix_a%d" % p])
                tv = PSB[:, 4 * 1024: 5 * 1024].rearrange("p (a b) -> p a b", a=8)
                for c in range(8):
                    O.tr(tv[:, c, :n], mix_b[:n, c * 128:(c + 1) * 128], ident_b[:n, :n],
                         ["mix_a%d" % p, "mix_g%d" % p, "ident_b"], ["b4"])
                O.cp("dve", mixT[:, :, :n], tv[:, :, :n], ["b4"], ["mixT"])
                x1p = PS[:, 5 * 512: 7 * 512]
                for half in range(2):
                    for c in range(8):
                        O.mm(x1p[:n, half * 512:(half + 1) * 512], mixT[:, c, :n], w_out_b[:, c, half * 512:(half + 1) * 512],
                             c == 0, c == 7, ["mixT", "w_out_b"], ["b56"])
                O.tt("dve", x1s2[p][:n, :], x1p[:n, :], xt2[p][:n, :], ALU.add, ["b56", "xt%d" % p], ["x1s%d" % p])
                O.S.dma("pool", "x1o%d" % p, lambda e: e.dma_start(out=x1_d[x1_idx, 0:n, :], in_=x1s2[p][:n, :]), ["x1s%d" % p], [])

            LOOK = 2
            for ui in range(min(LOOK, len(units))):
                qk(ui)
            for ui in range(len(units)):
                if ui + LOOK < len(units):
                    qk(ui + LOOK)
                expv(ui)

        def load_cache(q, ncols, kts):
            for h in range(8):
                O.ld("ktc", KTc[:, h, 0:ncols], KT_d[q, :, h, 0:ncols], [], ["KTc"])
            full = [j for (j, nk, c0) in kts if nk == 128]
            if full:
                for j0 in range(full[0], full[-1] + 1, 4):
                    j1 = min(j0 + 4, full[-1] + 1)
                    O.ld("vc", Vc[:, j0:j1, :], V_d[q, j0:j1, :, :].rearrange("t p c -> p t c"), [], ["Vc"])
            for (j, nk, c0) in kts:
                if nk != 128:
                    O.ld("vc", Vc[:nk, j, :], V_d[q, j, 0:nk, :], [], ["Vc"])

        for s in range(NP):
            load_cache(s, LP, [(0, 16, 0)] + [(jj + 1, 128, 16 + 128 * jj) for jj in range(NFT)])
            tl = []
            for i in range(NFT):
                kts = [(0, 16, 0)] + [(jj + 1, 128, 16 + 128 * jj) for jj in range(i + 1)]
                tl.append((i, 128, xp[s, 128 * i:128 * (i + 1), :], kts, i + 1, s * NFT + i))
            attn_seq(s, tl)
        kts = [(0, 16, 0)] + [(jj + 1, 128, 16 + 128 * jj) for jj in range(CFT)] + [(CFT + 1, 64, LC)]
        load_cache(NP, LC + 64, kts)
        attn_seq(NP, [(0, 64, xs, kts, -1, NP * NFT)])
        S.emit()
    if _STOP == '1b':
        return nc

    nc.all_engine_barrier()

    with ExitStack() as st:
        def sb(name, shape, dt=F32):
            return st.enter_context(nc.sbuf_tensor(name, shape, dt))

        PS = st.enter_context(nc.psum_tensor("psc", [128, 4096], F32))
        PSB = PS[:].bitcast(BF16)
        S = Sched(nc, "c")
        O = Ops(S)
        w_up_b = sb("w_up_b", [128, 8, 4096], BF16)
        w_dn_b = sb("w_dn_b", [128, 32, 1024], BF16)
        gP = sb("gP", [128, 8])
        ident_f = sb("ident_fc", [128, 128])
        ident_b = sb("ident_bc", [128, 128], BF16)
        eps_t = sb("eps_tc", [128, 1])
        x1t2 = [sb("x1t%d" % i, [128, 2, 1024]) for i in range(2)]
        hn2 = [sb("hn%d" % i, [128, 1024], BF16) for i in range(2)]
        hT2 = [sb("hT%d" % i, [128, 8, 256], BF16) for i in range(2)]
        rl = sb("rl", [128, 3, 512])
        upT = sb("upT", [128, 32, 256], BF16)
        ys2 = [sb("ys%d" % i, [128, 1024]) for i in range(2)]
        junk = sb("junkc", [128, 1024], BF16)
        st1 = sb("st1c", [128, 8])

        O.ld("c0", ident_f[:], c_ident, [], ["ident_f"])
        O.cp("dve", ident_b[:], ident_f[:], ["ident_f"], ["ident_b"])
        O.mset("dve", eps_t[:], EPS, ["eps_t"])
        tmpT = sb("tmpTc", [8, 128])
        O.ld("ldT", tmpT[:8, :], mlp_norm.rearrange("(k p) -> k p", p=128), [], ["tmpT"])
        O.tr(PS[:, 0:8], tmpT[:8, :], ident_f[:8, :8], ["tmpT", "ident_f"], ["b0"])
        O.cp("dve", gP[:, :], PS[:, 0:8], ["b0"], ["gP"])
        for kc in range(8):
            sp_ = kc % 2
            stg = x1t2[sp_][:, :, :].rearrange("p a d -> p (a d)")
            for hf in range(2):
                O.ld("stg%d" % sp_, stg[:, :], w_up[kc * 128:(kc + 1) * 128, hf * 2048:(hf + 1) * 2048], [], ["x1t%d" % sp_])
                O.tsm("dve", w_up_b[:, kc, hf * 2048:(hf + 1) * 2048], stg[:, :], gP[:, kc:kc + 1],
                      ["x1t%d" % sp_, "gP"], ["w_up_b"])
        for c2 in range(16):
            sp_ = c2 % 2
            O.ld("stg%d" % sp_, x1t2[sp_][:, :, :], w_down[c2 * 256:(c2 + 1) * 256, :].rearrange("(f p) d -> p f d", p=128),
                 [], ["x1t%d" % sp_])
            O.cp("dve", w_dn_b[:, 2 * c2:2 * c2 + 2, :], x1t2[sp_][:, :, :], ["x1t%d" % sp_], ["w_dn_b"])

        def mlp_front_a(bi, subs):
            p = bi % 2
            for si, (idx, n, y_out) in enumerate(subs):
                O.ld("x1%d" % p, x1t2[p][:n, si, :], x1_d[idx, 0:n, :], [], ["x1t%d" % p])
            for si, (idx, n, y_out) in enumerate(subs):
                O.act(junk[:n, :], x1t2[p][:n, si, :], AF.Square, ["x1t%d" % p], ["junk", "st"], accum_out=st1[:n, si:si + 1])
                O.act(st1[:n, si:si + 1], st1[:n, si:si + 1], AF.Ln, ["st", "eps_t"], ["st"], scale=1.0 / 1024, bias=eps_t[:n, 0:1])
                O.act(st1[:n, si:si + 1], st1[:n, si:si + 1], AF.Exp, ["st"], ["st"], scale=-0.5)
                O.asc(hn2[si][:n, :], x1t2[p][:n, si, :], st1[:n, si:si + 1], ["x1t%d" % p, "st"], ["hn%d" % si])

        def mlp_front_b(bi, subs):
            p = bi % 2
            for si, (idx, n, y_out) in enumerate(subs):
                tv = PSB[:, 0:1024].rearrange("p (a b) -> p a b", a=8)
                for kc in range(8):
                    O.tr(tv[:, kc, :n], hn2[si][:n, kc * 128:(kc + 1) * 128], ident_b[:n, :n], ["hn%d" % si, "ident_b"], ["b0"])
                O.cp("dve", hT2[p][:, :, si * 128:si * 128 + n], tv[:, :, :n], ["b0"], ["hT%d" % p])

        def mlp_body(bi, subs, mid=None):
            p = bi % 2
            nt = sum(n for (_, n, _) in subs) if len(subs) == 1 else 256
            UPB = (1, 2, 7)
            for fc in range(32):
                par = fc % 3
                reg = PS[:, UPB[par] * 512:UPB[par] * 512 + nt]
                rk = "bu%d" % par
                for kc in range(8):
                    O.mm(reg, w_up_b[:, kc, fc * 128:(fc + 1) * 128], hT2[p][:, kc, 0:nt], kc == 0, kc == 7,
                         ["w_up_b", "hT%d" % p], [rk])
                O.act(rl[:, par, 0:nt], reg, AF.Relu, [rk], ["rl%d" % par])
                O.tt("dve", upT[:, fc, 0:nt], rl[:, par, 0:nt], rl[:, par, 0:nt], ALU.mult,
                     ["rl%d" % par], ["upT"])
            if mid is not None:
                mid()
            for si, (idx, n, y_out) in enumerate(subs):
                yb = 5 if si == 0 else 3
                yp = PS[:, yb * 512: (yb + 2) * 512]
                yk = "by%d" % si
                for half in range(2):
                    for fc in range(32):
                        O.mm(yp[:n, half * 512:(half + 1) * 512], upT[:, fc, si * 128:si * 128 + n],
                             w_dn_b[:, fc, half * 512:(half + 1) * 512], fc == 0, fc == 31, ["upT", "w_dn_b"], [yk])
                ys = ys2[si]
                O.tt("dve", ys[:n, :], yp[:n, :], x1t2[p][:n, si, :], ALU.add, [yk, "x1t%d" % p], ["ys%d" % si])
                O.S.dma("pool", "yo%d" % si, lambda e, ys=ys, n=n, y_out=y_out: e.dma_start(out=y_out, in_=ys[:n, :]),
                        ["ys%d" % si], [])

        blocks = []
        flat = []
        for s in range(NP):
            for i in range(NFT):
                flat.append((s * NFT + i, 128, y_p[s, 128 * i:128 * (i + 1), :]))
        for i in range(0, len(flat), 2):
            blocks.append(flat[i:i + 2])
        blocks.append([(NP * NFT, 64, y_s)])
        mlp_front_a(0, blocks[0])
        mlp_front_b(0, blocks[0])
        for bi in range(len(blocks)):
            if bi + 1 < len(blocks):
                mlp_front_a(bi + 1, blocks[bi + 1])
                mlp_body(bi, blocks[bi], mid=lambda bi=bi: mlp_front_b(bi + 1, blocks[bi + 1]))
            else:
                mlp_body(bi, blocks[bi])
        S.emit()
    return nc


_NC_CACHE = {}


def run_cores(per_core_inputs, NP, SEQ, PAST):
    key = (NP, SEQ, PAST)
    if key not in _NC_CACHE:
        _NC_CACHE[key] = build_nc(NP, SEQ, PAST)
    nc = _NC_CACHE[key]
    res = run_bass_kernel_spmd(nc, per_core_inputs, core_ids=list(range(len(per_core_inputs))))
    return res.results


def make_core_inputs(c, NP, inputs, consts):
    f = lambda a: np.ascontiguousarray(np.asarray(a), dtype=np.float32)
    d = {
        "xp": f(inputs["x_prompt"][NP * c:NP * (c + 1)]),
        "meta": f(inputs["meta_tokens"]),
        "xs": f(inputs["x_sample"][c]),
        "clat": f(inputs["cache_kv_latent"][0, c]),
        "crope": f(inputs["cache_k_rope"][0, c]),
        "sgdn": f(inputs["state_gdn"][0, c]),
        "sconv": f(inputs["state_conv"][0, c]),
    }
    for k in ("w_in", "w_uq", "w_uk", "w_uv", "w_out", "w_up", "w_down", "attn_norm", "q_a_norm", "kv_a_norm", "q_norm",
              "k_norm", "mla_out_norm", "conv_w", "a_log", "dt_bias", "gdn_out_norm", "mlp_norm"):
        d[k] = f(inputs[k][0])
    d.update(consts)
    return d


def kernel(**inputs):
    NCORES = 8
    B, SEQ = inputs["x_prompt"].shape[:2]
    NP = B // NCORES
    PAST = inputs["cache_kv_latent"].shape[2] - 16
    LP = 16 + SEQ
    consts = host_consts(max(LP, 16 + PAST + 64))
    ins = [make_core_inputs(c, NP, inputs, consts) for c in range(NCORES)]
    res = run_cores(ins, NP, SEQ, PAST)
    cat = lambda k: np.concatenate([r[k] for r in res], axis=0)
    stk = lambda k: np.stack([r[k] for r in res], axis=0)
    y_p = cat("y_p")
    y_s = stk("y_s")
    return (y_p, y_s, cat("p_lat")[None], cat("p_rope")[None], cat("p_S")[None], cat("p_conv")[None],
            stk("s_lat")[None], stk("s_rope")[None], stk("s_S")[None], stk("s_conv")[None])
```

```python
import numpy as np
import ml_dtypes
from contextlib import ExitStack
import concourse.bass as bass
import concourse.mybir as mybir
from concourse.bass_utils import run_bass_kernel_spmd

F32 = mybir.dt.float32
BF16 = mybir.dt.bfloat16
AF = mybir.ActivationFunctionType
ALU = mybir.AluOpType
AX = mybir.AxisListType
EPS = 1e-6
NEG = -30000.0
ENGS = ("sp", "act", "dve", "pool", "pe")


class Sched:
    def __init__(self, nc, tag):
        self.nc = nc
        self.tag = tag
        self.lists = {e: [] for e in ENGS}
        self.last_writer = {}
        self.readers = {}
        self.dma_count = {}
        self.waited = {e: {} for e in ENGS}

    def _deps(self, eng, reads, writes):
        deps = []
        for k in reads:
            w = self.last_writer.get(k)
            if w is not None:
                deps.append((w, True))
        for k in writes:
            w = self.last_writer.get(k)
            if w is not None:
                deps.append((w, False))
            for t in self.readers.get(k, {}).values():
                deps.append((t, False))
        out = []
        wd = self.waited[eng]
        for t, raw in deps:
            if t[0] == "E" and t[1] == eng and eng == "pe":
                continue
            sid = (t[0], t[1])
            if wd.get(sid, -1) >= t[2]:
                continue
            wd[sid] = t[2]
            out.append(t)
        return out

    def _commit(self, token, stream, reads, writes):
        for k in reads:
            self.readers.setdefault(k, {})[stream] = token
        for k in writes:
            self.last_writer[k] = token
            self.readers[k] = {}

    def op(self, eng, fn, reads=(), writes=()):
        deps = self._deps(eng, reads, writes)
        idx = len(self.lists[eng])
        self.lists[eng].append({"fn": fn, "deps": deps, "flag": False, "dma": None})
        self._commit(("E", eng, idx), eng, reads, writes)

    def dma(self, eng, key, fn, reads=(), writes=(), n=1):
        deps = self._deps(eng, reads, writes)
        c = self.dma_count.get(key, 0) + 16 * n
        self.dma_count[key] = c
        self.lists[eng].append({"fn": fn, "deps": deps, "flag": False, "dma": key})
        self._commit(("D", key, c), "D" + key, reads, writes)

    def emit(self):
        nc = self.nc
        for e in ENGS:
            for rec in self.lists[e]:
                for t in rec["deps"]:
                    if t[0] == "E":
                        self.lists[t[1]][t[2]]["flag"] = True
        val = {}
        for e in ENGS:
            c = 0
            v = []
            for rec in self.lists[e]:
                if rec["flag"] and rec["dma"] is None:
                    c += 1
                v.append(c)
            val[e] = v
        with ExitStack() as st:
            esem = {e: st.enter_context(nc.semaphore(self.tag + "s_" + e)) for e in ENGS}
            dsem = {k: st.enter_context(nc.semaphore(self.tag + "d_" + k)) for k in self.dma_count}
            block = st.enter_context(nc.Block())
            final = dict(self.dma_count)

            def run(e, engine):
                for rec in self.lists[e]:
                    for t in rec["deps"]:
                        if t[0] == "E":
                            engine.wait_ge(esem[t[1]], val[t[1]][t[2]])
                        else:
                            engine.wait_ge(dsem[t[1]], t[2])
                    r = rec["fn"](engine)
                    if rec["dma"] is not None:
                        if not isinstance(r, (list, tuple)):
                            r = [r]
                        for ins in r:
                            ins.then_inc(dsem[rec["dma"]], 16)
                    elif rec["flag"]:
                        r.then_inc(esem[e], 1)
                if e == "sp":
                    for k, c in final.items():
                        engine.wait_ge(dsem[k], c)

            @block.sync
            def _(eng):
                run("sp", eng)

            @block.scalar
            def _(eng):
                run("act", eng)

            @block.vector
            def _(eng):
                run("dve", eng)

            @block.gpsimd
            def _(eng):
                run("pool", eng)

            @block.tensor
            def _(eng):
                run("pe", eng)


class Ops:
    def __init__(self, S):
        self.S = S

    def act(self, out, in_, func, r, w, **kw):
        self.S.op("act", lambda e: e.activation(out=out, in_=in_, func=func, **kw), r, w)

    def tt(self, eng, out, in0, in1, op, r, w):
        self.S.op(eng, lambda e: e.tensor_tensor(out=out, in0=in0, in1=in1, op=op), r, w)

    def stt(self, eng, out, in0, scalar, in1, op0, op1, r, w):
        self.S.op(eng, lambda e: e.scalar_tensor_tensor(out=out, in0=in0, scalar=scalar, in1=in1, op0=op0, op1=op1), r, w)

    def tsm(self, eng, out, in0, s1, r, w):
        self.S.op(eng, lambda e: e.tensor_scalar_mul(out=out, in0=in0, scalar1=s1), r, w)

    def asc(self, out, in_, sc, r, w):
        self.S.op("act", lambda e: e.activation(out=out, in_=in_, func=AF.Copy, scale=sc), r, w)

    def tsa(self, eng, out, in0, s1, r, w):
        self.S.op(eng, lambda e: e.tensor_scalar_add(out=out, in0=in0, scalar1=s1), r, w)

    def cp(self, eng, out, in_, r, w):
        if eng == "act":
            self.S.op("act", lambda e: e.copy(out=out, in_=in_), r, w)
        else:
            self.S.op(eng, lambda e: e.tensor_copy(out=out, in_=in_), r, w)

    def recip(self, out, in_, r, w):
        self.S.op("dve", lambda e: e.reciprocal(out=out, in_=in_), r, w)

    def red(self, out, in_, r, w):
        self.S.op("dve", lambda e: e.tensor_reduce(out=out, in_=in_, axis=AX.X, op=ALU.add), r, w)

    def mset(self, eng, ap, v, w):
        self.S.op(eng, lambda e: e.memset(ap, v), (), w)

    def mm(self, out, lhsT, rhs, start, stop, r, w):
        self.S.op("pe", lambda e: e.matmul(out, lhsT=lhsT, rhs=rhs, start=start, stop=stop), r, w)

    def tr(self, out, in_, ident, r, w):
        self.S.op("pe", lambda e: e.transpose(out=out, in_=in_, identity=ident), r, w)

    def ld(self, key, out, in_, r, w, slow=False):
        if slow:
            self.S.dma("sp", key, lambda e: e.dma_start(out=out, in_=in_, allow_slow_non_contiguous=True), r, w)
        else:
            self.S.dma("sp", key, lambda e: e.dma_start(out=out, in_=in_), r, w)


def host_consts(npos):
    c = {}
    c["c_ident"] = np.eye(128, dtype=np.float32)
    p = np.arange(128)[:, None]
    q = np.arange(128)[None, :]
    c["c_U"] = (p <= q).astype(np.float32)
    c["c_mLs"] = np.where(p > q, 0.0, NEG).astype(np.float32)
    c["c_mU"] = np.where(q >= p, 0.0, NEG).astype(np.float32)
    lv = np.zeros((128, 7, 128), np.float32)
    for l in range(7):
        b = 1 << l
        lv[:, l, :] = ((p // (2 * b) == q // (2 * b)) & (p % (2 * b) >= b) & (q % (2 * b) < b)).astype(np.float32)
    c["c_lvl"] = lv
    half = 16
    inv_freq = (np.float32(10000.0) ** (-np.arange(half, dtype=np.float32) / np.float32(half))).astype(np.float32)
    ang = np.arange(npos, dtype=np.float32)[:, None] * inv_freq[None, :]
    c["c_cs"] = np.concatenate([np.cos(ang), np.sin(ang)], axis=1).astype(np.float32)
    return c


import os
_STOP = os.environ.get('KSTOP', '')
_CUT = float(os.environ.get('KCUT', '99'))


def build_nc(NP, SEQ, PAST):
    NFT = SEQ // 128
    LP = 16 + SEQ
    CFT = PAST // 128
    LC = 16 + PAST
    NSEQ = NP + 1
    NCMAX = max(LP, LC + 64)
    NKT = max(NFT + 1, CFT + 2)
    NQT = max(NFT, 1)
    NX1 = NP * NFT + 1
    NPOS = max(LP, LC + 64)

    nc = bass.Bass("TRN2", target_bir_lowering=False)

    def din(name, shape):
        return nc.dram_tensor(name, shape, F32, kind="ExternalInput").ap()

    def dout(name, shape):
        return nc.dram_tensor(name, shape, F32, kind="ExternalOutput").ap()

    xp = din("xp", [NP, SEQ, 1024])
    meta = din("meta", [16, 1024])
    xs = din("xs", [64, 1024])
    clat = din("clat", [LC, 256])
    crope = din("crope", [LC, 32])
    sgdn = din("sgdn", [4, 128, 128])
    sconv = din("sconv", [3, 1536])
    w_in = din("w_in", [1024, 2728])
    w_uq = din("w_uq", [384, 768])
    w_uk = din("w_uk", [256, 512])
    w_uv = din("w_uv", [256, 512])
    w_out = din("w_out", [1024, 1024])
    w_up = din("w_up", [1024, 4096])
    w_down = din("w_down", [4096, 1024])
    attn_norm = din("attn_norm", [1024])
    q_a_norm = din("q_a_norm", [384])
    kv_a_norm = din("kv_a_norm", [256])
    q_norm = din("q_norm", [96])
    k_norm = din("k_norm", [96])
    mla_out_norm = din("mla_out_norm", [512])
    conv_w = din("conv_w", [4, 1536])
    a_log = din("a_log", [4])
    dt_bias = din("dt_bias", [4])
    gdn_out_norm = din("gdn_out_norm", [128])
    mlp_norm = din("mlp_norm", [1024])
    c_ident = din("c_ident", [128, 128])
    c_U = din("c_U", [128, 128])
    c_mLs = din("c_mLs", [128, 128])
    c_mU = din("c_mU", [128, 128])
    c_lvl = din("c_lvl", [128, 7, 128])
    c_cs = din("c_cs", [NPOS, 32])

    y_p = dout("y_p", [NP, SEQ, 1024])
    y_s = dout("y_s", [64, 1024])
    p_lat = dout("p_lat", [NP, LP, 256])
    p_rope = dout("p_rope", [NP, LP, 32])
    p_S = dout("p_S", [NP, 4, 128, 128])
    p_conv = dout("p_conv", [NP, 3, 1536])
    s_lat = dout("s_lat", [64, 256])
    s_rope = dout("s_rope", [64, 32])
    s_S = dout("s_S", [4, 128, 128])
    s_conv = dout("s_conv", [3, 1536])

    KT_d = nc.dram_tensor("KT_d", [NSEQ, 128, 8, NCMAX], BF16, kind="Internal").ap()
    V_d = nc.dram_tensor("V_d", [NSEQ, NKT, 128, 520], BF16, kind="Internal").ap()
    qT_d = nc.dram_tensor("qT_d", [NSEQ, NQT, 128, 8, 128], BF16, kind="Internal").ap()
    gm_d = nc.dram_tensor("gm_d", [NSEQ, NQT, 128, 512], BF16, kind="Internal").ap()
    x1_d = nc.dram_tensor("x1_d", [NX1, 128, 1024], F32, kind="Internal").ap()

    ATT_SCALE = 96.0 ** -0.5

    with ExitStack() as st:
        def sb(name, shape, dt=F32):
            return st.enter_context(nc.sbuf_tensor(name, shape, dt))

        PS = st.enter_context(nc.psum_tensor("ps", [128, 4096], F32))
        PSB = PS[:].bitcast(BF16)

        def bank(b, w=512, off=0):
            return PS[:, b * 512 + off: b * 512 + off + w]

        S = Sched(nc, "a")
        O = Ops(S)

        w_in_b = sb("w_in_b", [128, 8, 2728], BF16)
        w_uq_b = sb("w_uq_b", [128, 3, 768], BF16)
        w_uk_b = sb("w_uk_b", [128, 2, 512], BF16)
        w_uv_b = sb("w_uv_b", [128, 2, 512], BF16)
        convd = sb("convd", [128, 48, 128], BF16)
        stg = sb("stg", [128, 2728], F32)
        gA = sb("gA", [128, 8])
        gQ = sb("gQ", [128, 3])
        cw = sb("cw", [128, 4, 12])
        ident_f = sb("ident_f", [128, 128])
        ident_b = sb("ident_b", [128, 128], BF16)
        U_f = sb("U_f", [128, 128])
        ones_f = sb("ones_f", [128, 128])
        mLs = sb("mLs", [128, 128])
        mU = sb("mU", [128, 128])
        lvl = sb("lvl", [128, 7, 128])
        gkv = sb("gkv", [128, 256])
        gq = sb("gq", [128, 96])
        gk = sb("gk", [128, 96])
        go = sb("go", [128, 128])
        negA = sb("negA", [128, 4])
        dtb = sb("dtb", [128, 4])
        eps_t = sb("eps_t", [128, 1])
        one_t = sb("one_t", [128, 1])

        O.ld("c0", ident_f[:], c_ident, [], ["ident_f"])
        O.ld("c1", U_f[:], c_U, [], ["U_f"])
        O.ld("c2", mLs[:], c_mLs, [], ["mLs"])
        O.ld("c3", mU[:], c_mU, [], ["mU"])
        O.ld("c4", lvl[:], c_lvl, [], ["lvl"])
        O.ld("c5", gkv[:], kv_a_norm.partition_broadcast(128), [], ["gkv"])
        O.ld("c6", gq[:], q_norm.partition_broadcast(128), [], ["gq"])
        O.ld("c7", gk[:], k_norm.partition_broadcast(128), [], ["gk"])
        O.ld("c8", go[:], gdn_out_norm.partition_broadcast(128), [], ["go"])
        O.ld("c9", negA[:], a_log.partition_broadcast(128), [], ["negA"])
        O.ld("c10", dtb[:], dt_bias.partition_broadcast(128), [], ["dtb"])
        tmpT = sb("tmpT", [48, 128])

        def ld_T(dst, src2d, k, key):
            O.ld("ldT", tmpT[:k, :], src2d, [], ["tmpT"])
            O.tr(PS[:, 0:k], tmpT[:k, :], ident_f[:k, :k], ["tmpT", "ident_f"], ["b0"])
            O.cp("dve", dst, PS[:, 0:k], ["b0"], [key])

        ld_T(gA[:, :], attn_norm.rearrange("(k p) -> k p", p=128), 8, "gA")
        ld_T(gQ[:, :], q_a_norm.rearrange("(k p) -> k p", p=128), 3, "gQ")
        ld_T(cw[:, :, :].rearrange("p w c -> p (w c)"), conv_w.rearrange("w (c p) -> (w c) p", p=128), 48, "cw")
        O.cp("dve", ident_b[:], ident_f[:], ["ident_f"], ["ident_b"])
        O.mset("dve", ones_f[:], 1.0, ["ones_f"])
        O.mset("dve", eps_t[:], EPS, ["eps_t"])
        O.mset("dve", one_t[:], 1.0, ["one_t"])
        O.act(negA[:], negA[:], AF.Exp, ["negA"], ["negA"])
        O.tsm("dve", negA[:], negA[:], -1.0, ["negA"], ["negA"])
        for kc in range(8):
            O.ld("stg", stg[:], w_in[kc * 128:(kc + 1) * 128, :], [], ["stg"])
            O.tsm("dve", w_in_b[:, kc, :], stg[:], gA[:, kc:kc + 1], ["stg", "gA"], ["w_in_b"])
        for kc in range(3):
            O.ld("stg", stg[:, 0:768], w_uq[kc * 128:(kc + 1) * 128, :], [], ["stg"])
            O.tsm("dve", w_uq_b[:, kc, :], stg[:, 0:768], gQ[:, kc:kc + 1], ["stg", "gQ"], ["w_uq_b"])
        for kc in range(2):
            O.ld("stg", stg[:, 0:512], w_uk[kc * 128:(kc + 1) * 128, :], [], ["stg"])
            O.cp("dve", w_uk_b[:, kc, :], stg[:, 0:512], ["stg"], ["w_uk_b"])
            O.ld("stg", stg[:, 0:512], w_uv[kc * 128:(kc + 1) * 128, :], [], ["stg"])
            O.cp("dve", w_uv_b[:, kc, :], stg[:, 0:512], ["stg"], ["w_uv_b"])
        for w4 in range(4):
            for c in range(12):
                O.tsm("dve", convd[:, w4 * 12 + c, :], ident_f[:], cw[:, w4, c:c + 1],
                      ["ident_f", "cw"], ["convd"])

        xt2 = [sb("xt%d" % i, [128, 1024]) for i in range(2)]
        junk = sb("junk", [128, 1024], BF16)
        scF = sb("scF", [128, 768])
        scB = sb("scB", [128, 512])
        sc2 = sb("sc2", [128, 1024])
        st1 = sb("st1", [128, 64])
        xn = sb("xn", [128, 1024], BF16)
        xT = sb("xT", [128, 8, 128], BF16)
        zTh = sb("zTh", [128, 12, 131], BF16)
        sg2 = [sb("sg%d" % i, [128, 512]) for i in range(2)]
        hs2 = [sb("hs%d" % i, [128, 32]) for i in range(2)]
        k_tm2 = [sb("k_tm%d" % i, [128, 4, 128]) for i in range(2)]
        kbg2 = [sb("kbg%d" % i, [128, 4, 128], BF16) for i in range(2)]
        vb2 = [sb("vb%d" % i, [128, 4, 128], BF16) for i in range(2)]
        gkT2 = [sb("gkT%d" % i, [128, 4, 128], BF16) for i in range(2)]
        gqT2 = [sb("gqT%d" % i, [128, 4, 128], BF16) for i in range(2)]
        cst = sb("cst", [128, 1536])
        qln = sb("qln", [128, 384], BF16)
        qlT = sb("qlT", [128, 3, 128], BF16)
        qtmp = sb("qtmp", [128, 8, 96])
        rtmp = sb("rtmp", [128, 8, 64])
        q_pad = sb("q_pad", [128, 8, 128], BF16)
        qT = sb("qT", [128, 8, 128], BF16)
        lat_n = sb("lat_n", [128, 256])
        lat_b = sb("lat_b", [128, 256], BF16)
        latT = sb("latT", [128, 2, 128], BF16)
        rope_raw = sb("rope_raw", [128, 32])
        cs_t = sb("cs_t", [128, 32])
        rg = sb("rg", [128, 32])
        rr = sb("rr", [128, 32])
        t4 = sb("t4", [128, 64])
        k_pad = sb("k_pad", [128, 8, 128], BF16)
        kTt = sb("kTt", [128, 8, 128], BF16)
        v_b = sb("v_b", [128, 8, 65], BF16)
        csT = sb("csT", [128, 12, 128], BF16)
        qkv_tm = sb("qkv_tm", [128, 12, 128])
        e_dec = sb("e_dec", [128, 4])
        e_last = sb("e_last", [128, 4])
        rs8 = sb("rs8", [128, 8])
        UG = sb("UG", [128, 4, 128])
        dtmp = sb("dtmp", [128, 4, 128])
        MQs = sb("MQs", [128, 4, 128])
        MQT = sb("MQT", [128, 4, 128])
        A_b = sb("A_b", [128, 4, 128], BF16)
        At_b = sb("At_b", [128, 4, 128], BF16)
        TTb = sb("TTb", [128, 4, 2, 128], BF16)
        TTf = sb("TTf", [128, 4, 2, 128])
        X_b = sb("X_b", [128, 4, 128], BF16)
        k_tmb = sb("k_tmb", [128, 4, 128], BF16)
        q_tm = sb("q_tm", [128, 4, 128], BF16)
        kdec = sb("kdec", [128, 4, 128], BF16)
        u_sb = sb("u_sb", [128, 4, 128])
        wT_b = sb("wT_b", [128, 4, 128], BF16)
        vnb = sb("vnb", [128, 4, 128], BF16)
        qkT_b = sb("qkT_b", [128, 4, 128], BF16)
        oi = sb("oi", [128, 4, 128])
        o_sb = sb("o_sb", [128, 4, 128])
        Sg = sb("Sg", [128, 4, 128])
        Sgb = sb("Sgb", [128, 4, 128], BF16)
        gm_b = sb("gm_b", [128, 512], BF16)

        O.mset("dve", q_pad[:], 0.0, ["q_pad"])
        O.mset("dve", k_pad[:], 0.0, ["k_pad"])
        O.mset("dve", v_b[:], 1.0, ["v_b"])

        def rsq(out, in_, scale, n, r, w):
            O.act(out, in_, AF.Ln, r + ["eps_t"], w, scale=scale, bias=eps_t[:n, 0:1])
            O.act(out, out, AF.Exp, w, w, scale=-0.5)

        def rope_ops(o1, o2, x1, x2, cos, sin, ta, tb, tc, td, r, w, tk):
            O.tt("dve", ta, cos, x1, ALU.mult, r, [tk + "a"])
            O.tt("dve", tb, sin, x2, ALU.mult, r, [tk + "b"])
            O.tt("dve", tc, cos, x2, ALU.mult, r, [tk + "c"])
            O.tt("dve", td, sin, x1, ALU.mult, r, [tk + "d"])
            O.tt("dve", o1, ta, tb, ALU.subtract, [tk + "a", tk + "b"], w)
            O.tt("dve", o2, tc, td, ALU.add, [tk + "c", tk + "d"], w)

        tvF = PSB[:, 0:1024].rearrange("p (a b) -> p a b", a=8)

        def kv_build(q, j, n, col0, pos0):
            O.ld("cs", cs_t[:n, :], c_cs[pos0:pos0 + n, :], [], ["cs_t"])
            O.cp("act", lat_b[:n, :], lat_n[:n, :], ["lat_n"], ["lat_b"])
            for kc in range(2):
                O.tr(tvF[:, kc, :n], lat_b[:n, kc * 128:(kc + 1) * 128], ident_b[:n, :n], ["lat_b", "ident_b"], ["b0"])
            O.cp("dve", latT[:, :, :n], tvF[:, 0:2, :n], ["b0"], ["latT"])
            for kc in range(2):
                O.mm(bank(3)[:n, :], latT[:, kc, :n], w_uk_b[:, kc, :], kc == 0, kc == 1, ["latT", "w_uk_b"], ["b3"])
            for kc in range(2):
                O.mm(bank(4)[:n, :], latT[:, kc, :n], w_uv_b[:, kc, :], kc == 0, kc == 1, ["latT", "w_uv_b"], ["b4"])
            yield
            O.cp("act", v_b[:n, :, 0:64], bank(4)[:n, :].rearrange("p (h d) -> p h d", h=8), ["b4"], ["v_b"])
            O.S.dma("sp", "vd", lambda e: e.dma_start(out=V_d[q, j, 0:n, :], in_=v_b[:n, :, :].rearrange("p h d -> p (h d)")),
                    ["v_b"], [])
            O.act(scF[:n, 0:512], bank(3)[:n, :], AF.Square, ["b3"], ["scF"])
            O.red(st1[:n, 0:8], scF[:n, 0:512].rearrange("p (h d) -> p h d", h=8), ["scF"], ["st_k"])
            O.act(junk[:n, 0:32], rope_raw[:n, :], AF.Square, ["rope_raw"], ["junk", "st_kr"], accum_out=st1[:n, 8:9])
            O.tsa("dve", st1[:n, 0:8], st1[:n, 0:8], st1[:n, 8:9], ["st_k", "st_kr"], ["st_k"])
            rsq(st1[:n, 0:8], st1[:n, 0:8], 1.0 / 96, n, ["st_k"], ["st_k"])
            yield
            kn3 = bank(3)[:n, :].rearrange("p (h d) -> p h d", h=8)
            O.tt("dve", rtmp[:n, :, :], kn3, st1[:n, 0:8].unsqueeze(2).to_broadcast([n, 8, 64]), ALU.mult,
                 ["b3", "st_k"], ["rtmp"])
            O.tt("dve", k_pad[:n, :, 0:64], gk[:n, 0:64].unsqueeze(1).to_broadcast([n, 8, 64]), rtmp[:n, :, :], ALU.mult,
                 ["rtmp", "gk"], ["k_pad"])
            O.tt("pool", rg[:n, :], rope_raw[:n, :], gk[:n, 64:96], ALU.mult, ["rope_raw", "gk"], ["rg"])
            rope_ops(rr[:n, 0:16], rr[:n, 16:32], rg[:n, 0:16], rg[:n, 16:32], cs_t[:n, 0:16], cs_t[:n, 16:32],
                     t4[:n, 0:16], t4[:n, 16:32], t4[:n, 32:48], t4[:n, 48:64], ["rg", "cs_t"], ["rr"], "t4")
            O.tt("dve", k_pad[:n, :, 64:96], rr[:n, :].unsqueeze(1).to_broadcast([n, 8, 32]),
                 st1[:n, 0:8].unsqueeze(2).to_broadcast([n, 8, 32]), ALU.mult, ["rr", "st_k"], ["k_pad"])
            yield
            for h in range(8):
                O.tr(tvF[:, h, :n], k_pad[:n, h, :], ident_b[:n, :n], ["k_pad", "ident_b"], ["b0"])
            O.cp("act", kTt[:, :, :n], tvF[:, :, :n], ["b0"], ["kTt"])
            O.S.dma("sp", "ktd", lambda e: e.dma_start(out=KT_d[q, :, :, col0:col0 + n], in_=kTt[:, :, :n]), ["kTt"], [])
            yield

        def cache_kv(q, j, n, row0):
            O.ld("lat", lat_n[:n, :], clat[row0:row0 + n, :], [], ["lat_n"])
            O.ld("rop", rope_raw[:n, :], crope[row0:row0 + n, :], [], ["rope_raw"])
            yield from kv_build(q, j, n, row0, row0)

        def load_x(c):
            O.ld("xt%d" % c["par"], xt2[c["par"]][:c["n"], :], c["x_src"], [], ["xt%d" % c["par"]])

        def front(c, nxt):
            q, j, qi, n, pos0 = c["q"], c["j"], c["qi"], c["n"], c["pos0"]
            need_out, first, last, is_sample, par = c["need_out"], c["first"], c["last"], c["is_sample"], c["par"]
            xt = xt2[par]
            xk, zk, sgk, hk = "xt%d" % par, "zTh", "sg%d" % par, "hs%d" % par
            hs, k_tm, kbg, vb, gkT, gqT = hs2[par], k_tm2[par], kbg2[par], vb2[par], gkT2[par], gqT2[par]
            beta, g_t, gc, ngc, e_gc, bge = (hs[:, 0:4], hs[:, 4:8], hs[:, 8:12], hs[:, 12:16], hs[:, 16:20], hs[:, 20:24])
            if c["load_self"]:
                load_x(c)
            if nxt is not None:
                load_x(nxt)
            O.act(junk[:n, 0:1024], xt[:n, :], AF.Square, [xk], ["junk", "st_x"], accum_out=st1[:n, 16:17])
            rsq(st1[:n, 16:17], st1[:n, 16:17], 1.0 / 1024, n, ["st_x"], ["st_x"])
            O.asc(xn[:n, :], xt[:n, :], st1[:n, 16:17], [xk, "st_x"], ["xn"])
            yield
            for kc in range(8):
                O.tr(tvF[:, kc, :n], xn[:n, kc * 128:(kc + 1) * 128], ident_b[:n, :n], ["xn", "ident_b"], ["b0"])
            O.cp("dve", xT[:, :, :n], tvF[:, :, :n], ["b0"], ["xT"])
            yield
            Z1a, Z1b, Z3, Z4 = bank(1), bank(2, 160), bank(3), bank(2, 8, 256)
            for (dst, c0, c1, key) in ((Z1a, 0, 512, "b1"), (Z1b, 512, 672, "b2"), (Z3, 2208, 2720, "b3"), (Z4, 2720, 2728, "b2x")):
                for kc in range(8):
                    O.mm(dst[:n, :], xT[:, kc, :n], w_in_b[:, kc, c0:c1], kc == 0, kc == 7, ["xT", "w_in_b"], [key])
            yield
            if need_out:
                sg = sg2[par]
                O.act(sg[:n, :], Z3[:n, :], AF.Exp, ["b3"], [sgk], scale=-1.0)
                O.tsa("dve", sg[:n, :], sg[:n, :], 1.0, [sgk], [sgk])
                O.recip(sg[:n, :], sg[:n, :], [sgk], [sgk])
                O.tt("dve", sg[:n, :], sg[:n, :], Z3[:n, :], ALU.mult, [sgk, "b3"], [sgk])
            if first:
                if is_sample:
                    ld_T(cst[:, 0:36], sconv.rearrange("w (c p) -> (w c) p", p=128), 36, "cst")
                    for w3 in range(3):
                        O.cp("dve", zTh[:, :, w3], cst[:, w3 * 12:(w3 + 1) * 12], ["cst"], [zk])
                else:
                    O.mset("dve", zTh[:, :, 0:3], 0.0, [zk])
            else:
                pn = c["prev_n"]
                O.cp("pool", zTh[:, :, 0:3], zTh[:, :, pn:pn + 3], [zk], [zk])
            yield
            for g in range(3):
                bk = 4 if g % 2 == 0 else 3
                zc = bank(bk)[:, 0:4 * n].rearrange("p (c t) -> p c t", c=4)
                for c4 in range(4):
                    cc = 4 * g + c4
                    for kc in range(8):
                        O.mm(zc[:, c4, :], w_in_b[:, kc, 672 + cc * 128: 672 + (cc + 1) * 128], xT[:, kc, :n], kc == 0, kc == 7,
                             ["xT", "w_in_b"], ["b%d" % bk])
                O.cp("act", zTh[:, 4 * g:4 * g + 4, 3:3 + n], zc[:, :, :], ["b%d" % bk], [zk])
                yield
            if last:
                for c3 in range(3):
                    bk = 3 if c3 % 2 == 0 else 4
                    for kc in range(8):
                        O.mm(bank(bk)[:3, :], xT[:, kc, n - 3:n], w_in_b[:, kc, 672 + c3 * 512: 672 + (c3 + 1) * 512],
                             kc == 0, kc == 7, ["xT", "w_in_b"], ["b%d" % bk])
                    O.cp("act", cst[:3, c3 * 512:(c3 + 1) * 512], bank(bk)[:3, :], ["b%d" % bk], ["cst"])
                conv_out = c["conv_out"]
                O.S.dma("sp", "cso", lambda e: e.dma_start(out=conv_out, in_=cst[:3, :]), ["cst"], [])
                yield
            if need_out:
                O.act(junk[:n, 0:384], Z1a[:n, 0:384], AF.Square, ["b1"], ["junk", "st_q"], accum_out=st1[:n, 17:18])
                rsq(st1[:n, 17:18], st1[:n, 17:18], 1.0 / 384, n, ["st_q"], ["st_q"])
                O.act(qln[:n, :], Z1a[:n, 0:384], AF.Copy, ["b1", "st_q"], ["qln"], scale=st1[:n, 17:18])
                for kc in range(3):
                    O.tr(tvF[:, kc, :n], qln[:n, kc * 128:(kc + 1) * 128], ident_b[:n, :n], ["qln", "ident_b"], ["b0"])
                O.cp("dve", qlT[:, :, :n], tvF[:, 0:3, :n], ["b0"], ["qlT"])
                qr = PS[:, 3 * 512: 3 * 512 + 768]
                for (c0, c1) in ((0, 512), (512, 768)):
                    for kc in range(3):
                        O.mm(qr[:n, c0:c1], qlT[:, kc, :n], w_uq_b[:, kc, c0:c1], kc == 0, kc == 2, ["qlT", "w_uq_b"], ["b3", "b4"])
                yield
                O.act(scF[:n, 0:768], qr[:n, :], AF.Square, ["b3", "b4"], ["scF"])
                O.red(st1[:n, 24:32], scF[:n, 0:768].rearrange("p (h d) -> p h d", h=8), ["scF"], ["st_qh"])
                rsq(st1[:n, 24:32], st1[:n, 24:32], 1.0 / 96, n, ["st_qh"], ["st_qh"])
                qr3 = qr[:n, :].rearrange("p (h d) -> p h d", h=8)
                O.tt("dve", qtmp[:n, :, :], qr3, st1[:n, 24:32].unsqueeze(2).to_broadcast([n, 8, 96]), ALU.mult,
                     ["b3", "b4", "st_qh"], ["qtmp"])
                O.tt("dve", qtmp[:n, :, :], gq[:n, :].unsqueeze(1).to_broadcast([n, 8, 96]), qtmp[:n, :, :], ALU.mult,
                     ["qtmp", "gq"], ["qtmp"])
                O.ld("cs", cs_t[:n, :], c_cs[pos0:pos0 + n, :], [], ["cs_t"])
                cosb = cs_t[:n, 0:16].unsqueeze(1).to_broadcast([n, 8, 16])
                sinb = cs_t[:n, 16:32].unsqueeze(1).to_broadcast([n, 8, 16])
                rope_ops(q_pad[:n, :, 64:80], q_pad[:n, :, 80:96], qtmp[:n, :, 64:80], qtmp[:n, :, 80:96], cosb, sinb,
                         rtmp[:n, :, 0:16], rtmp[:n, :, 16:32], rtmp[:n, :, 32:48], rtmp[:n, :, 48:64],
                         ["qtmp", "cs_t"], ["q_pad"], "rtmp")
                O.cp("dve", q_pad[:n, :, 0:64], qtmp[:n, :, 0:64], ["qtmp"], ["q_pad"])
                yield
                for h in range(8):
                    O.tr(tvF[:, h, :n], q_pad[:n, h, :], ident_b[:n, :n], ["q_pad", "ident_b"], ["b0"])
                O.cp("act", qT[:, :, :n], tvF[:, :, :n], ["b0"], ["qT"])
                O.S.dma("sp", "qtd", lambda e: e.dma_start(out=qT_d[q, qi, :, :, 0:n], in_=qT[:, :, :n]), ["qT"], [])
                yield
            kvl = PS[:, 512 + 384: 512 + 640]
            O.act(junk[:n, 0:256], kvl[:n, :], AF.Square, ["b1", "b2"], ["junk", "st_kv"], accum_out=st1[:n, 18:19])
            rsq(st1[:n, 18:19], st1[:n, 18:19], 1.0 / 256, n, ["st_kv"], ["st_kv"])
            O.stt("dve", lat_n[:n, :], kvl[:n, :], st1[:n, 18:19], gkv[:n, :], ALU.mult, ALU.mult,
                  ["b1", "b2", "st_kv", "gkv"], ["lat_n"])
            O.cp("act", rope_raw[:n, :], Z1b[:n, 128:160], ["b2"], ["rope_raw"])
            lat_out, rope_out = c["lat_out"], c["rope_out"]
            O.S.dma("sp", "lato", lambda e: e.dma_start(out=lat_out, in_=lat_n[:n, :]), ["lat_n"], [])
            O.S.dma("sp", "ropeo", lambda e: e.dma_start(out=rope_out, in_=rope_raw[:n, :]), ["rope_raw"], [])
            yield
            yield from kv_build(q, j, n, pos0, pos0)
            O.act(beta[:n, :], Z4[:n, 0:4], AF.Exp, ["b2x"], [hk], scale=-1.0)
            O.tsa("dve", beta[:n, :], beta[:n, :], 1.0, [hk], [hk])
            O.recip(beta[:n, :], beta[:n, :], [hk], [hk])
            O.tt("dve", g_t[:n, :], Z4[:n, 4:8], dtb[:n, :], ALU.add, ["b2x", "dtb"], [hk])
            O.act(g_t[:n, :], g_t[:n, :], AF.Exp, [hk], [hk])
            O.act(g_t[:n, :], g_t[:n, :], AF.Ln, [hk, "one_t"], [hk], bias=one_t[:n, 0:1])
            O.tt("dve", g_t[:n, :], g_t[:n, :], negA[:n, :], ALU.mult, [hk, "negA"], [hk])
            gcp = bank(2, 4, 300)
            O.mm(gcp[:n, :], U_f[:n, :n], g_t[:n, :], True, True, ["U_f", hk], ["b2y"])
            O.cp("dve", gc[:n, :], gcp[:n, :], ["b2y"], [hk])
            O.tsm("dve", ngc[:n, :], gc[:n, :], -1.0, [hk], [hk])
            O.act(e_gc[:n, :], gc[:n, :], AF.Exp, [hk], [hk])
            O.tt("dve", bge[:n, :], beta[:n, :], e_gc[:n, :], ALU.mult, [hk], [hk])
            yield
            g_lo = 0 if need_out else 1
            c_lo = 4 * g_lo
            for g in range(g_lo, 3):
                bk = 3 if g % 2 == 0 else 4
                cps = bank(bk)[:, 0:4 * n].rearrange("p (c t) -> p c t", c=4)
                for c4 in range(4):
                    cc = 4 * g + c4
                    for w4 in range(4):
                        O.mm(cps[:, c4, :], convd[:, w4 * 12 + cc, :], zTh[:, cc, w4:w4 + n], w4 == 0, w4 == 3,
                             ["convd", zk], ["b%d" % bk])
                sv = scB[:, 0:4 * n].rearrange("p (c t) -> p c t", c=4)
                O.act(sv, cps, AF.Exp, ["b%d" % bk], ["scB"], scale=-1.0)
                O.tsa("dve", sv, sv, 1.0, ["scB"], ["scB"])
                O.recip(sv, sv, ["scB"], ["scB"])
                O.tt("dve", csT[:, 4 * g:4 * g + 4, :n], cps, sv, ALU.mult, ["b%d" % bk, "scB"], ["csT"])
                yield
            tvg = PSB[:, 0:512].rearrange("p (a b) -> p a b", a=4)
            for g in range(g_lo, 3):
                for c4 in range(4):
                    O.tr(tvg[:n, c4, :], csT[:, 4 * g + c4, :n], ident_b[:, :], ["csT", "ident_b"], ["b0"])
                O.cp("act", qkv_tm[:n, 4 * g:4 * g + 4, :], tvg[:n, :, :], ["b0"], ["qkv_tm"])
            yield
            sc2v = sc2[:, :].rearrange("p (c t) -> p c t", c=8)
            O.act(sc2v[:n, c_lo:8, :], qkv_tm[:n, c_lo:8, :], AF.Square, ["qkv_tm"], ["sc2"])
            O.red(rs8[:n, c_lo:8], sc2v[:n, c_lo:8, :], ["sc2"], ["rs8"])
            O.act(rs8[:n, c_lo:8], rs8[:n, c_lo:8], AF.Ln, ["rs8", "eps_t"], ["rs8"], bias=eps_t[:n, 0:1])
            O.act(rs8[:n, c_lo:8], rs8[:n, c_lo:8], AF.Exp, ["rs8"], ["rs8"], scale=-0.5)
            if need_out:
                O.tsm("dve", rs8[:n, 0:4], rs8[:n, 0:4], 128.0 ** -0.5, ["rs8"], ["rs8"])
                O.tt("dve", q_tm[:n, :, :], qkv_tm[:n, 0:4, :], rs8[:n, 0:4].unsqueeze(2).to_broadcast([n, 4, 128]), ALU.mult,
                     ["qkv_tm", "rs8"], ["q_tm"])
            O.tt("dve", k_tm[:n, :, :], qkv_tm[:n, 4:8, :], rs8[:n, 4:8].unsqueeze(2).to_broadcast([n, 4, 128]), ALU.mult,
                 ["qkv_tm", "rs8"], ["k_tm%d" % par])
            O.tt("dve", k_tmb[:n, :, :], qkv_tm[:n, 4:8, :], rs8[:n, 4:8].unsqueeze(2).to_broadcast([n, 4, 128]), ALU.mult,
                 ["qkv_tm", "rs8"], ["k_tmb"])
            yield
            for h in range(4):
                O.tsm("dve", kbg[:n, h, :], k_tm[:n, h, :], bge[:n, h:h + 1], ["k_tm%d" % par, hk], ["kbg%d" % par])
                O.tsm("pool", vb[:n, h, :], qkv_tm[:n, 8 + h, :], beta[:n, h:h + 1], ["qkv_tm", hk], ["vb%d" % par])
            for h in range(4):
                O.tr(tvF[:, h, :n], k_tmb[:n, h, :], ident_b[:n, :n], ["k_tmb", "ident_b"], ["b0"])
            if need_out:
                for h in range(4):
                    O.tr(tvF[:, 4 + h, :n], q_tm[:n, h, :], ident_b[:n, :n], ["q_tm", "ident_b"], ["b0"])
                O.cp("act", gqT[:, :, :n], tvF[:, 4:8, :n], ["b0"], ["gqT%d" % par])
            O.cp("dve", gkT[:, :, :n], tvF[:, 0:4, :n], ["b0"], ["gkT%d" % par])
            yield

        tvB = PSB[:, 5 * 1024: 6 * 1024].rearrange("p (a b) -> p a b", a=8)

        def back(c):
            q, j, qi, n, pos0 = c["q"], c["j"], c["qi"], c["n"], c["pos0"]
            need_out, first, last, is_sample, par = c["need_out"], c["first"], c["last"], c["is_sample"], c["par"]
            sgk, hk = "sg%d" % par, "hs%d" % par
            hs, k_tm, kbg, vb, gkT, gqT = hs2[par], k_tm2[par], kbg2[par], vb2[par], gkT2[par], gqT2[par]
            beta, g_t, gc, ngc, e_gc, bge = (hs[:, 0:4], hs[:, 4:8], hs[:, 8:12], hs[:, 12:16], hs[:, 16:20], hs[:, 20:24])
            kk, kbk, vbk, gkk, gqk = "k_tm%d" % par, "kbg%d" % par, "vb%d" % par, "gkT%d" % par, "gqT%d" % par
            O.tt("dve", UG[:n, :, :n], U_f[:n, :n].unsqueeze(1).to_broadcast([n, 4, n]),
                 g_t[:n, :].unsqueeze(2).to_broadcast([n, 4, n]), ALU.mult, ["U_f", hk], ["UG"])
            Grow = bank(6)[:, 0:4 * n].rearrange("p (h t) -> p h t", h=4)
            O.mm(Grow, ones_f[:n, :], UG[:n, :, :n], True, True, ["ones_f", "UG"], ["b6"])
            yield
            O.tt("dve", e_dec[:n, :], Grow[:n, :, n - 1], gc[:n, :], ALU.subtract, ["b6", hk], ["e_dec"])
            O.act(e_dec[:n, :], e_dec[:n, :], AF.Exp, ["e_dec"], ["e_dec"])
            O.act(e_last[:, :], Grow[:, :, n - 1], AF.Exp, ["b6"], ["e_last"])
            yield
            for h in range(4):
                O.tt("dve", dtmp[:n, h, :n], Grow[:n, h, :], mLs[:n, :n], ALU.subtract, ["b6", "mLs"], ["dtmp"])
            for h in range(4):
                O.act(MQs[:n, h, :n], dtmp[:n, h, :n], AF.Exp, ["dtmp", hk], ["MQs"], bias=gc[:n, h:h + 1], scale=-1.0)
            yield
            if need_out:
                for h in range(4):
                    O.tt("dve", dtmp[:n, h, :n], Grow[:n, h, :], mU[:n, :n], ALU.add, ["b6", "mU"], ["dtmp"])
                for h in range(4):
                    O.act(MQT[:n, h, :n], dtmp[:n, h, :n], AF.Exp, ["dtmp", hk], ["MQT"], bias=ngc[:n, h:h + 1])
                yield
            for h in range(4):
                O.tsm("pool", kdec[:n, h, :], k_tm[:n, h, :], e_dec[:n, h:h + 1], [kk, "e_dec"], ["kdec"])
            Gp = bank(6)[:, 0:4 * n].rearrange("p (h t) -> p h t", h=4)
            for h in range(4):
                O.mm(Gp[:n, h, :], gkT[:, h, :n], gkT[:, h, :n], True, True, [gkk], ["b6"])
            yield
            for h in range(4):
                O.tsm("dve", dtmp[:n, h, :n], Gp[:n, h, :], beta[:n, h:h + 1], ["b6", hk], ["dtmp"])
            O.tt("dve", A_b[:n, :, :n], dtmp[:n, :, :n], MQs[:n, :, :n], ALU.mult, ["dtmp", "MQs"], ["A_b"])
            yield
            tv7 = PSB[:, 7 * 1024: 8 * 1024].rearrange("p (a b) -> p a b", a=8)
            for h in range(4):
                O.tr(tv7[:n, h, :n], A_b[:n, h, :n], ident_b[:n, :n], ["A_b", "ident_b"], ["b7"])
            O.cp("act", At_b[:n, :, :n], tv7[:n, 0:4, :n], ["b7"], ["At_b"])
            for s2 in range(2):
                O.cp("pool", TTb[:n, :, s2, :n], ident_f[:n, :n].unsqueeze(1).to_broadcast([n, 4, n]), ["ident_f"], ["TTb"])
                O.cp("pool", TTf[:n, :, s2, :n], ident_f[:n, :n].unsqueeze(1).to_broadcast([n, 4, n]), ["ident_f"], ["TTf"])
            yield
            Xp = bank(5)[:, 0:4 * n].rearrange("p (h t) -> p h t", h=4)
            Yp = PS[:, 6 * 512: 8 * 512].rearrange("p (h s t) -> p h s t", h=4, s=2)
            l = 0
            while (1 << l) < n:
                for h in range(4):
                    O.mm(Xp[:n, h, :], At_b[:n, h, :n], TTb[:n, h, 0, :n], True, True, ["At_b", "TTb"], ["b5"])
                for h in range(4):
                    O.tt("dve", X_b[:n, h, :n], Xp[:n, h, :], lvl[:n, l, :n], ALU.mult, ["b5", "lvl"], ["X_b"])
                yield
                for h in range(4):
                    O.mm(Yp[:n, h, 0, :n], TTb[:n, h, 1, :n], X_b[:n, h, :n], True, True, ["TTb", "X_b"], ["b6", "b7"])
                    O.mm(Yp[:n, h, 1, :n], X_b[:n, h, :n], TTb[:n, h, 1, :n], True, True, ["TTb", "X_b"], ["b6", "b7"])
                O.tt("dve", TTf[:n, :, :, :n], TTf[:n, :, :, :n], Yp[:n, :, :, :n], ALU.subtract, ["TTf", "b6", "b7"], ["TTf"])
                O.cp("act", TTb[:n, :, :, :n], TTf[:n, :, :, :n], ["TTf"], ["TTb"])
                l += 1
                yield
            up_ = bank(5)[:, :].rearrange("p (h t) -> p h t", h=4)
            wTp = bank(6)[:, 0:4 * n].rearrange("p (h t) -> p h t", h=4)
            for h in range(4):
                O.mm(up_[:n, h, :], TTb[:n, h, 1, :n], vb[:n, h, :], True, True, ["TTb", vbk], ["b5"])
            for h in range(4):
                O.mm(wTp[:, h, :], kbg[:n, h, :], TTb[:n, h, 1, :n], True, True, ["TTb", kbk], ["b6"])
            yield
            O.cp("act", u_sb[:n, :, :], up_[:n, :, :], ["b5"], ["u_sb"])
            O.cp("act", wT_b[:, :, :n], wTp[:, :, :], ["b6"], ["wT_b"])
            yield
            if first:
                if is_sample:
                    O.ld("sg", Sg[:, :, :], sgdn.rearrange("h k v -> k h v"), [], ["Sg"])
                else:
                    O.mset("pool", Sg[:, :, :], 0.0, ["Sg"])
                O.cp("act", Sgb[:, :, :], Sg[:, :, :], ["Sg"], ["Sgb"])
            wSp = bank(7)[:, :].rearrange("p (h t) -> p h t", h=4)
            qSp = bank(5)[:, :].rearrange("p (h t) -> p h t", h=4)
            qkTp = bank(6)[:, 0:4 * n].rearrange("p (h t) -> p h t", h=4)
            for h in range(4):
                O.mm(wSp[:n, h, :], wT_b[:, h, :n], Sgb[:, h, :], True, True, ["wT_b", "Sgb"], ["b7"])
            O.tt("dve", vnb[:n, :, :], u_sb[:n, :, :], wSp[:n, :, :], ALU.subtract, ["u_sb", "b7"], ["vnb"])
            yield
            if need_out:
                for h in range(4):
                    O.mm(qSp[:n, h, :], gqT[:, h, :n], Sgb[:, h, :], True, True, [gqk, "Sgb"], ["b5"])
                for h in range(4):
                    O.mm(qkTp[:n, h, :], gkT[:, h, :n], gqT[:, h, :n], True, True, [gkk, gqk], ["b6"])
                for h in range(4):
                    O.tsm("dve", oi[:n, h, :], qSp[:n, h, :], e_gc[:n, h:h + 1], ["b5", hk], ["oi"])
                O.tt("dve", qkT_b[:n, :, :n], qkTp[:n, :, :], MQT[:n, :, :n], ALU.mult, ["b6", "MQT"], ["qkT_b"])
                o2p = bank(7)[:, :].rearrange("p (h t) -> p h t", h=4)
                for h in range(4):
                    O.mm(o2p[:n, h, :], qkT_b[:n, h, :n], vnb[:n, h, :], True, True, ["qkT_b", "vnb"], ["b7"])
                O.tt("dve", o_sb[:n, :, :], oi[:n, :, :], o2p[:n, :, :], ALU.add, ["oi", "b7"], ["o_sb"])
            yield
            dSp = bank(5)[:, :].rearrange("p (h t) -> p h t", h=4)
            for h in range(4):
                O.mm(dSp[:, h, :], kdec[:n, h, :], vnb[:n, h, :], True, True, ["kdec", "vnb"], ["b5"])
            for h in range(4):
                O.tsm("dve", Sg[:, h, :], Sg[:, h, :], e_last[:, h:h + 1], ["Sg", "e_last"], ["Sg"])
            O.tt("dve", Sg[:, :, :], Sg[:, :, :], dSp[:, :, :], ALU.add, ["Sg", "b5"], ["Sg"])
            O.cp("act", Sgb[:, :, :], Sg[:, :, :], ["Sg"], ["Sgb"])
            if last:
                S_out = c["S_out"]
                O.S.dma("sp", "so", lambda e: e.dma_start(out=S_out.rearrange("h k v -> k h v"), in_=Sg[:, :, :]), ["Sg"], [])
            if need_out:
                sg = sg2[par]
                O.act(sc2[:n, 0:512], o_sb[:n, :, :].rearrange("p h d -> p (h d)"), AF.Square, ["o_sb"], ["sc2"])
                O.red(st1[:n, 40:44], sc2[:n, 0:512].rearrange("p (h d) -> p h d", h=4), ["sc2"], ["st_o"])
                rsq(st1[:n, 40:44], st1[:n, 40:44], 1.0 / 128, n, ["st_o"], ["st_o"])
                for h in range(4):
                    O.tsm("dve", o_sb[:n, h, :], o_sb[:n, h, :], st1[:n, 40 + h:41 + h], ["o_sb", "st_o"], ["o_sb"])
                O.tt("dve", o_sb[:n, :, :], go[:n, :].unsqueeze(1).to_broadcast([n, 4, 128]), o_sb[:n, :, :], ALU.mult,
                     ["o_sb", "go"], ["o_sb"])
                O.tt("dve", gm_b[:n, :], o_sb[:n, :, :].rearrange("p h d -> p (h d)"), sg[:n, :], ALU.mult, ["o_sb", sgk], ["gm_b"])
                O.S.dma("sp", "gmd", lambda e: e.dma_start(out=gm_d[q, qi, 0:n, :], in_=gm_b[:n, :]), ["gm_b"], [])
            yield

        jobs = []
        if _STOP != 'prep':
            for s in range(NP):
                jobs.append(dict(q=s, j=0, qi=0, n=16, pos0=0, x_src=meta, need_out=False, first=True, last=False, is_sample=False,
                                 lat_out=p_lat[s, 0:16, :], rope_out=p_rope[s, 0:16, :], conv_out=None, S_out=None, prev_n=0))
                for i in range(NFT):
                    jobs.append(dict(q=s, j=i + 1, qi=i, n=128, pos0=16 + 128 * i, x_src=xp[s, 128 * i:128 * (i + 1), :],
                                     need_out=True, first=False, last=(i == NFT - 1), is_sample=False,
                                     lat_out=p_lat[s, 16 + 128 * i:16 + 128 * (i + 1), :],
                                     rope_out=p_rope[s, 16 + 128 * i:16 + 128 * (i + 1), :],
                                     conv_out=p_conv[s], S_out=p_S[s], prev_n=(16 if i == 0 else 128)))
            jobs.append(dict(q=NP, j=CFT + 1, qi=0, n=64, pos0=LC, x_src=xs, need_out=True, first=True, last=True, is_sample=True,
                             lat_out=s_lat, rope_out=s_rope, conv_out=s_conv, S_out=s_S, prev_n=0))
        for k, c in enumerate(jobs):
            c["par"] = k % 2
            c["load_self"] = (k == 0)
        cache_jobs = [(NP, 0, 16, 0)] + [(NP, i + 1, 128, 16 + 128 * i) for i in range(CFT)]
        if _STOP == 'prep':
            cache_jobs = []

        def run_all(g):
            for _ in g:
                pass

        def interleave(ga, gb):
            da = db = False
            while not (da and db):
                if not da:
                    try:
                        next(ga)
                    except StopIteration:
                        da = True
                if not db:
                    try:
                        next(gb)
                    except StopIteration:
                        db = True

        def front_plus(k):
            yield from front(jobs[k], jobs[k + 1] if k + 1 < len(jobs) else None)
            if cache_jobs and (k % 2 == 1 or len(jobs) - k <= len(cache_jobs)):
                cj = cache_jobs.pop(0)
                yield from cache_kv(*cj)

        if jobs:
            run_all(front_plus(0))
            for k in range(len(jobs)):
                if k + 1 < len(jobs):
                    interleave(back(jobs[k]), front_plus(k + 1))
                else:
                    run_all(back(jobs[k]))
        while cache_jobs:
            run_all(cache_kv(*cache_jobs.pop(0)))
        S.emit()
    if _STOP in ('prep', '1a'):
        return nc

    nc.all_engine_barrier()

    with ExitStack() as st:
        def sb(name, shape, dt=F32):
            return st.enter_context(nc.sbuf_tensor(name, shape, dt))

        PS = st.enter_context(nc.psum_tensor("psb", [128, 4096], F32))
        PSB = PS[:].bitcast(BF16)

        def bank(b, w=512, off=0):
            return PS[:, b * 512 + off: b * 512 + off + w]

        S = Sched(nc, "b")
        O = Ops(S)
        KTc = sb("KTc", [128, 8, NCMAX], BF16)
        Vc = sb("Vc", [128, NKT, 520], BF16)
        w_out_b = sb("w_out_b", [128, 8, 1024], BF16)
        stg = sb("stgb", [128, 1024])
        gM = sb("gM", [128, 4])
        ident_f = sb("ident_fb", [128, 128])
        ident_b = sb("ident_bb", [128, 128], BF16)
        rowa = sb("rowa", [1, 128], BF16)
        rowb = sb("rowb", [1, 128], BF16)
        eps_t = sb("eps_tb", [128, 1])
        qTt2 = [sb("qTt%d" % i, [128, 8, 128], BF16) for i in range(2)]
        mix_b2 = [sb("mix_b%d" % i, [128, 1024], BF16) for i in range(2)]
        mixT = sb("mixT", [128, 8, 128], BF16)
        xt2 = [sb("xtb%d" % i, [128, 1024]) for i in range(2)]
        x1s2 = [sb("x1s%d" % i, [128, 1024]) for i in range(2)]
        P_sb = sb("P_sb", [128, 3, 4, 128], BF16)
        attn_tm = sb("attn_tm", [128, 512])
        junk = sb("junkb", [128, 512], BF16)
        st1 = sb("st1b", [128, 16])

        O.ld("c0", ident_f[:], c_ident, [], ["ident_f"])
        O.cp("dve", ident_b[:], ident_f[:], ["ident_f"], ["ident_b"])
        O.mset("dve", eps_t[:], EPS, ["eps_t"])
        O.mset("dve", rowa[:, :], 1.0, ["rowa"])
        O.mset("dve", rowa[:, 0:64], 0.0, ["rowa"])
        O.mset("dve", rowb[:, :], -30000.0, ["rowb"])
        O.mset("dve", rowb[:, 64:128], 0.0, ["rowb"])
        tmpT = sb("tmpTb", [8, 128])
        O.ld("ldT", tmpT[:4, :], mla_out_norm.rearrange("(k p) -> k p", p=128), [], ["tmpT"])
        O.tr(PS[:, 0:4], tmpT[:4, :], ident_f[:4, :4], ["tmpT", "ident_f"], ["bs0"])
        O.cp("dve", gM[:, :], PS[:, 0:4], ["bs0"], ["gM"])
        for kc in range(8):
            O.ld("stg", stg[:], w_out[kc * 128:(kc + 1) * 128, :], [], ["stg"])
            if kc < 4:
                O.tsm("dve", w_out_b[:, kc, :], stg[:], gM[:, kc:kc + 1], ["stg", "gM"], ["w_out_b"])
            else:
                O.cp("dve", w_out_b[:, kc, :], stg[:], ["stg"], ["w_out_b"])

        SBANK = (0, 1, 7)
        tcount = [0]

        def attn_seq(q, tiles):
            units = []
            for ti, (qi, n, x_src, keytiles, diag_j, x1_idx) in enumerate(tiles):
                groups = []
                cur = []
                for kt in keytiles:
                    if kt[1] != 128:
                        if cur:
                            groups.append(cur)
                            cur = []
                        groups.append([kt])
                    else:
                        cur.append(kt)
                        if len(cur) == 4:
                            groups.append(cur)
                            cur = []
                if cur:
                    groups.append(cur)
                for h in range(8):
                    for gi, g in enumerate(groups):
                        units.append((ti, h, g, gi == 0, gi == len(groups) - 1))
            tpar = {}

            def prologue(ti):
                qi, n, x_src, keytiles, diag_j, x1_idx = tiles[ti]
                p = tcount[0] % 2
                tcount[0] += 1
                tpar[ti] = p
                O.ld("qt%d" % p, qTt2[p][:, :, :n], qT_d[q, qi, :, :, 0:n], [], ["qTt%d" % p])
                O.ld("gm%d" % p, mix_b2[p][:n, 512:1024], gm_d[q, qi, 0:n, :], [], ["mix_g%d" % p])
                O.ld("xt%d" % p, xt2[p][:n, :], x_src, [], ["xt%d" % p])

            def qk(ui):
                ti, h, g, fg, lg = units[ui]
                qi, n, x_src, keytiles, diag_j, x1_idx = tiles[ti]
                if ti not in tpar:
                    prologue(ti)
                p = tpar[ti]
                par = ui % 3
                Sp = bank(SBANK[par])[:, :].rearrange("p (s t) -> p s t", s=4)
                for s_, (j, nk, col0) in enumerate(g):
                    dg = (j == diag_j)
                    O.mm(Sp[:nk, s_, :n], KTc[0:96, h, col0:col0 + nk], qTt2[p][0:96, h, :n], True, not dg,
                         ["KTc", "qTt%d" % p], ["bs%d" % par])
                    if dg:
                        O.mm(Sp[:nk, s_, :n], rowa[0:1, :nk], rowb[0:1, :n], False, True, ["rowa", "rowb"], ["bs%d" % par])

            def expv(ui):
                ti, h, g, fg, lg = units[ui]
                qi, n, x_src, keytiles, diag_j, x1_idx = tiles[ti]
                p = tpar[ti]
                par = ui % 3
                Sp = bank(SBANK[par])[:, :].rearrange("p (s t) -> p s t", s=4)
                Op = bank(2 + (h % 2))
                ok = "bo%d" % (h % 2)
                nk0 = g[0][1]
                O.act(P_sb[:nk0, par, 0:len(g), :n], Sp[:nk0, 0:len(g), :n], AF.Exp, ["bs%d" % par], ["P%d" % par], scale=ATT_SCALE)
                for s_, (j, nk, col0) in enumerate(g):
                    O.mm(Op[:n, 0:65], P_sb[:nk, par, s_, :n], Vc[:nk, j, h * 65:(h + 1) * 65], fg and s_ == 0,
                         lg and s_ == len(g) - 1, ["P%d" % par, "Vc"], [ok])
                if lg:
                    O.recip(st1[:n, h:h + 1], Op[:n, 64:65], [ok], ["st_r%d" % h])
                    O.tsm("dve", attn_tm[:n, h * 64:(h + 1) * 64], Op[:n, 0:64], st1[:n, h:h + 1], [ok, "st_r%d" % h], ["attn_tm"])
                    if h == 7:
                        epilogue(ti)

            def epilogue(ti):
                qi, n, x_src, keytiles, diag_j, x1_idx = tiles[ti]
                p = tpar[ti]
                mix_b = mix_b2[p]
                O.act(junk[:n, :], attn_tm[:n, :], AF.Square, ["attn_tm"], ["junk", "st_a"], accum_out=st1[:n, 8:9])
                O.act(st1[:n, 8:9], st1[:n, 8:9], AF.Ln, ["st_a", "eps_t"], ["st_a"], scale=1.0 / 512, bias=eps_t[:n, 0:1])
                O.act(st1[:n, 8:9], st1[:n, 8:9], AF.Exp, ["st_a"], ["st_a"], scale=-0.5)
                O.asc(mix_b[:n, 0:512], attn_tm[:n, :], st1[:n, 8:9], ["attn_tm", "st_a"], ["mix_a%d" % p])
                tv = PSB[:, 4 * 1024: 5 * 1024].rearrange("p (a b) -> p a b", a=8)
                for c in range(8):
                    O.tr(tv[:, c, :n], mix_b[:n, c * 128:(c + 1) * 128], ident_b[:n, :n],
                         ["mix_a%d" % p, "mix_g%d" % p, "ident_b"], ["b4"])
                O.cp("dve", mixT[:, :, :n], tv[:, :, :n], ["b4"], ["mixT"])
                x1p = PS[:, 5 * 512: 7 * 512]
                for half in range(2):
                    for c in range(8):
                        O.mm(x1p[:n, half * 512:(half + 1) * 512], mixT[:, c, :n], w_out_b[:, c, half * 512:(half + 1) * 512],
                             c == 0, c == 7, ["mixT", "w_out_b"], ["b56"])
                O.tt("dve", x1s2[p][:n, :], x1p[:n, :], xt2[p][:n, :], ALU.add, ["b56", "xt%d" % p], ["x1s%d" % p])
                O.S.dma("pool", "x1o%d" % p, lambda e: e.dma_start(out=x1_d[x1_idx, 0:n, :], in_=x1s2[p][:n, :]), ["x1s%d" % p], [])

            LOOK = 2
            for ui in range(min(LOOK, len(units))):
                qk(ui)
            for ui in range(len(units)):
                if ui + LOOK < len(units):
                    qk(ui + LOOK)
                expv(ui)

        def load_cache(q, ncols, kts):
            for h in range(8):
                O.ld("ktc", KTc[:, h, 0:ncols], KT_d[q, :, h, 0:ncols], [], ["KTc"])
            full = [j for (j, nk, c0) in kts if nk == 128]
            if full:
                for j0 in range(full[0], full[-1] + 1, 4):
                    j1 = min(j0 + 4, full[-1] + 1)
                    O.ld("vc", Vc[:, j0:j1, :], V_d[q, j0:j1, :, :].rearrange("t p c -> p t c"), [], ["Vc"])
            for (j, nk, c0) in kts:
                if nk != 128:
                    O.ld("vc", Vc[:nk, j, :], V_d[q, j, 0:nk, :], [], ["Vc"])

        for s in range(NP):
            load_cache(s, LP, [(0, 16, 0)] + [(jj + 1, 128, 16 + 128 * jj) for jj in range(NFT)])
            tl = []
            for i in range(NFT):
                kts = [(0, 16, 0)] + [(jj + 1, 128, 16 + 128 * jj) for jj in range(i + 1)]
                tl.append((i, 128, xp[s, 128 * i:128 * (i + 1), :], kts, i + 1, s * NFT + i))
            attn_seq(s, tl)
        kts = [(0, 16, 0)] + [(jj + 1, 128, 16 + 128 * jj) for jj in range(CFT)] + [(CFT + 1, 64, LC)]
        load_cache(NP, LC + 64, kts)
        attn_seq(NP, [(0, 64, xs, kts, -1, NP * NFT)])
        S.emit()
    if _STOP == '1b':
        return nc

    nc.all_engine_barrier()

    with ExitStack() as st:
        def sb(name, shape, dt=F32):
            return st.enter_context(nc.sbuf_tensor(name, shape, dt))

        PS = st.enter_context(nc.psum_tensor("psc", [128, 4096], F32))
        PSB = PS[:].bitcast(BF16)
        S = Sched(nc, "c")
        O = Ops(S)
        w_up_b = sb("w_up_b", [128, 8, 4096], BF16)
        w_dn_b = sb("w_dn_b", [128, 32, 1024], BF16)
        gP = sb("gP", [128, 8])
        ident_f = sb("ident_fc", [128, 128])
        ident_b = sb("ident_bc", [128, 128], BF16)
        eps_t = sb("eps_tc", [128, 1])
        x1t2 = [sb("x1t%d" % i, [128, 2, 1024]) for i in range(2)]
        hn2 = [sb("hn%d" % i, [128, 1024], BF16) for i in range(2)]
        hT2 = [sb("hT%d" % i, [128, 8, 256], BF16) for i in range(2)]
        rl = sb("rl", [128, 3, 512])
        upT = sb("upT", [128, 32, 256], BF16)
        ys2 = [sb("ys%d" % i, [128, 1024]) for i in range(2)]
        junk = sb("junkc", [128, 1024], BF16)
        st1 = sb("st1c", [128, 8])

        O.ld("c0", ident_f[:], c_ident, [], ["ident_f"])
        O.cp("dve", ident_b[:], ident_f[:], ["ident_f"], ["ident_b"])
        O.mset("dve", eps_t[:], EPS, ["eps_t"])
        tmpT = sb("tmpTc", [8, 128])
        O.ld("ldT", tmpT[:8, :], mlp_norm.rearrange("(k p) -> k p", p=128), [], ["tmpT"])
        O.tr(PS[:, 0:8], tmpT[:8, :], ident_f[:8, :8], ["tmpT", "ident_f"], ["b0"])
        O.cp("dve", gP[:, :], PS[:, 0:8], ["b0"], ["gP"])
        for kc in range(8):
            sp_ = kc % 2
            stg = x1t2[sp_][:, :, :].rearrange("p a d -> p (a d)")
            for hf in range(2):
                O.ld("stg%d" % sp_, stg[:, :], w_up[kc * 128:(kc + 1) * 128, hf * 2048:(hf + 1) * 2048], [], ["x1t%d" % sp_])
                O.tsm("dve", w_up_b[:, kc, hf * 2048:(hf + 1) * 2048], stg[:, :], gP[:, kc:kc + 1],
                      ["x1t%d" % sp_, "gP"], ["w_up_b"])
        for c2 in range(16):
            sp_ = c2 % 2
            O.ld("stg%d" % sp_, x1t2[sp_][:, :, :], w_down[c2 * 256:(c2 + 1) * 256, :].rearrange("(f p) d -> p f d", p=128),
                 [], ["x1t%d" % sp_])
            O.cp("dve", w_dn_b[:, 2 * c2:2 * c2 + 2, :], x1t2[sp_][:, :, :], ["x1t%d" % sp_], ["w_dn_b"])

        def mlp_front_a(bi, subs):
            p = bi % 2
            for si, (idx, n, y_out) in enumerate(subs):
                O.ld("x1%d" % p, x1t2[p][:n, si, :], x1_d[idx, 0:n, :], [], ["x1t%d" % p])
            for si, (idx, n, y_out) in enumerate(subs):
                O.act(junk[:n, :], x1t2[p][:n, si, :], AF.Square, ["x1t%d" % p], ["junk", "st"], accum_out=st1[:n, si:si + 1])
                O.act(st1[:n, si:si + 1], st1[:n, si:si + 1], AF.Ln, ["st", "eps_t"], ["st"], scale=1.0 / 1024, bias=eps_t[:n, 0:1])
                O.act(st1[:n, si:si + 1], st1[:n, si:si + 1], AF.Exp, ["st"], ["st"], scale=-0.5)
                O.asc(hn2[si][:n, :], x1t2[p][:n, si, :], st1[:n, si:si + 1], ["x1t%d" % p, "st"], ["hn%d" % si])

        def mlp_front_b(bi, subs):
            p = bi % 2
            for si, (idx, n, y_out) in enumerate(subs):
                tv = PSB[:, 0:1024].rearrange("p (a b) -> p a b", a=8)
                for kc in range(8):
                    O.tr(tv[:, kc, :n], hn2[si][:n, kc * 128:(kc + 1) * 128], ident_b[:n, :n], ["hn%d" % si, "ident_b"], ["b0"])
                O.cp("dve", hT2[p][:, :, si * 128:si * 128 + n], tv[:, :, :n], ["b0"], ["hT%d" % p])

        def mlp_body(bi, subs, mid=None):
            p = bi % 2
            nt = sum(n for (_, n, _) in subs) if len(subs) == 1 else 256
            UPB = (1, 2, 7)
            for fc in range(32):
                par = fc % 3
                reg = PS[:, UPB[par] * 512:UPB[par] * 512 + nt]
                rk = "bu%d" % par
                for kc in range(8):
                    O.mm(reg, w_up_b[:, kc, fc * 128:(fc + 1) * 128], hT2[p][:, kc, 0:nt], kc == 0, kc == 7,
                         ["w_up_b", "hT%d" % p], [rk])
                O.act(rl[:, par, 0:nt], reg, AF.Relu, [rk], ["rl%d" % par])
                O.tt("dve", upT[:, fc, 0:nt], rl[:, par, 0:nt], rl[:, par, 0:nt], ALU.mult,
                     ["rl%d" % par], ["upT"])
            if mid is not None:
                mid()
            for si, (idx, n, y_out) in enumerate(subs):
                yb = 5 if si == 0 else 3
                yp = PS[:, yb * 512: (yb + 2) * 512]
                yk = "by%d" % si
                for half in range(2):
                    for fc in range(32):
                        O.mm(yp[:n, half * 512:(half + 1) * 512], upT[:, fc, si * 128:si * 128 + n],
                             w_dn_b[:, fc, half * 512:(half + 1) * 512], fc == 0, fc == 31, ["upT", "w_dn_b"], [yk])
                ys = ys2[si]
                O.tt("dve", ys[:n, :], yp[:n, :], x1t2[p][:n, si, :], ALU.add, [yk, "x1t%d" % p], ["ys%d" % si])
                O.S.dma("pool", "yo%d" % si, lambda e, ys=ys, n=n, y_out=y_out: e.dma_start(out=y_out, in_=ys[:n, :]),
                        ["ys%d" % si], [])

        blocks = []
        flat = []
        for s in range(NP):
            for i in range(NFT):
                flat.append((s * NFT + i, 128, y_p[s, 128 * i:128 * (i + 1), :]))
        for i in range(0, len(flat), 2):
            blocks.append(flat[i:i + 2])
        blocks.append([(NP * NFT, 64, y_s)])
        mlp_front_a(0, blocks[0])
        mlp_front_b(0, blocks[0])
        for bi in range(len(blocks)):
            if bi + 1 < len(blocks):
                mlp_front_a(bi + 1, blocks[bi + 1])
                mlp_body(bi, blocks[bi], mid=lambda bi=bi: mlp_front_b(bi + 1, blocks[bi + 1]))
            else:
                mlp_body(bi, blocks[bi])
        S.emit()
    return nc


_NC_CACHE = {}


def run_cores(per_core_inputs, NP, SEQ, PAST):
    key = (NP, SEQ, PAST)
    if key not in _NC_CACHE:
        _NC_CACHE[key] = build_nc(NP, SEQ, PAST)
    nc = _NC_CACHE[key]
    res = run_bass_kernel_spmd(nc, per_core_inputs, core_ids=list(range(len(per_core_inputs))))
    return res.results


def make_core_inputs(c, NP, inputs, consts):
    f = lambda a: np.ascontiguousarray(np.asarray(a), dtype=np.float32)
    d = {
        "xp": f(inputs["x_prompt"][NP * c:NP * (c + 1)]),
        "meta": f(inputs["meta_tokens"]),
        "xs": f(inputs["x_sample"][c]),
        "clat": f(inputs["cache_kv_latent"][0, c]),
        "crope": f(inputs["cache_k_rope"][0, c]),
        "sgdn": f(inputs["state_gdn"][0, c]),
        "sconv": f(inputs["state_conv"][0, c]),
    }
    for k in ("w_in", "w_uq", "w_uk", "w_uv", "w_out", "w_up", "w_down", "attn_norm", "q_a_norm", "kv_a_norm", "q_norm",
              "k_norm", "mla_out_norm", "conv_w", "a_log", "dt_bias", "gdn_out_norm", "mlp_norm"):
        d[k] = f(inputs[k][0])
    d.update(consts)
    return d


def kernel(**inputs):
    NCORES = 8
    B, SEQ = inputs["x_prompt"].shape[:2]
    NP = B // NCORES
    PAST = inputs["cache_kv_latent"].shape[2] - 16
    LP = 16 + SEQ
    consts = host_consts(max(LP, 16 + PAST + 64))
    ins = [make_core_inputs(c, NP, inputs, consts) for c in range(NCORES)]
    res = run_cores(ins, NP, SEQ, PAST)
    cat = lambda k: np.concatenate([r[k] for r in res], axis=0)
    stk = lambda k: np.stack([r[k] for r in res], axis=0)
    y_p = cat("y_p")
    y_s = stk("y_s")
    return (y_p, y_s, cat("p_lat")[None], cat("p_rope")[None], cat("p_S")[None], cat("p_conv")[None],
            stk("s_lat")[None], stk("s_rope")[None], stk("s_S")[None], stk("s_conv")[None])
```
